# Optimizing a Trainium2 kernel written in Bass

```python
import math
import jax
import jax.numpy as jnp
from jax import lax
import numpy as np

D_MODEL = 1024
BATCH = 16
SEQ = 256
DEPTH = 2
DEC_BATCH = 4
DEC_SEQ = 1024
PAST_LEN = 256

GRID_W = 64
D_HY = 1024
HY_ORDER = 2
HY_BANDS = 16
HY_EMB = 1 + 2 * HY_BANDS
HY_FF = 64
HY_SHIFT = 0.05
HY_MIN_DECAY = math.log(1e-2) / -1.5
HY_MAX_DECAY = math.log(1e-2) / -0.3
D_ML = 1024
ML_HEADS = 4
ML_DH = D_ML // ML_HEADS
ML_CHUNK = 128
SHORT_CONV = 3
SPLIT_SIZES = (3 * D_HY, D_HY, 2 * D_ML, D_ML, D_ML, D_ML, 4 * ML_HEADS, D_MODEL, D_MODEL)
N_IN = sum(SPLIT_SIZES)
SPLIT_IDX = tuple(int(i) for i in np.cumsum(SPLIT_SIZES)[:-1])
EPS = 1e-6
NEG = -1e30
F32 = jnp.float32

kernel_name = 'hyena_mlstm_prefix_diffusion_step'


def _rmsnorm(x, g):
    xf = x.astype(F32)
    y = xf * lax.rsqrt(jnp.mean(xf * xf, axis=-1, keepdims=True) + EPS)
    return (y * g.astype(F32)).astype(x.dtype)


def _short_conv(x, w, b):
    L = x.shape[1]
    pad = SHORT_CONV // 2
    xp = jnp.pad(x, ((0, 0), (pad, pad), (0, 0)))
    y = b
    for j in range(SHORT_CONV):
        y = y + xp[:, j:j + L] * w[j]
    return y


def _hyena_filters(L, w1, b1, w2, b2, w3, b3, freq, decay):
    t = jnp.arange(L, dtype=F32) / L
    bands = jnp.arange(1, HY_BANDS + 1, dtype=F32)
    ang = 2.0 * math.pi * t[:, None] * bands[None, :]
    feats = jnp.concatenate([t[:, None], jnp.cos(ang), jnp.sin(ang)], axis=-1)
    f = freq.astype(F32)
    hdn = jnp.sin(f[0] * (feats @ w1.astype(F32) + b1.astype(F32)))
    hdn = jnp.sin(f[1] * (hdn @ w2.astype(F32) + b2.astype(F32)))
    h = (hdn @ w3.astype(F32) + b3.astype(F32)).reshape(L, HY_ORDER, 2, D_HY)
    window = jnp.exp(-t[:, None, None, None] * jnp.abs(decay.astype(F32))[None]) + HY_SHIFT
    h = h * window
    fwd = h[:, :, 0]
    bwd = h[1:, :, 1][::-1]
    l1 = jnp.sum(jnp.abs(fwd), axis=0) + jnp.sum(jnp.abs(bwd), axis=0)
    kern = jnp.concatenate([fwd, jnp.zeros((1, HY_ORDER, D_HY), F32), bwd], axis=0) / l1
    return jnp.fft.rfft(kern, axis=0)


def _hyena_mixer(u3, conv_w, conv_b, filt, bias):
    L = u3.shape[1]
    u3 = _short_conv(u3, conv_w, conv_b).astype(F32)
    x1, x2, v = jnp.split(u3, 3, axis=-1)
    gates = (x1, x2)
    bias = bias.astype(F32)
    z = v
    for o in range(HY_ORDER):
        zf = jnp.fft.rfft(z, n=2 * L, axis=1)
        y = jnp.fft.irfft(zf * filt[None, :, o], n=2 * L, axis=1)[:, :L]
        z = gates[o] * (y + bias[o] * z)
    return z


def _mlstm_chunkwise(q, k, v, logi, logf, C0, n0, m0):
    B, L, H, DH = q.shape
    NC = L // ML_CHUNK
    T = ML_CHUNK
    def to_chunks(a):
        a = a.reshape((B, NC, T) + a.shape[2:])
        return jnp.moveaxis(jnp.moveaxis(a, 1, 0), 2, -1 if a.ndim == 4 else 3)
    qc = q.reshape(B, NC, T, H, DH).transpose(1, 0, 3, 2, 4)
    kc = k.reshape(B, NC, T, H, DH).transpose(1, 0, 3, 2, 4)
    vc = v.reshape(B, NC, T, H, DH).transpose(1, 0, 3, 2, 4)
    ic = logi.reshape(B, NC, T, H).transpose(1, 0, 3, 2)
    fc = logf.reshape(B, NC, T, H).transpose(1, 0, 3, 2)
    tril = jnp.tril(jnp.ones((T, T), dtype=bool))

    def step(carry, xs):
        C, n, m = carry
        qb, kb, vb, ib, fb = xs
        b = jnp.cumsum(fb, axis=-1)
        dlog = jnp.where(tril, b[..., :, None] - b[..., None, :] + ib[..., None, :], NEG)
        inter = b + m[..., None]
        mt = jnp.maximum(inter, jnp.max(dlog, axis=-1))
        s = jnp.einsum('bhtd,bhsd->bhts', qb, kb) * jnp.exp(dlog - mt[..., None])
        iw = jnp.exp(inter - mt)
        num = jnp.einsum('bhts,bhsv->bhtv', s, vb) + iw[..., None] * jnp.einsum('bhtk,bhkv->bhtv', qb, C)
        den = jnp.sum(s, axis=-1) + iw * jnp.einsum('bhtk,bhk->bht', qb, n)
        h = num / jnp.maximum(jnp.abs(den), jnp.exp(-mt))[..., None]
        bl = b[..., -1]
        wlog = bl[..., None] - b + ib
        mn = jnp.maximum(bl + m, jnp.max(wlog, axis=-1))
        ws = jnp.exp(wlog - mn[..., None])
        dec = jnp.exp(bl + m - mn)
        Cn = dec[..., None, None] * C + jnp.einsum('bhs,bhsk,bhsv->bhkv', ws, kb, vb)
        nn = dec[..., None] * n + jnp.einsum('bhs,bhsk->bhk', ws, kb)
        return (Cn, nn, mn), h

    (C, n, m), hs = lax.scan(step, (C0, n0, m0), (qc, kc, vc, ic, fc))
    h = hs.transpose(1, 0, 3, 2, 4).reshape(B, L, H, DH)
    return h, C, n, m


def _mlstm_mixer(qk, v, gpre, conv_w, conv_b, if_b, states):
    B, L, _ = v.shape
    qk = jax.nn.silu(_short_conv(qk, conv_w, conv_b)).astype(F32)
    q, k = jnp.split(qk, 2, axis=-1)
    q = q.reshape(B, L, ML_HEADS, ML_DH)
    k = k.reshape(B, L, ML_HEADS, ML_DH) * (ML_DH ** -0.5)
    vf = v.astype(F32).reshape(B, L, ML_HEADS, ML_DH)
    g = gpre.astype(F32).reshape(B, L, 2, 2, ML_HEADS) + if_b.astype(F32)
    h_sum = None
    finals = []
    for d in range(2):
        logi = g[:, :, d, 0]
        logf = jax.nn.log_sigmoid(g[:, :, d, 1])
        seq = (q, k, vf, logi, logf)
        if d == 1:
            seq = tuple(jnp.flip(a, axis=1) for a in seq)
        C0, n0, m0 = (s.astype(F32) for s in states[d])
        h, C, n, m = _mlstm_chunkwise(seq[0], seq[1], seq[2], seq[3], seq[4], C0, n0, m0)
        if d == 1:
            h = jnp.flip(h, axis=1)
        h_sum = h if h_sum is None else h_sum + h
        finals.append((C, n, m))
    return h_sum, finals


def _layer(x, mod, states, norm_g, w_in, hy_conv_w, hy_conv_b, hy_w1, hy_b1, hy_w2, hy_b2,
           hy_w3, hy_b3, hy_freq, hy_decay, hy_bias, ml_conv_w, ml_conv_b, ml_if_b,
           ml_norm_g, w_pa, w_pb, w_out):
    B, L, _ = x.shape
    shift, scale, gate = jnp.split(mod, 3, axis=-1)
    h = _rmsnorm(x, norm_g) * (1.0 + scale[:, None]) + shift[:, None]
    u = h @ w_in
    hy_u, hy_z, ml_qk, ml_v, ml_o, ml_z, ml_g, g_a, g_b = jnp.split(u, SPLIT_IDX, axis=-1)
    filt = _hyena_filters(L, hy_w1, hy_b1, hy_w2, hy_b2, hy_w3, hy_b3, hy_freq, hy_decay)
    ya = _hyena_mixer(hy_u, hy_conv_w, hy_conv_b, filt, hy_bias).astype(x.dtype) * jax.nn.silu(hy_z)
    hm, finals = _mlstm_mixer(ml_qk, ml_v, ml_g, ml_conv_w, ml_conv_b, ml_if_b, states)
    hm = hm * lax.rsqrt(jnp.mean(hm * hm, axis=-1, keepdims=True) + EPS)
    hm = (hm.reshape(B, L, D_ML) * ml_norm_g.astype(F32)).astype(x.dtype)
    yb = hm * jax.nn.sigmoid(ml_o) * jax.nn.silu(ml_z)
    merged = jax.nn.sigmoid(g_a) * (ya @ w_pa) + jax.nn.sigmoid(g_b) * (yb @ w_pb)
    return x + gate[:, None] * (merged @ w_out), finals


def setup_inputs(seed: int = 0) -> dict:
    key = jax.random.key(seed)
    ks = jax.random.split(key, 32)
    def nrm(k, shape, s):
        return jax.random.normal(k, shape, F32) * s
    x_prompt = nrm(ks[0], (BATCH, SEQ, D_MODEL), 1.0)
    x_sample = nrm(ks[1], (DEC_BATCH, DEC_SEQ, D_MODEL), 1.0)
    state_C = nrm(ks[2], (DEC_BATCH, DEPTH, 2, ML_HEADS, ML_DH, ML_DH), 0.5)
    state_n = nrm(ks[3], (DEC_BATCH, DEPTH, 2, ML_HEADS, ML_DH), 0.5)
    state_m = nrm(ks[4], (DEC_BATCH, DEPTH, 2, ML_HEADS), 0.5)
    c = nrm(ks[5], (DEC_BATCH, D_MODEL), 1.0)
    c_ctx = nrm(ks[6], (D_MODEL,), 1.0)
    norm_g = 1.0 + nrm(ks[7], (DEPTH, D_MODEL), 0.02)
    w_ada = nrm(ks[8], (DEPTH, D_MODEL, 3 * D_MODEL), 0.5 * D_MODEL ** -0.5)
    b_ada = nrm(ks[9], (DEPTH, 3 * D_MODEL), 0.02)
    w_in = nrm(ks[10], (DEPTH, D_MODEL, N_IN), D_MODEL ** -0.5)
    hy_conv_w = nrm(ks[11], (DEPTH, SHORT_CONV, 3 * D_HY), SHORT_CONV ** -0.5)
    hy_conv_b = nrm(ks[12], (DEPTH, 3 * D_HY), 0.02)
    hy_w1 = nrm(ks[13], (DEPTH, HY_EMB, HY_FF), HY_EMB ** -0.5)
    hy_b1 = nrm(ks[14], (DEPTH, HY_FF), 0.1)
    hy_w2 = nrm(ks[15], (DEPTH, HY_FF, HY_FF), HY_FF ** -0.5)
    hy_b2 = nrm(ks[16], (DEPTH, HY_FF), 0.1)
    hy_w3 = nrm(ks[17], (DEPTH, HY_FF, HY_ORDER * 2 * D_HY), HY_FF ** -0.5)
    hy_b3 = nrm(ks[18], (DEPTH, HY_ORDER * 2 * D_HY), 0.1)
    hy_freq = 1.0 + nrm(ks[19], (DEPTH, 2, HY_FF), 0.1)
    decay_base = jnp.linspace(HY_MIN_DECAY, HY_MAX_DECAY, D_HY, dtype=F32)
    hy_decay = decay_base + nrm(ks[20], (DEPTH, HY_ORDER, 2, D_HY), 0.1)
    hy_bias = nrm(ks[21], (DEPTH, HY_ORDER, D_HY), 1.0)
    ml_conv_w = nrm(ks[22], (DEPTH, SHORT_CONV, 2 * D_ML), SHORT_CONV ** -0.5)
    ml_conv_b = nrm(ks[23], (DEPTH, 2 * D_ML), 0.02)
    i_b = nrm(ks[24], (DEPTH, 2, 1, ML_HEADS), 0.1)
    f_b = jnp.linspace(3.0, 6.0, ML_HEADS, dtype=F32) + nrm(ks[25], (DEPTH, 2, 1, ML_HEADS), 0.1)
    ml_if_b = jnp.concatenate([i_b, f_b], axis=2)
    ml_norm_g = 1.0 + nrm(ks[26], (DEPTH, D_ML), 0.02)
    w_pa = nrm(ks[27], (DEPTH, D_HY, D_MODEL), D_HY ** -0.5)
    w_pb = nrm(ks[28], (DEPTH, D_ML, D_MODEL), D_ML ** -0.5)
    w_out = nrm(ks[29], (DEPTH, D_MODEL, D_MODEL), D_MODEL ** -0.5)
    final_g = 1.0 + nrm(ks[30], (D_MODEL,), 0.02)
    return {'x_prompt': x_prompt, 'x_sample': x_sample, 'state_C': state_C, 'state_n': state_n,
            'state_m': state_m, 'c': c, 'c_ctx': c_ctx, 'norm_g': norm_g, 'w_ada': w_ada,
            'b_ada': b_ada, 'w_in': w_in, 'hy_conv_w': hy_conv_w, 'hy_conv_b': hy_conv_b,
            'hy_w1': hy_w1, 'hy_b1': hy_b1, 'hy_w2': hy_w2, 'hy_b2': hy_b2, 'hy_w3': hy_w3,
            'hy_b3': hy_b3, 'hy_freq': hy_freq, 'hy_decay': hy_decay, 'hy_bias': hy_bias,
            'ml_conv_w': ml_conv_w, 'ml_conv_b': ml_conv_b, 'ml_if_b': ml_if_b,
            'ml_norm_g': ml_norm_g, 'w_pa': w_pa, 'w_pb': w_pb, 'w_out': w_out,
            'final_g': final_g}


def reference(x_prompt, x_sample, state_C, state_n, state_m, c, c_ctx, norm_g, w_ada, b_ada,
              w_in, hy_conv_w, hy_conv_b, hy_w1, hy_b1, hy_w2, hy_b2, hy_w3, hy_b3, hy_freq,
              hy_decay, hy_bias, ml_conv_w, ml_conv_b, ml_if_b, ml_norm_g, w_pa, w_pb, w_out,
              final_g):
    B = x_prompt.shape[0]
    zero_state = (jnp.zeros((B, ML_HEADS, ML_DH, ML_DH), F32),
                  jnp.zeros((B, ML_HEADS, ML_DH), F32),
                  jnp.zeros((B, ML_HEADS), F32))
    ctx_states = (zero_state, zero_state)
    xp = x_prompt
    xs = x_sample
    new_C, new_n, new_m = [], [], []
    for l in range(DEPTH):
        weights = (norm_g[l], w_in[l], hy_conv_w[l], hy_conv_b[l], hy_w1[l], hy_b1[l], hy_w2[l],
                   hy_b2[l], hy_w3[l], hy_b3[l], hy_freq[l], hy_decay[l], hy_bias[l],
                   ml_conv_w[l], ml_conv_b[l], ml_if_b[l], ml_norm_g[l], w_pa[l], w_pb[l], w_out[l])
        mod_ctx = (jax.nn.silu(c_ctx) @ w_ada[l] + b_ada[l])[None]
        xp, fin = _layer(xp, mod_ctx, ctx_states, *weights)
        new_C.append(jnp.stack([fin[0][0], fin[1][0]], axis=1))
        new_n.append(jnp.stack([fin[0][1], fin[1][1]], axis=1))
        new_m.append(jnp.stack([fin[0][2], fin[1][2]], axis=1))
        mod_lat = jax.nn.silu(c) @ w_ada[l] + b_ada[l]
        cached = ((state_C[:, l, 0], state_n[:, l, 0], state_m[:, l, 0]),
                  (state_C[:, l, 1], state_n[:, l, 1], state_m[:, l, 1]))
        xs, _ = _layer(xs, mod_lat, cached, *weights)
    y_prompt = _rmsnorm(xp, final_g)
    y_sample = _rmsnorm(xs, final_g)
    new_state_C = jnp.stack(new_C, axis=1)
    new_state_n = jnp.stack(new_n, axis=1)
    new_state_m = jnp.stack(new_m, axis=1)
    return (y_prompt, y_sample, new_state_C, new_state_n, new_state_m)
```

```python
import math
from contextlib import ExitStack

import numpy as np
import concourse.bass as bass
import concourse.mybir as mybir
from concourse.alu_op_type import AluOpType as ALU
from concourse.bass_utils import run_bass_kernel_spmd

F32 = mybir.dt.float32
BF16 = mybir.dt.bfloat16
AF = mybir.ActivationFunctionType
AX = mybir.AxisListType

T = 1024
D = 1024
N_IN = 11280
MAGIC = 12582912.0
PI_LO = 3.1415925
EPS = 1e-6
DEBUG = False
SEQ_DEBUG = False
UD_UNITS = 16.0


class Buf:
    __slots__ = ("name", "w", "r", "excl")

    def __init__(self, name, excl=False):
        self.name = name
        self.w = None
        self.r = []
        self.excl = excl


class _Rec:
    def __init__(self):
        self.calls = []

    def __getattr__(self, name):
        def m(*a, **k):
            self.calls.append((name, a, k))
            return None
        return m


def _record(fn):
    r = _Rec()
    fn(r)
    assert r.calls
    return r.calls


class Sched:
    ENGS = ("pe", "act", "dve", "pool", "sp")

    def __init__(self, nc, es):
        self.nc = nc
        self.es = es
        self.streams = {e: [] for e in self.ENGS}
        self.esem = {}
        for e in ("pe", "act", "dve", "pool"):
            self.esem[e] = es.enter_context(nc.semaphore("s_" + e))
        self.ecount = {e: 0 for e in ("pe", "act", "dve", "pool")}
        self.dsem = {}
        self.dcount = {}
        self.waited = {e: {} for e in self.ENGS}
        self.nwait = 0
        self.nop = 0

    def _dma_sem(self, key):
        if key not in self.dsem:
            self.dsem[key] = self.es.enter_context(self.nc.semaphore("d_" + key))
            self.dcount[key] = 0
        return self.dsem[key]

    def _resolve(self, ev):
        kind, key, val = ev
        if kind == "eng":
            return ("e:" + key, self.esem[key], val)
        return ("d:" + key, self.dsem[key], self.dcount[key])

    def _emit_waits(self, eng, evs):
        need = {}
        for ev in evs:
            if ev is None:
                continue
            if ev[0] == "eng" and ev[1] == eng and eng == "pe":
                continue
            name, sem, val = self._resolve(ev)
            if val <= self.waited[eng].get(name, 0):
                continue
            if name not in need or need[name][1] < val:
                need[name] = (sem, val)
        for name, (sem, val) in need.items():
            self.waited[eng][name] = val
            self.streams[eng].append(("wait", sem, val))
            self.nwait += 1

    @staticmethod
    def _deps(reads, writes, eng=None):
        evs = []
        for b in reads:
            evs.append(b.w)
            if b.excl:
                evs.extend(r for r in b.r if not (r[0] == "eng" and r[1] == eng))
        for b in writes:
            evs.append(b.w)
            evs.extend(b.r)
        return evs

    def _commit(self, ev, reads, writes):
        for b in reads:
            b.r = [r for r in b.r if not (r[0] == ev[0] and r[1] == ev[1])]
            b.r.append(ev)
        for b in writes:
            b.w = ev
            b.r = []
        self.nop += 1

    def op(self, eng, fn, reads=(), writes=()):
        self._emit_waits(eng, self._deps(reads, writes, eng))
        self.ecount[eng] += 1
        ev = ("eng", eng, self.ecount[eng])
        self.streams[eng].append(("op", _record(fn), self.esem[eng], 1))
        self._commit(ev, reads, writes)
        return ev

    def dma(self, queue, key, fn, reads=(), writes=()):
        key = queue + "_" + key
        sem = self._dma_sem(key)
        deps = [ev for ev in self._deps(reads, writes) if not (ev is not None and ev[0] == "dma" and ev[1] == key)]
        self._emit_waits(queue, deps)
        self.dcount[key] += 16
        ev = ("dma", key, None)
        self.streams[queue].append(("op", _record(fn), sem, 16))
        self._commit(ev, reads, writes)
        return ev

    def barrier(self):
        for eng in ("pe", "act", "dve", "pool"):
            evs = [("eng", e, self.ecount[e]) for e in ("pe", "act", "dve", "pool") if self.ecount[e] > 0]
            evs += [("dma", k, None) for k in self.dsem]
            self._emit_waits(eng, evs)
        evs = [("eng", e, self.ecount[e]) for e in ("pe", "act", "dve", "pool") if self.ecount[e] > 0]
        evs += [("dma", k, None) for k in self.dsem]
        self._emit_waits("sp", evs)

    def final_wait(self, eng="sp"):
        for key, sem in self.dsem.items():
            name = "d:" + key
            val = self.dcount[key]
            if val > self.waited[eng].get(name, 0):
                self.waited[eng][name] = val
                self.streams[eng].append(("wait", sem, val))

    def emit(self):
        nc = self.nc
        handles = {"pe": "tensor", "act": "scalar", "dve": "vector", "pool": "gpsimd", "sp": "sync"}
        with nc.Block() as block:
            for e in self.ENGS:
                items = self.streams[e]

                def body(engine, items=items):
                    for it in items:
                        if it[0] == "wait":
                            engine.wait_ge(it[1], it[2])
                        else:
                            inst = None
                            for (name, a, k) in it[1]:
                                inst = getattr(engine, name)(*a, **k)
                            inst.then_inc(it[2], it[3])

                getattr(block, handles[e])(body)


PPO = {}
_o = 0
for _n, _w in [("b_ada", 24), ("norm_g", 8), ("hcw", 72), ("hcb", 24), ("hbias", 16), ("mcw", 48), ("mcb", 16),
               ("mng", 8), ("hb1", 1), ("hb2", 1), ("hfr", 2), ("gbi", 1), ("gbf", 1), ("fing", 8)]:
    PPO[_n] = (_o, _w)
    _o += _w
NPP = _o
CCO = {}
_o = 0
for _n, _w in [("tnneg", 8), ("maskb", 8), ("nmaskb", 8), ("bfixneg", 1), ("invhs", 1), ("keep", 8), ("keepc", 8), ("cvec", 8)]:
    CCO[_n] = (_o, _w)
    _o += _w
NCC = _o

ARENA_F32 = 9888


class _Stop(Exception):
    pass


STOP_AT = None


def build_program():
    nc = bass.Bass("TRN2", target_bir_lowering=False)
    es = ExitStack()
    with es:
        S = Sched(nc, es)
        phases = []

        def phase(name):
            phases.append(name)
            if STOP_AT is not None and name == STOP_AT:
                raise _Stop()

        def din(name, shape, dt=F32):
            return nc.dram_tensor(name, shape, dt, kind="ExternalInput").ap()

        def dout(name, shape, dt=F32):
            return nc.dram_tensor(name, shape, dt, kind="ExternalOutput").ap()

        def sb(name, shape, dt=F32):
            return es.enter_context(nc.sbuf_tensor(name, shape, dt))

        xin = din("xin", [T, D])
        wsym_d = din("wsym", [T, 2 * T], BF16)
        w0_d = din("w0t", [T, 2 * T], BF16)
        feats_d = din("featsT", [33, T])
        cc_d = din("cc", [128, NCC])
        resetm_d = din("resetm", [64, T])
        identf_d = din("identf", [128, 128])
        triu_d = din("triu", [128, 128])
        tril_d = din("tril", [128, 128])
        c0_d = din("c0aug", [2, 2, 4, 256, 257])
        m0_d = din("m0", [2, 64, 1])
        pp_d = din("pp", [2, 128, NPP])
        w_in_d = din("w_in", [2, D, N_IN])
        w_ada_d = din("w_ada", [2, D, 3 * D])
        w_pa_d = din("w_pa", [2, D, D])
        w_pb_d = din("w_pb", [2, D, D])
        w_out_d = din("w_out", [2, D, D])
        hy_w3_d = din("hy_w3", [2, 64, 4096])
        hy_b3_d = din("hy_b3", [2, 1, 4096])
        hy_dec_d = din("hy_dec", [2, 1, 4096])
        hy_w1_d = din("hy_w1", [2, 33, 64])
        hy_w2_d = din("hy_w2", [2, 64, 64])
        wgi_d = din("wgi", [2, D, 128])
        wgf_d = din("wgf", [2, D, 128])
        y_d = dout("y", [T, D])
        sc_d = dout("sc", [2, 4, 2, 4, 256, 257])
        sm_d = dout("sm", [2, 64, 8])
        dbg_outs = {}

        X = sb("X", [128, 8, T])
        hT = sb("hT", [128, 8, T], BF16)
        tabs = sb("tabs", [128, 2 * 8 * 2 * T], BF16)
        WS = tabs[:, 0:8 * 2 * T].rearrange("p (c f) -> p c f", c=8)
        W0 = tabs[:, 8 * 2 * T:2 * 8 * 2 * T].rearrange("p (c f) -> p c f", c=8)
        arenaB = tabs[:].bitcast(F32)
        yaT = sb("yaT", [128, 8, T], BF16)
        wbufs = [sb("wbuf%d" % i, [128, 8, 512], BF16) for i in range(2)]
        PP = sb("PP", [128, NPP])
        CC = sb("CC", [128, NCC])
        identf = sb("identf_s", [128, 128])
        identb = sb("identb", [128, 128], BF16)
        triu = sb("triu_s", [128, 128])
        tril = sb("tril_s", [128, 128])
        ones_bf = sb("ones_bf", [128, 128], BF16)
        ones_f = sb("ones_f", [128, 128])
        resetm = sb("resetm_s", [64, T], BF16)
        hdn2 = sb("hdn2", [65, T], BF16)
        waT = sb("waT", [128, 8, 8])
        ebT = sb("ebT", [128, 8, 8])
        wmkB = sb("wmkB", [128, 8, 8])
        small = sb("small", [128, 128])
        fixw = sb("fixw", [128, 80])
        arena = sb("arena", [128, ARENA_F32 + 4096])
        ybT = arena[:, ARENA_F32:ARENA_F32 + 4096].bitcast(BF16).rearrange("p (c t) -> p c t", c=8)
        ps = es.enter_context(nc.psum_tensor("ps", [128, 8, 512], F32))

        bX = [Buf("X%d" % i) for i in range(8)]
        bhT = [Buf("hT%d" % i) for i in range(8)]
        bWS, bW0 = Buf("WS"), Buf("W0")
        byaT = [Buf("yaT%d" % i) for i in range(8)]
        bybT = [Buf("ybT%d" % i) for i in range(8)]
        bwbuf = [Buf("wbuf0"), Buf("wbuf1")]
        bPP, bCC, bconst = Buf("PP"), Buf("CC"), Buf("const")
        bhdn2 = Buf("hdn2")
        bwaT, bebT, bwmkB = Buf("waT"), Buf("ebT"), Buf("wmkB")
        bsmall = Buf("small")
        bden, bss, bfix, bzero = Buf("den"), Buf("ss"), Buf("fix"), Buf("zero")
        bden2 = Buf("den2")
        pb = [Buf("psb%d" % i, excl=True) for i in range(8)]

        def pp(name, a=0, n=None, rows=slice(0, 128)):
            o, w = PPO[name]
            n = w - a if n is None else n
            return PP[rows, o + a:o + a + n]

        def ccv(name, a=0, n=None, rows=slice(0, 128)):
            o, w = CCO[name]
            n = w - a if n is None else n
            return CC[rows, o + a:o + a + n]

        SM = {"modsb": (0, 24), "gs": (24, 8), "sc": (32, 8), "fb": (40, 2), "ngbf": (42, 1), "m0": (43, 1), "meff": (44, 1),
              "amax": (48, 8), "negtot": (56, 8), "Rc": (64, 8), "dmc": (72, 8), "mst": (80, 8), "wmk": (88, 8),
              "ss": (96, 8), "rs": (104, 8), "den": (112, 2), "w0fix": (114, 1), "w2fix": (115, 1), "zero": (116, 1), "den2": (118, 2)}

        def sm(name, a=0, n=None, rows=slice(0, 128)):
            o, w = SM[name]
            n = w - a if n is None else n
            return small[rows, o + a:o + a + n]

        scb = sb("scb", [128, 8], BF16)

        astate = {"off": 0, "lim": ARENA_F32}

        def areset(ext=False):
            S.barrier()
            astate["off"] = 0
            astate["lim"] = ARENA_F32 + (4096 if ext else 0)

        def af32(n):
            o = astate["off"]
            astate["off"] = o + n
            assert astate["off"] <= astate["lim"], ("arena overflow", astate["off"])
            return arena[:, o:o + n]

        def abf(n):
            assert n % 2 == 0
            o = astate["off"]
            astate["off"] = o + n // 2
            assert astate["off"] <= astate["lim"], ("arena overflow", astate["off"])
            return arena[:, o:o + n // 2].bitcast(BF16)

        pstate = {"one": 0, "pair": 0}

        def pbank():
            i = pstate["one"]
            pstate["one"] = (i + 1) % 7
            return i, pb[i]

        def ppair():
            k = pstate["pair"]
            pstate["pair"] = (k + 1) % 3
            return 2 * k, [pb[2 * k], pb[2 * k + 1]]

        def dbg(name, ap, bufs, shape, dt=F32):
            if not DEBUG:
                return
            o = dout("dbg_" + name, shape, dt)
            dbg_outs[name] = o
            S.dma("sp", "dbg", lambda e: e.dma_start(out=o, in_=ap), reads=bufs)

        wstate = {"idx": 0}
        plan = []
        for l_ in range(2):
            for cb in range(6):
                plan.append([(w_ada_d[l_][:, cb * 512:(cb + 1) * 512], 0)])
            plan.append([(wgi_d[l_], 0), (wgf_d[l_], 128)])
            for blk in range(8):
                c0 = blk * 128
                plan.append([(w_in_d[l_][:, c0:c0 + 128], 0), (w_in_d[l_][:, 1024 + c0:1024 + c0 + 128], 128),
                             (w_in_d[l_][:, 2048 + c0:2048 + c0 + 128], 256), (w_in_d[l_][:, 3072 + c0:3072 + c0 + 128], 384)])
            for hh in range(4):
                c0 = hh * 256
                plan.append([(w_in_d[l_][:, 4096 + c0:4096 + c0 + 256], 0), (w_in_d[l_][:, 5120 + c0:5120 + c0 + 256], 256)])
                plan.append([(w_in_d[l_][:, 6144 + c0:6144 + c0 + 256], 0), (w_in_d[l_][:, 7168 + c0:7168 + c0 + 256], 256)])
                plan.append([(w_in_d[l_][:, 8192 + c0:8192 + c0 + 256], 0)])
            for r4 in range(2):
                plan.append([(w_in_d[l_][:, 9232 + r4 * 512:9232 + (r4 + 1) * 512], 0)])
                plan.append([(w_pa_d[l_][:, r4 * 512:(r4 + 1) * 512], 0)])
                plan.append([(w_in_d[l_][:, 10256 + r4 * 512:10256 + (r4 + 1) * 512], 0)])
                plan.append([(w_pb_d[l_][:, r4 * 512:(r4 + 1) * 512], 0)])
            for r4 in range(2):
                plan.append([(w_out_d[l_][:, r4 * 512:(r4 + 1) * 512], 0)])
        issued = [False] * len(plan)

        def _issue(i):
            if issued[i]:
                return
            issued[i] = True
            wb, bb = wbufs[i % 2], bwbuf[i % 2]
            for src, off in plan[i]:
                n = src.shape[1]
                S.dma("pool", "w%d" % (i % 2),
                      lambda e, src=src, off=off, n=n, wb=wb: e.dma_start(
                          out=wb[:, :, off:off + n], in_=src.rearrange("(c p) n -> p c n", p=128)),
                      writes=[bb])

        def wload(pieces, prefetch=True):
            i = wstate["idx"]
            wstate["idx"] += 1
            assert len(pieces) == len(plan[i]) and pieces[0][0].shape[1] == plan[i][0][0].shape[1], ("weight plan mismatch", i)
            _issue(i)
            if prefetch and i + 1 < len(plan):
                _issue(i + 1)
            return wbufs[i % 2], bwbuf[i % 2]

        try:
            S.dma("sp", "const", lambda e: e.dma_start(out=CC[:], in_=cc_d[:, :]), writes=[bCC])
            S.dma("sp", "const", lambda e: e.dma_start(out=identf[:], in_=identf_d[:, :]), writes=[bconst])
            S.dma("sp", "const", lambda e: e.dma_start(out=triu[:], in_=triu_d[:, :]), writes=[bconst])
            S.dma("sp", "const", lambda e: e.dma_start(out=tril[:], in_=tril_d[:, :]), writes=[bconst])
            S.dma("pool", "const", lambda e: e.dma_start(out=identb[:], in_=identf_d[:, :]), writes=[bconst])
            S.dma("pool", "const", lambda e: e.dma_start(out=resetm[:], in_=resetm_d[:, :]), writes=[bconst])
            S.op("pool", lambda e: e.memset(ones_bf[:], 1.0), writes=[bconst])
            S.op("pool", lambda e: e.memset(ones_f[:], 1.0), writes=[bconst])
            S.op("pool", lambda e: e.memset(hdn2[64:65, :], 1.0), writes=[bhdn2])
            S.op("pool", lambda e: e.memset(small[:], 0.0), writes=[bsmall, bden, bden2, bss, bfix, bzero])
            S.op("act", lambda e: e.activation(out=scb[:], in_=ccv("cvec"), func=AF.Silu), reads=[bCC], writes=[bsmall])

            phase('const')
            areset()
            xst = [af32(1024), af32(1024)]
            bxst = [Buf("xst0"), Buf("xst1")]
            for i in range(8):
                xt, bx = xst[i % 2], bxst[i % 2]
                S.dma("sp", "xin%d" % (i % 2), lambda e, i=i, xt=xt: e.dma_start(out=xt, in_=xin[i * 128:(i + 1) * 128, :]), writes=[bx])
                for g in range(2):
                    bk, bb = pbank()

                    def f(e, xt=xt, g=g, bk=bk):
                        r = None
                        for q in range(4):
                            r = e.transpose(out=ps[:, bk, q * 128:(q + 1) * 128], in_=xt[:, (4 * g + q) * 128:(4 * g + q + 1) * 128], identity=identf[:])
                        return r
                    S.op("pe", f, reads=[bx, bconst], writes=[bb])
                    eng = "act" if g == 0 else "dve"
                    if eng == "act":
                        S.op("act", lambda e, g=g, i=i, bk=bk: e.copy(out=X[:, 4 * g:4 * g + 4, i * 128:(i + 1) * 128], in_=ps[:, bk, :].rearrange("p (q t) -> p q t", q=4)),
                             reads=[bb], writes=bX[4 * g:4 * g + 4])
                    else:
                        S.op("dve", lambda e, g=g, i=i, bk=bk: e.tensor_copy(out=X[:, 4 * g:4 * g + 4, i * 128:(i + 1) * 128], in_=ps[:, bk, :].rearrange("p (q t) -> p q t", q=4)),
                             reads=[bb], writes=bX[4 * g:4 * g + 4])

            def load_tables():
                for c in range(8):
                    S.dma("sp", "tabs", lambda e, c=c: e.dma_start(out=W0[:, c, :], in_=w0_d[c * 128:(c + 1) * 128, :]), writes=[bW0])
                    S.dma("sp", "tabs", lambda e, c=c: e.dma_start(out=WS[:, c, :], in_=wsym_d[c * 128:(c + 1) * 128, :]), writes=[bWS])
            load_tables()
            phase('xload')
            def rms_rstd():
                xsq = [abf(1024), abf(1024)]
                bxsq = [Buf("xsq0"), Buf("xsq1")]
                r1 = af32(1024)
                rstd = af32(1024)
                br = Buf("rstd")
                pk, pbs = ppair()
                for dch in range(8):
                    xs, bx = xsq[dch % 2], bxsq[dch % 2]
                    S.op("act", lambda e, dch=dch, xs=xs: e.activation(out=xs, in_=X[:, dch, :], func=AF.Square), reads=[bX[dch]], writes=[bx])

                    def f(e, dch=dch, xs=xs):
                        e.matmul(ps[:, pk, :], ones_bf[:], xs[:, 0:512], start=(dch == 0), stop=(dch == 7))
                        return e.matmul(ps[:, pk + 1, :], ones_bf[:], xs[:, 512:1024], start=(dch == 0), stop=(dch == 7))
                    S.op("pe", f, reads=[bx, bconst], writes=pbs)
                S.op("act", lambda e: e.activation(out=r1.rearrange("p (a b) -> p a b", a=2), in_=ps[:, pk:pk + 2, :], func=AF.Sqrt, scale=1.0 / D, bias=EPS),
                     reads=pbs, writes=[br])
                S.op("dve", lambda e: e.reciprocal(out=rstd, in_=r1), reads=[br], writes=[br])
                return rstd, br

            def proj_fm(pk, pbs, w, bw, col, rhs=None, brhs=None):
                rhs = hT if rhs is None else rhs
                brhs = bhT if brhs is None else brhs

                def f(e):
                    r = None
                    for half in range(2):
                        for dch in range(8):
                            r = e.matmul(ps[:, pk + half, :], w[:, dch, col:col + 128], rhs[:, dch, half * 512:(half + 1) * 512],
                                         start=(dch == 0), stop=(dch == 7))
                    return r
                S.op("pe", f, reads=[bw] + list(brhs), writes=pbs)

            def conv3(pk, pbs, acc, bacc, w0, w1, w2, bias, fixw0, fixw2):
                pin = ps[:, pk:pk + 2, :]
                flat = pin.rearrange("p a b -> p (a b)")
                S.op("act", lambda e: e.activation(out=acc.rearrange("p (a b) -> p a b", a=2), in_=pin, func=AF.Identity, scale=w1, bias=bias),
                     reads=pbs + [bPP], writes=[bacc])
                S.op("dve", lambda e: e.scalar_tensor_tensor(out=acc[:, 1:1024], in0=flat[:, 0:1023], scalar=w0, in1=acc[:, 1:1024], op0=ALU.mult, op1=ALU.add),
                     reads=pbs + [bPP], writes=[bacc])
                S.op("dve", lambda e: e.scalar_tensor_tensor(out=acc[:, 0:1023], in0=flat[:, 1:1024], scalar=w2, in1=acc[:, 0:1023], op0=ALU.mult, op1=ALU.add),
                     reads=pbs + [bPP], writes=[bacc])
                S.op("dve", lambda e: e.scalar_tensor_tensor(out=acc[:, 256:1024:256], in0=flat[:, 255:1023:256], scalar=fixw0, in1=acc[:, 256:1024:256], op0=ALU.mult, op1=ALU.add),
                     reads=pbs + [bfix], writes=[bacc])
                S.op("dve", lambda e: e.scalar_tensor_tensor(out=acc[:, 255:1023:256], in0=flat[:, 256:1024:256], scalar=fixw2, in1=acc[:, 255:1023:256], op0=ALU.mult, op1=ALU.add),
                     reads=pbs + [bfix], writes=[bacc])

            def transpose_fm_to_tok(src, bsrc, dst_fn, bdst, scale=None):
                for g in range(2):
                    bk, bb = pbank()
                    psb = ps[:, bk, :].bitcast(BF16)

                    def f(e, g=g, psb=psb):
                        r = None
                        for q in range(4):
                            tch = 4 * g + q
                            r = e.transpose(out=psb[:, q * 128:(q + 1) * 128], in_=src[:, tch * 128:(tch + 1) * 128], identity=identb[:])
                        return r
                    S.op("pe", f, reads=[bsrc, bconst], writes=[bb])
                    for q in range(4):
                        tch = 4 * g + q
                        if scale is None:
                            S.op("act" if q % 2 == 0 else "dve",
                                 (lambda e, q=q, tch=tch, psb=psb: e.copy(out=dst_fn(tch), in_=psb[:, q * 128:(q + 1) * 128])) if q % 2 == 0 else
                                 (lambda e, q=q, tch=tch, psb=psb: e.tensor_copy(out=dst_fn(tch), in_=psb[:, q * 128:(q + 1) * 128])),
                                 reads=[bb], writes=[bdst])
                        else:
                            S.op("act", lambda e, q=q, tch=tch, psb=psb: e.activation(out=dst_fn(tch), in_=psb[:, q * 128:(q + 1) * 128], func=AF.Identity, scale=scale),
                                 reads=[bb], writes=[bdst])

            for l in range(2):
                areset()
                S.dma("sp", "pp", lambda e, l=l: e.dma_start(out=PP[:], in_=pp_d[l]), writes=[bPP])

                for (nm_, j_, o_, n_) in (("hcw", 0, 0, 24), ("hcw", 2, 24, 24), ("mcw", 0, 48, 16), ("mcw", 2, 64, 16)):
                    S.op("dve", lambda e: e.tensor_scalar(out=fixw[:, o_:o_ + n_], in0=pp(nm_, j_ * n_, n_), scalar1=ccv("bfixneg"), scalar2=sm("zero"), op0=ALU.mult, op1=ALU.add),
                         reads=[bPP, bCC, bzero], writes=[bfix])
                phase('norm%d' % l)
                feats = af32(1024)
                arg = af32(1024)
                kq = af32(1024)
                hd1 = af32(1024)
                w1s = af32(64)
                w2s = af32(64)
                bF = Buf("filt")
                S.dma("sp", "hyw", lambda e: e.dma_start(out=feats[0:33, :], in_=feats_d[:, :]), writes=[bF])
                S.dma("sp", "hyw", lambda e, l=l: e.dma_start(out=w1s[0:33, :], in_=hy_w1_d[l]), writes=[bF])
                S.dma("sp", "hyw", lambda e, l=l: e.dma_start(out=w2s[0:64, :], in_=hy_w2_d[l]), writes=[bF])
                S.op("dve", lambda e: e.tensor_tensor(out=sm("fb", 0, 1, slice(0, 64)), in0=pp("hfr", 0, 1, slice(0, 64)), in1=pp("hb1", 0, 1, slice(0, 64)), op=ALU.mult),
                     reads=[bPP], writes=[bsmall])
                S.op("dve", lambda e: e.tensor_tensor(out=sm("fb", 1, 1, slice(0, 64)), in0=pp("hfr", 1, 1, slice(0, 64)), in1=pp("hb2", 0, 1, slice(0, 64)), op=ALU.mult),
                     reads=[bPP], writes=[bsmall])
                for layer in range(2):
                    pk, pbs = ppair()
                    if layer == 0:
                        def f(e, pk=pk):
                            e.matmul(ps[0:64, pk, :], w1s[0:33, 0:64], feats[0:33, 0:512], start=True, stop=True)
                            return e.matmul(ps[0:64, pk + 1, :], w1s[0:33, 0:64], feats[0:33, 512:1024], start=True, stop=True)
                    else:
                        def f(e, pk=pk):
                            e.matmul(ps[0:64, pk, :], w2s[0:64, 0:64], hd1[0:64, 0:512], start=True, stop=True)
                            return e.matmul(ps[0:64, pk + 1, :], w2s[0:64, 0:64], hd1[0:64, 512:1024], start=True, stop=True)
                    S.op("pe", f, reads=[bF], writes=pbs)
                    S.op("dve", lambda e, pk=pk, layer=layer: e.tensor_scalar(
                        out=arg[0:64, :].rearrange("p (a b) -> p a b", a=2), in0=ps[0:64, pk:pk + 2, :],
                        scalar1=pp("hfr", layer, 1, slice(0, 64)), scalar2=sm("fb", layer, 1, slice(0, 64)), op0=ALU.mult, op1=ALU.add),
                        reads=pbs + [bPP, bsmall], writes=[bF])
                    S.op("dve", lambda e: e.tensor_scalar(out=kq[0:64, :], in0=arg[0:64, :], scalar1=1.0 / (2 * math.pi), scalar2=MAGIC, op0=ALU.mult, op1=ALU.add),
                         reads=[bF], writes=[bF])
                    S.op("dve", lambda e: e.tensor_scalar(out=kq[0:64, :], in0=kq[0:64, :], scalar1=-MAGIC, scalar2=0.0, op0=ALU.add, op1=ALU.add),
                         reads=[bF], writes=[bF])
                    S.op("dve", lambda e: e.scalar_tensor_tensor(out=arg[0:64, :], in0=kq[0:64, :], scalar=-2 * math.pi, in1=arg[0:64, :], op0=ALU.mult, op1=ALU.add),
                         reads=[bF], writes=[bF])
                    S.op("dve", lambda e: e.tensor_scalar(out=arg[0:64, :], in0=arg[0:64, :], scalar1=PI_LO, scalar2=-PI_LO, op0=ALU.min, op1=ALU.max),
                         reads=[bF], writes=[bF])
                    if layer == 0:
                        S.op("act", lambda e: e.activation(out=hd1[0:64, :], in_=arg[0:64, :], func=AF.Sin), reads=[bF], writes=[bF])
                    else:
                        S.op("act", lambda e: e.activation(out=hdn2[0:64, :], in_=arg[0:64, :], func=AF.Sin), reads=[bF], writes=[bhdn2])

                rstd, br = rms_rstd()
                bk, bb = 7, pb[7]
                for cb in range(6):
                    w, bw = wload([(w_ada_d[l][:, cb * 512:(cb + 1) * 512], 0)])

                    def f(e, cb=cb, w=w, bk=bk):
                        r = None
                        for q in range(4):
                            for kch in range(8):
                                r = e.matmul(ps[:, bk, cb * 4 + q:cb * 4 + q + 1], w[:, kch, q * 128:(q + 1) * 128], scb[:, kch:kch + 1],
                                             start=(kch == 0), stop=(kch == 7))
                        return r
                    S.op("pe", f, reads=[bw, bsmall], writes=[bb])
                S.op("dve", lambda e, bk=bk: e.tensor_tensor(out=sm("modsb"), in0=ps[:, bk, 0:24], in1=pp("b_ada"), op=ALU.add), reads=[bb, bPP], writes=[bsmall])
                S.op("dve", lambda e: e.scalar_tensor_tensor(out=sm("gs"), in0=sm("modsb", 8, 8), scalar=1.0, in1=pp("norm_g"), op0=ALU.add, op1=ALU.mult),
                     reads=[bsmall, bPP], writes=[bsmall])

                phase('ada%d' % l)
                tmpf = [af32(1024), af32(1024)]
                btmp = [Buf("tmpf0"), Buf("tmpf1")]
                for dch in range(8):
                    tf, bt = tmpf[dch % 2], btmp[dch % 2]
                    S.op("dve", lambda e, dch=dch, tf=tf: e.scalar_tensor_tensor(out=tf, in0=X[:, dch, :], scalar=sm("gs", dch, 1), in1=rstd, op0=ALU.mult, op1=ALU.mult),
                         reads=[bX[dch], bsmall, br], writes=[bt])
                    S.op("act", lambda e, dch=dch, tf=tf: e.activation(out=hT[:, dch, :], in_=tf, func=AF.Identity, bias=sm("modsb", dch, 1), scale=1.0),
                         reads=[bt, bsmall], writes=[bhT[dch]])
                dbg("hT%d" % l, hT[:], bhT, [128, 8, T], BF16)

                phase('F0%d' % l)
                areset()
                w, bw = wload([(wgi_d[l], 0), (wgf_d[l], 128)])
                pki, pbi = ppair()
                pkf, pbf = ppair()
                proj_fm(pki, pbi, w, bw, 0)
                proj_fm(pkf, pbf, w, bw, 128)
                R64 = slice(0, 64)
                Gi = af32(1024)
                ee = af32(1024)
                sp_ = af32(1024)
                cs = af32(1024)
                csx = af32(1024)
                aa = af32(1024)
                bG = Buf("gates")
                mcur0 = sm("m0", 0, 1, R64)
                S.dma("sp", "m0", lambda e, l=l: e.dma_start(out=mcur0, in_=m0_d[l]), writes=[bsmall])
                S.op("dve", lambda e: e.tensor_scalar(out=sm("ngbf", 0, 1, R64), in0=pp("gbf", 0, 1, R64), scalar1=-1.0, scalar2=0.0, op0=ALU.mult, op1=ALU.add),
                     reads=[bPP], writes=[bsmall])
                S.op("act", lambda e: e.activation(out=Gi[R64, :].rearrange("p (a b) -> p a b", a=2), in_=ps[R64, pki:pki + 2, :], func=AF.Identity,
                                                   bias=pp("gbi", 0, 1, R64), scale=1.0), reads=pbi + [bPP], writes=[bG])
                S.op("act", lambda e: e.activation(out=ee[R64, :].rearrange("p (a b) -> p a b", a=2), in_=ps[R64, pkf:pkf + 2, :], func=AF.Exp,
                                                   bias=sm("ngbf", 0, 1, R64), scale=-1.0), reads=pbf + [bsmall], writes=[bG])
                S.op("act", lambda e: e.activation(out=sp_[R64, :], in_=ee[R64, :], func=AF.Ln, bias=1.0, scale=1.0), reads=[bG], writes=[bG])
                S.op("dve", lambda e: e.tensor_tensor_scan(out=cs[R64, :], data0=resetm[R64, :], data1=sp_[R64, :], initial=0.0, op0=ALU.mult, op1=ALU.add),
                     reads=[bG, bconst], writes=[bG])
                R32a, R32b = slice(0, 32), slice(32, 64)
                S.op("pool", lambda e: e.tensor_copy(out=csx[R32a, :], in_=cs[R32a, :]), reads=[bG], writes=[bG])
                S.op("dve", lambda e: e.tensor_tensor(out=csx[R32b, :], in0=sp_[R32b, :], in1=cs[R32b, :], op=ALU.subtract), reads=[bG], writes=[bG])
                S.op("dve", lambda e: e.tensor_tensor(
                    out=csx[R32b, :].rearrange("p (j t) -> p j t", t=128), in0=csx[R32b, :].rearrange("p (j t) -> p j t", t=128),
                    in1=cs[R32b, :].rearrange("p (j t) -> p j t", t=128)[:, :, 127:128].to_broadcast([32, 8, 128]), op=ALU.add), reads=[bG], writes=[bG])
                S.op("dve", lambda e: e.tensor_tensor(out=aa[R64, :], in0=Gi[R64, :], in1=csx[R64, :], op=ALU.add), reads=[bG], writes=[bG])
                S.op("dve", lambda e: e.tensor_reduce(out=sm("amax", 0, 8, R64), in_=aa[R64, :].rearrange("p (j t) -> p j t", t=128), axis=AX.X, op=ALU.max),
                     reads=[bG], writes=[bsmall])
                S.op("dve", lambda e: e.tensor_scalar(out=sm("negtot", 0, 8, R64).unsqueeze(2), in0=cs[R64, :].rearrange("p (j t) -> p j t", t=128)[:, :, 127:128],
                                                      scalar1=-1.0, scalar2=0.0, op0=ALU.mult, op1=ALU.add), reads=[bG], writes=[bsmall])
                for k in range(8):
                    mcur = mcur0 if k == 0 else sm("mst", k - 1, 1, R64)
                    S.op("dve", lambda e, k=k, mcur=mcur: e.tensor_tensor(out=sm("meff", 0, 1, R64), in0=mcur, in1=ccv("keep", k, 1, R64), op=ALU.mult),
                         reads=[bsmall, bCC], writes=[bsmall])
                    for rr, jj in ((R32a, k), (R32b, 7 - k)):
                        S.op("dve", lambda e, rr=rr, jj=jj: e.tensor_tensor(out=sm("Rc", jj, 1, rr), in0=sm("meff", 0, 1, rr), in1=sm("amax", jj, 1, rr), op=ALU.max),
                             reads=[bsmall], writes=[bsmall])
                        S.op("dve", lambda e, rr=rr, jj=jj: e.tensor_tensor(out=sm("dmc", jj, 1, rr), in0=sm("meff", 0, 1, rr), in1=sm("Rc", jj, 1, rr), op=ALU.subtract),
                             reads=[bsmall], writes=[bsmall])
                        S.op("dve", lambda e, rr=rr, jj=jj, k=k: e.tensor_tensor(out=sm("mst", k, 1, rr), in0=sm("negtot", jj, 1, rr), in1=sm("Rc", jj, 1, rr), op=ALU.add),
                             reads=[bsmall], writes=[bsmall])
                S.dma("sp", "smout", lambda e, l=l: e.dma_start(out=sm_d[l], in_=sm("mst", 0, 8, R64)), reads=[bsmall])
                S.op("act", lambda e: e.activation(out=sm("wmk", 0, 8, R64), in_=sm("dmc", 0, 8, R64), func=AF.Exp), reads=[bsmall], writes=[bsmall])
                S.op("dve", lambda e: e.tensor_tensor(out=sm("wmk", 0, 8, R64), in0=sm("wmk", 0, 8, R64), in1=ccv("keepc", 0, 8, R64), op=ALU.mult),
                     reads=[bsmall, bCC], writes=[bsmall])
                Rbc = sm("Rc", 0, 8, R64).unsqueeze(2).to_broadcast([64, 8, 128])
                S.op("dve", lambda e: e.tensor_tensor(out=Gi[R64, :].rearrange("p (j t) -> p j t", t=128), in0=aa[R64, :].rearrange("p (j t) -> p j t", t=128), in1=Rbc, op=ALU.subtract),
                     reads=[bG, bsmall], writes=[bG])
                S.op("dve", lambda e: e.tensor_tensor(out=ee[R64, :].rearrange("p (j t) -> p j t", t=128), in0=csx[R64, :].rearrange("p (j t) -> p j t", t=128), in1=Rbc, op=ALU.subtract),
                     reads=[bG, bsmall], writes=[bG])
                S.op("act", lambda e: e.activation(out=Gi[R64, :], in_=Gi[R64, :], func=AF.Exp), reads=[bG], writes=[bG])
                S.op("act", lambda e: e.activation(out=ee[R64, :], in_=ee[R64, :], func=AF.Exp), reads=[bG], writes=[bG])
                for src, dst, bdst in ((Gi, waT, bwaT), (ee, ebT, bebT)):
                    bk, bb = pbank()

                    def f(e, src=src, bk=bk):
                        r = None
                        for j in range(8):
                            r = e.transpose(out=ps[:, bk, j * 64:(j + 1) * 64], in_=src[R64, j * 128:(j + 1) * 128], identity=identf[0:64, 0:64])
                        return r
                    S.op("pe", f, reads=[bG, bconst], writes=[bb])
                    for d in range(2):
                        S.op("dve", lambda e, dst=dst, bk=bk, d=d: e.tensor_copy(out=dst[:, :, 4 * d:4 * d + 4], in_=ps[:, bk, :].rearrange("p (j c) -> p j c", c=64)[:, :, 32 * d:32 * d + 4]),
                             reads=[bb], writes=[bdst])
                Dm = csx
                S.op("dve", lambda e: e.tensor_tensor(out=Dm[R64, 0:512].rearrange("p (j c) -> p j c", c=64),
                                                      in0=identf[0:64, 0:64].unsqueeze(1).to_broadcast([64, 8, 64]),
                                                      in1=sm("wmk", 0, 8, R64).unsqueeze(2).to_broadcast([64, 8, 64]), op=ALU.mult),
                     reads=[bG, bsmall, bconst], writes=[bG])
                bk, bb = pbank()
                S.op("pe", lambda e, bk=bk: e.matmul(ps[:, bk, :], ones_f[0:64, 0:128], Dm[R64, 0:512], start=True, stop=True), reads=[bG, bconst], writes=[bb])
                for d in range(2):
                    S.op("dve", lambda e, bk=bk, d=d: e.tensor_copy(out=wmkB[:, :, 4 * d:4 * d + 4], in_=ps[:, bk, :].rearrange("p (j c) -> p j c", c=64)[:, :, 32 * d:32 * d + 4]),
                         reads=[bb], writes=[bwmkB])

                phase('M0%d' % l)
                areset(ext=True)
                x1T = abf(1024); x2T = abf(1024); vT = abf(1024); zT = abf(1024)
                z1T = abf(1024)
                vtok = abf(1024)
                KS = abf(2048)
                YY = abf(2048)
                HA = [abf(2048), abf(2048)]
                HB = [abf(2048), abf(2048)]
                acc = af32(1024)
                acc2 = af32(1024)
                kt = [af32(256), af32(256)]
                absk = [abf(256), abf(256), abf(256)]
                Et = [af32(256), af32(256)]
                absdec = [af32(256), af32(256)]
                tmpA = [abf(512)] * 2
                tmpB = [abf(512)] * 2
                w3b = [abf(256), abf(256)]
                hsc = af32(8)
                bx1, bx2, bv, bz, bz1, bvt, bKS, bYY, bacc = (Buf(n) for n in ("x1T", "x2T", "vT", "zT", "z1T", "vtok", "KS", "YY", "acc"))
                bacc2 = Buf("acc2")
                bHs = [Buf("H0"), Buf("H1")]
                bkt = [Buf("kt0"), Buf("kt1")]
                babs = [Buf("absk0"), Buf("absk1"), Buf("absk2")]
                bEt = [Buf("E0"), Buf("E1")]
                bdec, bw3 = [Buf("absdec0"), Buf("absdec1")], [Buf("w3b0"), Buf("w3b1")]
                btA, btB = [Buf("tA0")] * 2, [Buf("tB0")] * 2
                bhsc = [Buf("hsc0"), Buf("hsc1")]
                vtok3 = vtok.rearrange("p (t c) -> p t c", c=128)
                KS3 = KS.rearrange("p (j c) -> p j c", c=128)
                YY3 = YY.rearrange("p (j c) -> p j c", c=128)
                HA3 = [h.rearrange("p (j c) -> p j c", c=256) for h in HA]
                HB3 = [h.rearrange("p (j c) -> p j c", c=256) for h in HB]

                def g_filter(blk, o, hs):
                    c0 = blk * 128
                    wb3, bwb3 = w3b[hs], bw3[hs]
                    dec, bdc = absdec[hs], bdec[hs]
                    stg, bstg = Et[1], bEt[1]
                    for dr in range(2):
                        cc0 = o * 2048 + dr * 1024 + c0
                        S.dma("sp", "w3_%d" % hs, lambda e: e.dma_start(out=stg[0:64, dr * 128:(dr + 1) * 128], in_=hy_w3_d[l][:, cc0:cc0 + 128]), writes=[bstg])
                        S.dma("sp", "w3_%d" % hs, lambda e: e.dma_start(out=stg[64:65, dr * 128:(dr + 1) * 128], in_=hy_b3_d[l][:, cc0:cc0 + 128]), writes=[bstg])
                        S.dma("sp", "dec%d" % hs, lambda e: e.dma_start(out=dec[:, dr * 128:(dr + 1) * 128], in_=hy_dec_d[l][:, cc0:cc0 + 128].partition_broadcast(128)), writes=[bdc])
                    S.op("act", lambda e: e.copy(out=wb3[0:65, 0:256], in_=stg[0:65, 0:256]), reads=[bstg], writes=[bwb3])
                    S.op("act", lambda e: e.activation(out=dec, in_=dec, func=AF.Abs), reads=[bdc], writes=[bdc])
                    yield
                    l1col = ps[:, 7, 32 + hs:33 + hs]
                    pending_l1 = []
                    for tch in range(8):
                        k_t, bk_t = kt[tch % 2], bkt[tch % 2]
                        a_t, ba_t = absk[tch % 3], babs[tch % 3]
                        E_t, bE_t = Et[tch % 2], bEt[tch % 2]
                        hk, hb = pbank()
                        S.op("pe", lambda e: e.matmul(ps[:, hk, 0:256], hdn2[0:65, tch * 128:(tch + 1) * 128], wb3[0:65, 0:256], start=True, stop=True),
                             reads=[bhdn2, bwb3], writes=[hb])
                        if len(pending_l1) >= 2:
                            pending_l1.pop(0)()
                        S.op("act", lambda e: e.activation(out=E_t, in_=dec, func=AF.Exp, scale=ccv("tnneg", tch, 1)), reads=[bdc, bCC], writes=[bE_t])
                        S.op("dve", lambda e: e.scalar_tensor_tensor(out=k_t, in0=E_t, scalar=0.05, in1=ps[:, hk, 0:256], op0=ALU.add, op1=ALU.mult),
                             reads=[bE_t, hb], writes=[bk_t])
                        S.op("dve", lambda e: e.scalar_tensor_tensor(out=KS3[:, tch, :], in0=k_t[:, 128:256], scalar=ccv("maskb", tch, 1), in1=k_t[:, 0:128], op0=ALU.mult, op1=ALU.add),
                             reads=[bk_t, bCC], writes=[bKS])
                        S.op("dve", lambda e: e.scalar_tensor_tensor(out=KS3[:, 8 + tch, :], in0=k_t[:, 128:256], scalar=ccv("nmaskb", tch, 1), in1=k_t[:, 0:128], op0=ALU.mult, op1=ALU.add),
                             reads=[bk_t, bCC], writes=[bKS])
                        S.op("act", lambda e: e.activation(out=a_t[:, 0:128], in_=k_t[:, 0:128], func=AF.Abs), reads=[bk_t], writes=[ba_t])
                        S.op("act", lambda e: e.activation(out=a_t[:, 128:256], in_=k_t[:, 128:256], func=AF.Abs, scale=ccv("maskb", tch, 1)),
                             reads=[bk_t, bCC], writes=[ba_t])

                        def emit_l1(a_t=a_t, ba_t=ba_t, tch=tch):
                            def f(e):
                                e.matmul(l1col, a_t[:, 0:128], ones_bf[:, 0:1], start=(tch == 0), stop=False)
                                e.matmul(l1col, a_t[:, 128:256], ones_bf[:, 0:1], start=False, stop=(tch == 7))
                            S.op("pe", f, reads=[ba_t, bconst], writes=[pb[7]])
                        pending_l1.append(emit_l1)
                        yield
                    while pending_l1:
                        pending_l1.pop(0)()
                    yield ("start", 2 * blk + o)
                    S.op("dve", lambda e: e.tensor_scalar(out=hsc[:, 4 * hs:4 * hs + 1], in0=l1col, scalar1=ccv("invhs"), scalar2=sm("zero"), op0=ALU.mult, op1=ALU.add),
                         reads=[pb[7], bCC, bzero], writes=[bhsc[hs]])
                    S.op("dve", lambda e: e.reciprocal(out=hsc[:, 4 * hs + 1:4 * hs + 2], in_=hsc[:, 4 * hs:4 * hs + 1]), reads=[bhsc[hs]], writes=[bhsc[hs]])
                    S.op("dve", lambda e: e.tensor_tensor(out=hsc[:, 4 * hs + 2:4 * hs + 3], in0=hsc[:, 4 * hs:4 * hs + 1], in1=pp("hbias", o * 8 + blk, 1), op=ALU.mult),
                         reads=[bhsc[hs], bPP], writes=[bhsc[hs]])
                    for jp in range(4):
                        hk, hb = pbank()

                        def f(e):
                            for u in range(2):
                                jj = 2 * jp + u
                                for tch in range(8):
                                    e.matmul(ps[:, hk, u * 256:u * 256 + 128], W0[:, tch, jj * 128:(jj + 1) * 128], KS3[:, tch, :], start=(tch == 0), stop=(tch == 7))
                                for tch in range(8):
                                    e.matmul(ps[:, hk, u * 256 + 128:u * 256 + 256], W0[:, tch, 1024 + jj * 128:1024 + (jj + 1) * 128], KS3[:, 8 + tch, :], start=(tch == 0), stop=(tch == 7))
                        S.op("pe", f, reads=[bW0, bKS], writes=[hb])
                        pv = ps[:, hk, :].rearrange("p (u r c) -> p u r c", u=2, r=2)
                        S.op("act", lambda e: e.copy(out=HA3[hs][:, 2 * jp:2 * jp + 2, :], in_=ps[:, hk, :].rearrange("p (u c) -> p u c", u=2)), reads=[hb], writes=[bHs[hs]])
                        S.op("act", lambda e: e.copy(out=HB3[hs][:, 2 * jp:2 * jp + 2, 0:128], in_=pv[:, :, 1, :]), reads=[hb], writes=[bHs[hs]])
                        S.op("act", lambda e: e.copy(out=HB3[hs][:, 2 * jp:2 * jp + 2, 128:256], in_=pv[:, :, 0, :]), reads=[hb], writes=[bHs[hs]])
                        yield
                    if blk <= 1:
                        dbg("HA_%d_%d_%d" % (l, blk, o), HA[hs], [bHs[hs]], [128, 2048], BF16)
                        dbg("KS_%d_%d_%d" % (l, blk, o), KS, [bKS], [128, 2048], BF16)

                def g_pre(blk):
                    c0 = blk * 128
                    w, bw = wload([(w_in_d[l][:, c0:c0 + 128], 0), (w_in_d[l][:, 1024 + c0:1024 + c0 + 128], 128),
                                   (w_in_d[l][:, 2048 + c0:2048 + c0 + 128], 256), (w_in_d[l][:, 3072 + c0:3072 + c0 + 128], 384)])
                    for si, (dstT, bd) in enumerate(((vT, bv), (x1T, bx1), (x2T, bx2))):
                        sec = (2, 0, 1)[si]
                        pk, pbs = ppair()
                        proj_fm(pk, pbs, w, bw, sec * 128)
                        ch = sec * 8 + blk
                        ac_, bac_ = (acc, bacc) if si % 2 == 0 else (acc2, bacc2)
                        conv3(pk, pbs, ac_, bac_, pp("hcw", 0 * 24 + ch, 1), pp("hcw", 1 * 24 + ch, 1), pp("hcw", 2 * 24 + ch, 1), pp("hcb", ch, 1),
                              fixw[:, ch:ch + 1], fixw[:, 24 + ch:25 + ch])
                        S.op("act", lambda e: e.copy(out=dstT, in_=ac_), reads=[bac_], writes=[bd])
                        yield
                        if si == 0:
                            transpose_fm_to_tok(vT, bv, lambda tch: vtok3[:, tch, :], bvt)
                            yield
                    pk, pbs = ppair()
                    proj_fm(pk, pbs, w, bw, 384)
                    S.op("act", lambda e: e.activation(out=zT.rearrange("p (a b) -> p a b", a=2), in_=ps[:, pk:pk + 2, :], func=AF.Silu), reads=pbs, writes=[bz])
                    if blk == 0:
                        dbg("x1T%d" % l, x1T, [bx1], [128, T], BF16)
                        dbg("vT%d" % l, vT, [bv], [128, T], BF16)
                    yield

                def g_data(blk, o, hs):
                    zcurT, bzc = (vT, bv) if o == 0 else (z1T, bz1)
                    for jp in range(4):
                        hk, hb = pbank()
                        tA, btA_ = tmpA[jp % 2], btA[jp % 2]
                        tB, btB_ = tmpB[jp % 2], btB[jp % 2]

                        def f(e):
                            for u in range(2):
                                jj = 2 * jp + u
                                for tch in range(8):
                                    e.matmul(ps[:, hk, u * 256:u * 256 + 128], WS[:, tch, jj * 128:(jj + 1) * 128], vtok3[:, tch, :], start=(tch == 0), stop=(tch == 7))
                                for tch in range(8):
                                    e.matmul(ps[:, hk, u * 256 + 128:u * 256 + 256], WS[:, tch, 1024 + jj * 128:1024 + (jj + 1) * 128], vtok3[:, tch, :], start=(tch == 0), stop=(tch == 7))
                        S.op("pe", f, reads=[bWS, bvt], writes=[hb])
                        S.op("dve", lambda e: e.tensor_tensor(out=tA, in0=ps[:, hk, :], in1=HA3[hs][:, 2 * jp:2 * jp + 2, :].rearrange("p u c -> p (u c)"), op=ALU.mult),
                             reads=[hb, bHs[hs]], writes=[btA_])
                        S.op("dve", lambda e: e.tensor_tensor(out=tB, in0=ps[:, hk, :], in1=HB3[hs][:, 2 * jp:2 * jp + 2, :].rearrange("p u c -> p (u c)"), op=ALU.mult),
                             reads=[hb, bHs[hs]], writes=[btB_])
                        tAv = tA.rearrange("p (u r c) -> p u r c", u=2, r=2)
                        tBv = tB.rearrange("p (u r c) -> p u r c", u=2, r=2)
                        S.op("pool", lambda e: e.tensor_tensor(out=YY3[:, 2 * jp:2 * jp + 2, :], in0=tAv[:, :, 0, :], in1=tAv[:, :, 1, :], op=ALU.subtract), reads=[btA_], writes=[bYY])
                        S.op("pool", lambda e: e.tensor_tensor(out=YY3[:, 8 + 2 * jp:8 + 2 * jp + 2, :], in0=tBv[:, :, 0, :], in1=tBv[:, :, 1, :], op=ALU.add), reads=[btB_], writes=[bYY])
                        yield
                    if blk <= 1:
                        dbg("YY_%d_%d_%d" % (l, blk, o), YY, [bYY], [128, 2048], BF16)
                        dbg("vtok_%d_%d_%d" % (l, blk, o), vtok, [bvt], [128, 1024], BF16)
                    pk, pbs = ppair()

                    def f(e):
                        for half in range(2):
                            for j in range(16):
                                base = 0 if j < 8 else 1024
                                e.matmul(ps[:, pk + half, :], YY3[:, j, :], WS[:, j % 8, base + half * 512:base + (half + 1) * 512], start=(j == 0), stop=(j == 15))
                    S.op("pe", f, reads=[bWS, bYY], writes=pbs)
                    gT, bg = (x1T, bx1) if o == 0 else (x2T, bx2)
                    S.op("dve", lambda e: e.scalar_tensor_tensor(
                        out=acc.rearrange("p (a b) -> p a b", a=2), in0=zcurT.rearrange("p (a b) -> p a b", a=2), scalar=hsc[:, 4 * hs + 2:4 * hs + 3],
                        in1=ps[:, pk:pk + 2, :], op0=ALU.mult, op1=ALU.add), reads=pbs + [bzc, bhsc[hs]], writes=[bacc])
                    if o == 0:
                        S.op("dve", lambda e: e.scalar_tensor_tensor(out=z1T, in0=acc, scalar=hsc[:, 4 * hs + 1:4 * hs + 2], in1=gT, op0=ALU.mult, op1=ALU.mult),
                             reads=[bacc, bhsc[hs], bg], writes=[bz1])
                        yield
                        transpose_fm_to_tok(z1T, bz1, lambda tch: vtok3[:, tch, :], bvt)
                    else:
                        S.op("dve", lambda e: e.scalar_tensor_tensor(out=acc, in0=acc, scalar=hsc[:, 4 * hs + 1:4 * hs + 2], in1=gT, op0=ALU.mult, op1=ALU.mult),
                             reads=[bacc, bhsc[hs], bg], writes=[bacc])
                        S.op("pool", lambda e: e.tensor_tensor(out=yaT[:, blk, :], in0=acc, in1=zT, op=ALU.mult), reads=[bacc, bz], writes=[byaT[blk]])
                    yield

                stages = [(blk, o) for blk in range(8) for o in range(2)]

                def Dchain():
                    for si_, (blk, o) in enumerate(stages):
                        if o == 0:
                            yield from g_pre(blk)
                        yield ("dstart", si_)
                        yield from g_data(blk, o, si_ % 2)
                        yield ("done", si_)

                def Fchain():
                    for si_, (blk, o) in enumerate(stages):
                        if si_ == 0:
                            continue
                        yield from g_filter(blk, o, si_ % 2)
                        yield ("fdone", si_)

                for _ in g_filter(0, 0, 0):
                    pass
                if SEQ_DEBUG:
                    for si_, (blk, o) in enumerate(stages):
                        if si_ + 1 < len(stages):
                            for _ in g_filter(stages[si_ + 1][0], stages[si_ + 1][1], (si_ + 1) % 2):
                                pass
                        for _ in (g_pre(blk) if o == 0 else ()):
                            pass
                        for _ in g_data(blk, o, si_ % 2):
                            pass
                else:
                    Dg, Fg = Dchain(), Fchain()
                    d_done, f_next = -1, None
                    f_done, d_next = 0, None
                    ud = uf = 0.0
                    d_alive = f_alive = True
                    while d_alive or f_alive:
                        f_ok = f_alive and (f_next is None or d_done >= f_next - 2)
                        d_ok = d_alive and (d_next is None or f_done >= d_next)
                        if f_ok and (uf <= ud or not d_ok):
                            try:
                                r = next(Fg)
                                uf += 16.0
                                f_next = None
                                if isinstance(r, tuple):
                                    if r[0] == "start":
                                        f_next = r[1]
                                    else:
                                        f_done = r[1]
                            except StopIteration:
                                f_alive = False
                        elif d_ok:
                            try:
                                r = next(Dg)
                                ud += UD_UNITS
                                d_next = None
                                if isinstance(r, tuple):
                                    if r[0] == "dstart":
                                        d_next = r[1]
                                    else:
                                        d_done = r[1]
                            except StopIteration:
                                d_alive = False
                        else:
                            raise AssertionError("pipeline driver stuck")
                dbg("yaT%d" % l, yaT[:], byaT, [128, 8, T], BF16)

                phase('hy%d' % l)
                areset()

                def make_alloc(base, limit):
                    st = {"off": 0}

                    def f32(n):
                        o = st["off"]
                        st["off"] = o + n
                        assert st["off"] <= limit, ("arena overflow", st["off"])
                        return base[:, o:o + n]

                    def bf(n):
                        o = st["off"]
                        st["off"] = o + n // 2
                        assert st["off"] <= limit, ("arena overflow", st["off"])
                        return base[:, o:o + n // 2].bitcast(BF16)
                    return f32, bf

                def make_head(f32, bf, tag):
                    H = {"tag": tag}
                    H["qT"] = bf(2048); H["kT"] = bf(2048); H["ktok"] = bf(2048); H["Vaug"] = bf(8 * 258); H["og"] = bf(2048)
                    H["STw"] = [bf(128), bf(128)]; H["kw"] = [bf(256), bf(256)]
                    H["Cst"] = f32(2 * 2 * 257); H["Csb"] = [bf(2 * 258), bf(2 * 258)]
                    H["hsum"] = f32(2048); H["sgt"] = [bf(256), bf(256)]
                    H["den"] = f32(4); H["ss"] = f32(16)
                    for n in ("qT", "kT", "ktok", "og", "hsum", "hsum2", "Vaug", "den0", "den1", "ss"):
                        H["b_" + n] = Buf(tag + n)
                    for n in ("STw", "kw", "C", "Cs", "sgt"):
                        H["b_" + n] = [Buf(tag + n + "0"), Buf(tag + n + "1")]
                    return H

                HS = [make_head(af32, abf, "A"), make_head(*make_alloc(arenaB, 16384), "B")]

                def head_P(hh, H):
                    qT3 = H["qT"].rearrange("p (q t) -> p q t", q=2)
                    kT3 = H["kT"].rearrange("p (q t) -> p q t", q=2)
                    ktok3 = H["ktok"].rearrange("p (j c) -> p j c", c=256)
                    Va3 = H["Vaug"].rearrange("p (j c) -> p j c", c=258)
                    og3 = H["og"].rearrange("p (j c) -> p j c", c=256)
                    Cst4 = H["Cst"].rearrange("p (d q v) -> p d q v", d=2, q=2)
                    hsum = H["hsum"]
                    bq, bkT, bkt_, bog, bhs, bhs2, bVa, bC = H["b_qT"], H["b_kT"], H["b_ktok"], H["b_og"], H["b_hsum"], H["b_hsum2"], H["b_Vaug"], H["b_C"]
                    for d in range(2):
                        S.dma("sp", "cst" + H["tag"], lambda e, d=d: e.dma_start(out=Cst4[:, d, :, :], in_=c0_d[l, d, hh].rearrange("(q p) v -> p q v", p=128)), writes=[bC[d]])
                    S.op("pool", lambda e: e.memset(Va3[:, :, 256:258], 1.0), writes=[bVa])
                    c0 = hh * 256
                    w, bw = wload([(w_in_d[l][:, 4096 + c0:4096 + c0 + 256], 0), (w_in_d[l][:, 5120 + c0:5120 + c0 + 256], 256)])
                    for si, (dst3, bd) in enumerate(((qT3, bq), (kT3, bkT))):
                        for q2 in range(2):
                            pk, pbs = ppair()
                            proj_fm(pk, pbs, w, bw, si * 256 + q2 * 128)
                            ch = si * 8 + hh * 2 + q2
                            ac_, bac_ = (hsum[:, 0:1024], bhs) if q2 == 0 else (hsum[:, 1024:2048], bhs2)
                            conv3(pk, pbs, ac_, bac_, pp("mcw", 0 * 16 + ch, 1), pp("mcw", 1 * 16 + ch, 1), pp("mcw", 2 * 16 + ch, 1), pp("mcb", ch, 1),
                                  fixw[:, 48 + ch:49 + ch], fixw[:, 64 + ch:65 + ch])
                            S.op("act", lambda e: e.activation(out=dst3[:, q2, :], in_=ac_, func=AF.Silu), reads=[bac_], writes=[bd])
                            yield
                    if hh == 0:
                        dbg("qT%d" % l, H["qT"], [bq], [128, 2048], BF16)
                    for q2 in range(2):
                        transpose_fm_to_tok(kT3[:, q2, :], bkT, lambda tch, q2=q2: ktok3[:, tch, q2 * 128:(q2 + 1) * 128], bkt_, scale=1.0 / 16.0)
                        yield
                    wB, bwB = wload([(w_in_d[l][:, 6144 + c0:6144 + c0 + 256], 0), (w_in_d[l][:, 7168 + c0:7168 + c0 + 256], 256)])
                    for i in range(8):
                        bk, bb = pbank()

                        def f(e):
                            for dch in range(8):
                                e.matmul(ps[:, bk, :], hT[:, dch, i * 128:(i + 1) * 128], wB[:, dch, :], start=(dch == 0), stop=(dch == 7))
                        S.op("pe", f, reads=[bwB] + bhT, writes=[bb])
                        S.op("dve", lambda e: e.tensor_copy(out=Va3[:, i, 0:256], in_=ps[:, bk, 0:256]), reads=[bb], writes=[bVa])
                        S.op("act", lambda e: e.activation(out=og3[:, i, :], in_=ps[:, bk, 256:512], func=AF.Sigmoid), reads=[bb], writes=[bog])
                        yield
                    wC, bwC = wload([(w_in_d[l][:, 8192 + c0:8192 + c0 + 256], 0)])
                    for i in range(8):
                        bk, bb = pbank()

                        def f(e):
                            for dch in range(8):
                                e.matmul(ps[:, bk, 0:256], hT[:, dch, i * 128:(i + 1) * 128], wC[:, dch, 0:256], start=(dch == 0), stop=(dch == 7))
                        S.op("pe", f, reads=[bwC] + bhT, writes=[bb])
                        sg_, bsg_ = H["sgt"][i % 2], H["b_sgt"][i % 2]
                        S.op("act", lambda e: e.activation(out=sg_, in_=ps[:, bk, 0:256], func=AF.Silu), reads=[bb], writes=[bsg_])
                        S.op("pool", lambda e: e.tensor_tensor(out=og3[:, i, :], in0=og3[:, i, :], in1=sg_, op=ALU.mult), reads=[bsg_, bog], writes=[bog])
                        yield

                def head_R(hh, H):
                    qT3 = H["qT"].rearrange("p (q t) -> p q t", q=2)
                    kT3 = H["kT"].rearrange("p (q t) -> p q t", q=2)
                    ktok3 = H["ktok"].rearrange("p (j c) -> p j c", c=256)
                    Va3 = H["Vaug"].rearrange("p (j c) -> p j c", c=258)
                    og3 = H["og"].rearrange("p (j c) -> p j c", c=256)
                    Cst4 = H["Cst"].rearrange("p (d q v) -> p d q v", d=2, q=2)
                    Cs3 = [c.rearrange("p (q v) -> p q v", q=2) for c in H["Csb"]]
                    hsum3 = H["hsum"].rearrange("p (j c) -> p j c", c=256)
                    STw, kw, den, ss = H["STw"], H["kw"], H["den"], H["ss"]
                    bq, bkT, bkt_, bog, bhs, bhs2, bVa, bC = H["b_qT"], H["b_kT"], H["b_ktok"], H["b_og"], H["b_hsum"], H["b_hsum2"], H["b_Vaug"], H["b_C"]
                    bSTw, bkw, bCs, bss = H["b_STw"], H["b_kw"], H["b_Cs"], H["b_ss"]
                    bdens = [H["b_den0"], H["b_den1"]]
                    iters = [(k, d) for k in range(8) for d in range(2)]
                    prep = {}

                    def pre(it):
                        k, d = iters[it]
                        j = k if d == 0 else 7 - k
                        col = 4 * d + hh
                        wmk = wmkB[:, j, col:col + 1]
                        wa = waT[:, j, col:col + 1]
                        cs3, bcs = Cs3[d], bCs[d]
                        stw, bstw = STw[it % 2], bSTw[it % 2]
                        kwt, bkwt = kw[it % 2], bkw[it % 2]
                        msk = triu if d == 0 else tril
                        sk, sbk = pbank()

                        def f(e):
                            e.matmul(ps[:, sk, 0:128], kT3[:, 0, j * 128:(j + 1) * 128], qT3[:, 0, j * 128:(j + 1) * 128], start=True, stop=False)
                            e.matmul(ps[:, sk, 0:128], kT3[:, 1, j * 128:(j + 1) * 128], qT3[:, 1, j * 128:(j + 1) * 128], start=False, stop=True)
                        S.op("pe", f, reads=[bkT, bq], writes=[sbk])
                        S.op("dve", lambda e: e.scalar_tensor_tensor(out=stw, in0=ps[:, sk, 0:128], scalar=wa, in1=msk[:], op0=ALU.mult, op1=ALU.mult),
                             reads=[sbk, bwaT, bconst], writes=[bstw])
                        S.op("act", lambda e: e.activation(out=kwt, in_=ktok3[:, j, :], func=AF.Identity, scale=wa),
                             reads=[bkt_, bwaT], writes=[bkwt])
                        S.op("pool", lambda e: e.tensor_scalar(out=cs3[:, :, 0:257], in0=Cst4[:, d, :, :], scalar1=wmk, scalar2=sm("zero"), op0=ALU.mult, op1=ALU.add),
                             reads=[bC[d], bwmkB, bzero], writes=[bcs])
                        prep[it] = (k, d, j, col, wmk, cs3, bcs, stw, bstw, kwt, bkwt)

                    def body(it):
                        k, d, j, col, wmk, cs3, bcs, stw, bstw, kwt, bkwt = prep.pop(it)
                        eb = ebT[:, j, col:col + 1]
                        nk, nb = pbank()

                        def f(e):
                            e.matmul(ps[:, nk, 0:257], stw, Va3[:, j, 0:257], start=True, stop=False)
                            e.matmul(ps[:, nk, 0:257], qT3[:, 0, j * 128:(j + 1) * 128], cs3[:, 0, 0:257], start=False, stop=False)
                            e.matmul(ps[:, nk, 0:257], qT3[:, 1, j * 128:(j + 1) * 128], cs3[:, 1, 0:257], start=False, stop=True)
                        S.op("pe", f, reads=[bstw, bVa, bq, bcs], writes=[nb])
                        uks = []
                        for q2 in range(2):
                            uk, ub = pbank()
                            S.op("pe", lambda e: e.matmul(ps[:, uk, 0:257], kwt[:, q2 * 128:(q2 + 1) * 128], Va3[:, j, 0:257], start=True, stop=True),
                                 reads=[bkwt, bVa], writes=[ub])
                            uks.append((uk, ub))
                        for q2 in range(2):
                            uk, ub = uks[q2]
                            S.op("dve", lambda e: e.scalar_tensor_tensor(out=Cst4[:, d, q2, :], in0=Cst4[:, d, q2, :], scalar=wmk, in1=ps[:, uk, 0:257], op0=ALU.mult, op1=ALU.add),
                                 reads=[ub, bC[d], bwmkB], writes=[bC[d]])
                        dn, rdn, bdn = den[:, 2 * (it % 2):2 * (it % 2) + 1], den[:, 2 * (it % 2) + 1:2 * (it % 2) + 2], bdens[it % 2]
                        S.op("act", lambda e: e.activation(out=dn, in_=ps[:, nk, 256:257], func=AF.Abs), reads=[nb], writes=[bdn])
                        S.op("dve", lambda e: e.tensor_tensor(out=dn, in0=dn, in1=eb, op=ALU.max), reads=[bdn, bebT], writes=[bdn])
                        S.op("dve", lambda e: e.reciprocal(out=rdn, in_=dn), reads=[bdn], writes=[bdn])
                        first = (d == 0 and j <= 3) or (d == 1 and j >= 4)
                        if first:
                            S.op("dve", lambda e: e.tensor_scalar(out=hsum3[:, j, :], in0=ps[:, nk, 0:256], scalar1=rdn, scalar2=sm("zero"), op0=ALU.mult, op1=ALU.add),
                                 reads=[nb, bdn, bzero], writes=[bhs, bhs2])
                        else:
                            S.op("dve", lambda e: e.scalar_tensor_tensor(out=hsum3[:, j, :], in0=ps[:, nk, 0:256], scalar=rdn, in1=hsum3[:, j, :], op0=ALU.mult, op1=ALU.add),
                                 reads=[nb, bdn], writes=[bhs])
                        if k % 2 == 1:
                            slot = (k - 1) // 2 if d == 0 else 3 - (k - 1) // 2
                            S.dma("sp", "scout" + H["tag"], lambda e: e.dma_start(out=sc_d[l, slot, d, hh].rearrange("(q p) v -> p q v", p=128), in_=Cst4[:, d, :, :]),
                                  reads=[bC[d]])

                    pre(0)
                    for it in range(16):
                        if it + 1 < 16:
                            pre(it + 1)
                        body(it)
                        yield
                    junk, bjunk = H["sgt"][0], H["b_sgt"][0]
                    for j in range(8):
                        S.op("act", lambda e: e.activation(out=junk, in_=hsum3[:, j, :], func=AF.Square, accum_out=ss[:, j:j + 1]), reads=[bhs], writes=[bjunk, bss])
                    yield
                    S.op("act", lambda e: e.activation(out=ss[:, 8:16], in_=ss[:, 0:8], func=AF.Sqrt, scale=1.0 / 256.0, bias=EPS), reads=[bss], writes=[bss])
                    S.op("dve", lambda e: e.reciprocal(out=ss[:, 8:16], in_=ss[:, 8:16]), reads=[bss], writes=[bss])
                    for j in range(8):
                        S.op("dve", lambda e: e.scalar_tensor_tensor(out=og3[:, j, :], in0=hsum3[:, j, :], scalar=ss[:, 8 + j:9 + j], in1=og3[:, j, :], op0=ALU.mult, op1=ALU.mult),
                             reads=[bhs, bss, bog], writes=[bog])
                    yield
                    for q2 in range(2):
                        ch = hh * 2 + q2
                        for g in range(2):
                            bk, bb = pbank()
                            psb = ps[:, bk, :].bitcast(BF16)

                            def f(e):
                                for q in range(4):
                                    i = 4 * g + q
                                    e.transpose(out=psb[:, q * 128:(q + 1) * 128], in_=og3[:, i, q2 * 128:(q2 + 1) * 128], identity=identb[:])
                            S.op("pe", f, reads=[bog, bconst], writes=[bb])
                            S.op("act", lambda e: e.activation(out=ybT[:, ch, g * 512:(g + 1) * 512], in_=psb[:, 0:512], func=AF.Identity, scale=pp("mng", ch, 1)),
                                 reads=[bb, bPP], writes=[bybT[ch]])
                            yield

                def interleave2(ga, gb):
                    alive = [ga, gb]
                    while alive:
                        for g in list(alive):
                            try:
                                next(g)
                            except StopIteration:
                                alive.remove(g)

                for _ in head_P(0, HS[0]):
                    pass
                for hh in range(4):
                    if hh + 1 < 4:
                        interleave2(head_R(hh, HS[hh % 2]), head_P(hh + 1, HS[(hh + 1) % 2]))
                    else:
                        for _ in head_R(hh, HS[hh % 2]):
                            pass
                phase('ml%dend' % l)
                dbg("ybT%d" % l, ybT[:], bybT, [128, 8, T], BF16)

                phase('ml%d' % l)
                areset()
                if l == 0:
                    load_tables()
                mT = abf(8 * 1024)
                mT3 = mT.rearrange("p (c t) -> p c t", c=8)
                bm = [Buf("mT%d" % i) for i in range(8)]
                sg = abf(1024)
                t1 = abf(1024)
                bsg_, bt1 = Buf("sg"), Buf("t1")
                for r4 in range(2):
                    wga, bwga = wload([(w_in_d[l][:, 9232 + r4 * 512:9232 + (r4 + 1) * 512], 0)], prefetch=False)
                    wpa, bwpa = wload([(w_pa_d[l][:, r4 * 512:(r4 + 1) * 512], 0)], prefetch=False)
                    pend = []
                    for rq in range(4):
                        r = r4 * 4 + rq
                        pkg, pbg = ppair()
                        pkp, pbp = ppair()
                        proj_fm(pkg, pbg, wga, bwga, rq * 128)
                        proj_fm(pkp, pbp, wpa, bwpa, rq * 128, rhs=yaT, brhs=byaT)
                        S.op("act", lambda e, pkg=pkg: e.activation(out=sg.rearrange("p (a b) -> p a b", a=2), in_=ps[:, pkg:pkg + 2, :], func=AF.Sigmoid), reads=pbg, writes=[bsg_])
                        S.op("dve", lambda e, pkp=pkp, r=r: e.tensor_tensor(out=mT3[:, r, :].rearrange("p (a b) -> p a b", a=2), in0=ps[:, pkp:pkp + 2, :], in1=sg.rearrange("p (a b) -> p a b", a=2), op=ALU.mult),
                             reads=pbp + [bsg_], writes=[bm[r]])
                    wgb, bwgb = wload([(w_in_d[l][:, 10256 + r4 * 512:10256 + (r4 + 1) * 512], 0)], prefetch=False)
                    wpb, bwpb = wload([(w_pb_d[l][:, r4 * 512:(r4 + 1) * 512], 0)], prefetch=False)
                    for rq in range(4):
                        r = r4 * 4 + rq
                        pkg, pbg = ppair()
                        pkp, pbp = ppair()
                        proj_fm(pkg, pbg, wgb, bwgb, rq * 128)
                        proj_fm(pkp, pbp, wpb, bwpb, rq * 128, rhs=ybT, brhs=bybT)
                        S.op("act", lambda e, pkg=pkg: e.activation(out=sg.rearrange("p (a b) -> p a b", a=2), in_=ps[:, pkg:pkg + 2, :], func=AF.Sigmoid), reads=pbg, writes=[bsg_])
                        S.op("dve", lambda e, pkp=pkp: e.tensor_tensor(out=t1.rearrange("p (a b) -> p a b", a=2), in0=ps[:, pkp:pkp + 2, :], in1=sg.rearrange("p (a b) -> p a b", a=2), op=ALU.mult),
                             reads=pbp + [bsg_], writes=[bt1])
                        S.op("pool", lambda e, r=r: e.tensor_tensor(out=mT3[:, r, :], in0=mT3[:, r, :], in1=t1, op=ALU.add), reads=[bt1, bm[r]], writes=[bm[r]])
                for r4 in range(2):
                    wo, bwo = wload([(w_out_d[l][:, r4 * 512:(r4 + 1) * 512], 0)])
                    for rq in range(4):
                        r = r4 * 4 + rq
                        pk, pbs = ppair()
                        proj_fm(pk, pbs, wo, bwo, rq * 128, rhs=mT3, brhs=bm)
                        S.op("dve", lambda e, pk=pk, r=r: e.scalar_tensor_tensor(out=X[:, r, :].rearrange("p (a b) -> p a b", a=2), in0=ps[:, pk:pk + 2, :], scalar=sm("modsb", 16 + r, 1),
                                                                             in1=X[:, r, :].rearrange("p (a b) -> p a b", a=2), op0=ALU.mult, op1=ALU.add),
                             reads=pbs + [bsmall, bX[r]], writes=[bX[r]])
                dbg("X%d" % l, X[:], bX, [128, 8, T])

            phase('merge')
            areset()
            rstd, br = rms_rstd()
            yf = [af32(1024), af32(1024)]
            byf = [Buf("yf0"), Buf("yf1")]
            yst = [af32(512) for _ in range(4)]
            byst = [Buf("yst%d" % i) for i in range(4)]
            y_v = y_d.rearrange("(i p) d -> p i d", p=128)
            n = 0
            for dch in range(8):
                yt, by = yf[dch % 2], byf[dch % 2]
                S.op("dve", lambda e: e.scalar_tensor_tensor(out=yt, in0=X[:, dch, :], scalar=pp("fing", dch, 1), in1=rstd, op0=ALU.mult, op1=ALU.mult),
                     reads=[bX[dch], bPP, br], writes=[by])
                for g in range(2):
                    bk, bb = pbank()
                    ys, bys = yst[n % 4], byst[n % 4]
                    n += 1

                    def f(e):
                        for q in range(4):
                            i = 4 * g + q
                            e.transpose(out=ps[:, bk, q * 128:(q + 1) * 128], in_=yt[:, i * 128:(i + 1) * 128], identity=identf[:])
                    S.op("pe", f, reads=[by, bconst], writes=[bb])
                    if g == 0:
                        S.op("act", lambda e: e.copy(out=ys, in_=ps[:, bk, :]), reads=[bb], writes=[bys])
                    else:
                        S.op("dve", lambda e: e.tensor_copy(out=ys, in_=ps[:, bk, :]), reads=[bb], writes=[bys])
                    S.dma("sp", "yout%d" % (n % 4), lambda e: e.dma_start(out=y_v[:, 4 * g:4 * g + 4, dch * 128:(dch + 1) * 128], in_=ys.rearrange("p (q d) -> p q d", q=4)), reads=[bys])

        except _Stop:
            pass
        S.final_wait("sp")
        S.emit()
        stats = dict(ops=S.nop, waits=S.nwait)
    return nc, stats, list(dbg_outs.keys())


def _tables(L):
    NS = T // L
    t = np.arange(T)
    tl = (t % L).astype(np.float64)
    f = np.arange(L)
    Wsym = np.zeros((T, 2 * T), np.float32)
    W0 = np.zeros((T, 2 * T), np.float32)
    tt = np.arange(L)
    th = np.pi * np.outer(tt + 0.5, f + 0.5) / L
    th0 = np.pi * np.outer(tt, f + 0.5) / L
    for s in range(NS):
        r = slice(s * L, (s + 1) * L)
        Wsym[r, s * L:(s + 1) * L] = np.cos(th)
        Wsym[r, T + s * L:T + (s + 1) * L] = -np.sin(th)
        W0[r, s * L:(s + 1) * L] = np.cos(th0)
        W0[r, T + s * L:T + (s + 1) * L] = -np.sin(th0)
    bands = np.arange(1, 17)
    tn = tl / L
    ang = 2 * np.pi * tn[:, None] * bands[None, :]
    feats = np.concatenate([tn[:, None], np.cos(ang), np.sin(ang)], axis=-1).astype(np.float32)
    maskb = (tl != 0).astype(np.float32)
    cpb = L // 128
    keep = np.array([0.0 if (k % cpb == 0 and k > 0) else 1.0 for k in range(8)], np.float32)
    cc = np.zeros((128, NCC), np.float32)

    def put(name, arr):
        o, w = CCO[name]
        cc[:, o:o + w] = arr
    put("tnneg", -(tn.reshape(8, 128).T))
    put("maskb", maskb.reshape(8, 128).T)
    put("nmaskb", -(maskb.reshape(8, 128).T))
    put("bfixneg", -1.0 if L == 256 else 0.0)
    put("invhs", float(L) / NS)
    put("keep", keep[None, :])
    keepc = np.zeros((128, 8), np.float32)
    keepc[:, :] = keep[None, :]
    keepc[32:64, :] = keep[::-1][None, :]
    put("keepc", keepc)
    import ml_dtypes
    return dict(wsym=Wsym.astype(ml_dtypes.bfloat16), w0t=W0.astype(ml_dtypes.bfloat16), featsT=np.ascontiguousarray(feats.T), cc=cc)


def _chunks(v, n):
    return np.ascontiguousarray(np.asarray(v, np.float32).reshape(n, 128).T)


_PROG = {}


def kernel(x_prompt, x_sample, state_C, state_n, state_m, c, c_ctx, norm_g, w_ada, b_ada,
           w_in, hy_conv_w, hy_conv_b, hy_w1, hy_b1, hy_w2, hy_b2, hy_w3, hy_b3, hy_freq,
           hy_decay, hy_bias, ml_conv_w, ml_conv_b, ml_if_b, ml_norm_g, w_pa, w_pb, w_out, final_g):
    f32 = np.float32
    A = lambda a: np.ascontiguousarray(np.asarray(a, f32))
    x_prompt, x_sample = A(x_prompt), A(x_sample)
    pp = np.zeros((2, 128, NPP), f32)

    def put(l, name, arr, rows=slice(0, 128)):
        o, w = PPO[name]
        pp[l, rows, o:o + w] = arr
    wgi = np.zeros((2, D, 128), f32)
    wgf = np.zeros((2, D, 128), f32)
    for l in range(2):
        put(l, "b_ada", _chunks(b_ada[l], 24))
        put(l, "norm_g", _chunks(norm_g[l], 8))
        put(l, "hcw", np.concatenate([_chunks(hy_conv_w[l][j], 24) for j in range(3)], 1))
        put(l, "hcb", _chunks(hy_conv_b[l], 24))
        put(l, "hbias", np.concatenate([_chunks(hy_bias[l][o], 8) for o in range(2)], 1))
        put(l, "mcw", np.concatenate([_chunks(ml_conv_w[l][j], 16) for j in range(3)], 1))
        put(l, "mcb", _chunks(ml_conv_b[l], 16))
        put(l, "mng", _chunks(ml_norm_g[l], 8))
        put(l, "hb1", np.asarray(hy_b1[l], f32)[:, None], slice(0, 64))
        put(l, "hb2", np.asarray(hy_b2[l], f32)[:, None], slice(0, 64))
        put(l, "hfr", np.asarray(hy_freq[l], f32).T, slice(0, 64))
        put(l, "fing", _chunks(final_g, 8))
        g0 = 9216
        for d in range(2):
            for h in range(4):
                wgi[l, :, 32 * d + h] = w_in[l][:, g0 + d * 8 + 0 * 4 + h]
                wgf[l, :, 32 * d + h] = w_in[l][:, g0 + d * 8 + 1 * 4 + h]
                pp[l, 32 * d + h, PPO["gbi"][0]] = ml_if_b[l][d, 0, h]
                pp[l, 32 * d + h, PPO["gbf"][0]] = ml_if_b[l][d, 1, h]
    shared = dict(
        pp=pp, w_in=A(w_in), w_ada=A(w_ada), w_pa=A(w_pa), w_pb=A(w_pb), w_out=A(w_out),
        hy_w3=A(hy_w3), hy_b3=A(hy_b3).reshape(2, 1, 4096), hy_dec=A(hy_decay).reshape(2, 1, 4096),
        hy_w1=A(hy_w1), hy_w2=A(hy_w2), wgi=wgi, wgf=wgf,
        identf=np.eye(128, dtype=f32), triu=(np.triu(np.ones((128, 128), f32)) / 16.0).astype(f32),
        tril=(np.tril(np.ones((128, 128), f32)) / 16.0).astype(f32),
    )
    resetm = np.ones((64, T), f32)
    resetm[:, ::128] = 0.0
    shared["resetm"] = resetm
    tabs = {256: _tables(256), 1024: _tables(1024)}
    in_maps = []
    for core in range(8):
        m = dict(shared)
        if core < 4:
            L = 1024
            m["xin"] = x_sample[core]
            cv = np.asarray(c[core], f32)
            c0 = np.concatenate([np.asarray(state_C[core], f32), np.asarray(state_n[core], f32)[..., None]], axis=-1)
            m0 = np.zeros((2, 64, 1), f32)
            for l in range(2):
                for d in range(2):
                    m0[l, 32 * d:32 * d + 4, 0] = state_m[core][l, d]
        else:
            L = 256
            b0 = 4 * (core - 4)
            m["xin"] = x_prompt[b0:b0 + 4].reshape(T, D)
            cv = np.asarray(c_ctx, f32)
            c0 = np.zeros((2, 2, 4, 256, 257), f32)
            m0 = np.zeros((2, 64, 1), f32)
        tb = tabs[L]
        cc = tb["cc"].copy()
        o, w = CCO["cvec"]
        cc[:, o:o + w] = _chunks(cv, 8)
        m.update(wsym=tb["wsym"], w0t=tb["w0t"], featsT=tb["featsT"], cc=cc, c0aug=np.ascontiguousarray(c0), m0=m0)
        in_maps.append(m)

    if "nc" not in _PROG:
        _PROG["nc"] = build_program()
    nc, stats, dbgn = _PROG["nc"]
    res = run_bass_kernel_spmd(nc, in_maps, core_ids=list(range(8)))
    R = res.results
    _PROG["last"] = R
    y_sample = np.stack([np.asarray(R[i]["y"], f32) for i in range(4)], 0)
    y_prompt = np.concatenate([np.asarray(R[i]["y"], f32).reshape(4, 256, D) for i in range(4, 8)], 0)
    new_C = np.zeros((16, 2, 2, 4, 256, 256), f32)
    new_n = np.zeros((16, 2, 2, 4, 256), f32)
    new_m = np.zeros((16, 2, 2, 4), f32)
    for i in range(4, 8):
        sc = np.asarray(R[i]["sc"], f32)
        smo = np.asarray(R[i]["sm"], f32)
        for slot in range(4):
            b = 4 * (i - 4) + slot
            new_C[b] = sc[:, slot, :, :, :, :256]
            new_n[b] = sc[:, slot, :, :, :, 256]
            for d in range(2):
                k = 2 * slot + 1 if d == 0 else 2 * (3 - slot) + 1
                new_m[b, :, d, :] = smo[:, 32 * d:32 * d + 4, k]
    return (y_prompt, y_sample, new_C, new_n, new_m)
```

```python
import math
from contextlib import ExitStack

import numpy as np
import concourse.bass as bass
import concourse.mybir as mybir
from concourse.alu_op_type import AluOpType as ALU
from concourse.bass_utils import run_bass_kernel_spmd

F32 = mybir.dt.float32
BF16 = mybir.dt.bfloat16
AF = mybir.ActivationFunctionType
AX = mybir.AxisListType

T = 1024
D = 1024
N_IN = 11280
MAGIC = 12582912.0
PI_LO = 3.1415925
EPS = 1e-6
DEBUG = False
SEQ_DEBUG = False
UD_UNITS = 26.0


class Buf:
    __slots__ = ("name", "w", "r", "excl")

    def __init__(self, name, excl=False):
        self.name = name
        self.w = None
        self.r = []
        self.excl = excl


class _Rec:
    def __init__(self):
        self.calls = []

    def __getattr__(self, name):
        def m(*a, **k):
            self.calls.append((name, a, k))
            return None
        return m


def _record(fn):
    r = _Rec()
    fn(r)
    assert r.calls
    return r.calls


class Sched:
    ENGS = ("pe", "act", "dve", "pool", "sp")

    def __init__(self, nc, es):
        self.nc = nc
        self.es = es
        self.streams = {e: [] for e in self.ENGS}
        self.esem = {}
        for e in ("pe", "act", "dve", "pool"):
            self.esem[e] = es.enter_context(nc.semaphore("s_" + e))
        self.ecount = {e: 0 for e in ("pe", "act", "dve", "pool")}
        self.dsem = {}
        self.dcount = {}
        self.waited = {e: {} for e in self.ENGS}
        self.nwait = 0
        self.nop = 0

    def _dma_sem(self, key):
        if key not in self.dsem:
            self.dsem[key] = self.es.enter_context(self.nc.semaphore("d_" + key))
            self.dcount[key] = 0
        return self.dsem[key]

    def _resolve(self, ev):
        kind, key, val = ev
        if kind == "eng":
            return ("e:" + key, self.esem[key], val)
        return ("d:" + key, self.dsem[key], self.dcount[key])

    def _emit_waits(self, eng, evs):
        need = {}
        for ev in evs:
            if ev is None:
                continue
            if ev[0] == "eng" and ev[1] == eng and eng == "pe":
                continue
            name, sem, val = self._resolve(ev)
            if val <= self.waited[eng].get(name, 0):
                continue
            if name not in need or need[name][1] < val:
                need[name] = (sem, val)
        for name, (sem, val) in need.items():
            self.waited[eng][name] = val
            self.streams[eng].append(("wait", sem, val))
            self.nwait += 1

    @staticmethod
    def _deps(reads, writes, eng=None):
        evs = []
        for b in reads:
            evs.append(b.w)
            if b.excl:
                evs.extend(r for r in b.r if not (r[0] == "eng" and r[1] == eng))
        for b in writes:
            evs.append(b.w)
            evs.extend(b.r)
        return evs

    def _commit(self, ev, reads, writes):
        for b in reads:
            b.r = [r for r in b.r if not (r[0] == ev[0] and r[1] == ev[1])]
            b.r.append(ev)
        for b in writes:
            b.w = ev
            b.r = []
        self.nop += 1

    def op(self, eng, fn, reads=(), writes=()):
        self._emit_waits(eng, self._deps(reads, writes, eng))
        self.ecount[eng] += 1
        ev = ("eng", eng, self.ecount[eng])
        self.streams[eng].append(("op", _record(fn), self.esem[eng], 1))
        self._commit(ev, reads, writes)
        return ev

    def dma(self, queue, key, fn, reads=(), writes=()):
        key = queue + "_" + key
        sem = self._dma_sem(key)
        deps = [ev for ev in self._deps(reads, writes) if not (ev is not None and ev[0] == "dma" and ev[1] == key)]
        self._emit_waits(queue, deps)
        self.dcount[key] += 16
        ev = ("dma", key, None)
        self.streams[queue].append(("op", _record(fn), sem, 16))
        self._commit(ev, reads, writes)
        return ev

    def barrier(self):
        for eng in ("pe", "act", "dve", "pool"):
            evs = [("eng", e, self.ecount[e]) for e in ("pe", "act", "dve", "pool") if self.ecount[e] > 0]
            evs += [("dma", k, None) for k in self.dsem]
            self._emit_waits(eng, evs)
        evs = [("eng", e, self.ecount[e]) for e in ("pe", "act", "dve", "pool") if self.ecount[e] > 0]
        evs += [("dma", k, None) for k in self.dsem]
        self._emit_waits("sp", evs)

    def final_wait(self, eng="sp"):
        for key, sem in self.dsem.items():
            name = "d:" + key
            val = self.dcount[key]
            if val > self.waited[eng].get(name, 0):
                self.waited[eng][name] = val
                self.streams[eng].append(("wait", sem, val))

    def emit(self):
        nc = self.nc
        handles = {"pe": "tensor", "act": "scalar", "dve": "vector", "pool": "gpsimd", "sp": "sync"}
        with nc.Block() as block:
            for e in self.ENGS:
                items = self.streams[e]

                def body(engine, items=items):
                    for it in items:
                        if it[0] == "wait":
                            engine.wait_ge(it[1], it[2])
                        else:
                            inst = None
                            for (name, a, k) in it[1]:
                                inst = getattr(engine, name)(*a, **k)
                            inst.then_inc(it[2], it[3])

                getattr(block, handles[e])(body)


PPO = {}
_o = 0
for _n, _w in [("b_ada", 24), ("norm_g", 8), ("hcw", 72), ("hcb", 24), ("hbias", 16), ("mcw", 48), ("mcb", 16),
               ("mng", 8), ("hb1", 1), ("hb2", 1), ("hfr", 2), ("gbi", 1), ("gbf", 1), ("fing", 8)]:
    PPO[_n] = (_o, _w)
    _o += _w
NPP = _o
CCO = {}
_o = 0
for _n, _w in [("tnneg", 8), ("maskb", 8), ("nmaskb", 8), ("bfixneg", 1), ("invhs", 1), ("keep", 8), ("keepc", 8), ("cvec", 8)]:
    CCO[_n] = (_o, _w)
    _o += _w
NCC = _o

ARENA_F32 = 9888


class _Stop(Exception):
    pass


STOP_AT = None


def build_program():
    nc = bass.Bass("TRN2", target_bir_lowering=False)
    es = ExitStack()
    with es:
        S = Sched(nc, es)
        phases = []

        def phase(name):
            phases.append(name)
            if STOP_AT is not None and name == STOP_AT:
                raise _Stop()

        def din(name, shape, dt=F32):
            return nc.dram_tensor(name, shape, dt, kind="ExternalInput").ap()

        def dout(name, shape, dt=F32):
            return nc.dram_tensor(name, shape, dt, kind="ExternalOutput").ap()

        def sb(name, shape, dt=F32):
            return es.enter_context(nc.sbuf_tensor(name, shape, dt))

        xin = din("xin", [T, D])
        wsym_d = din("wsym", [T, 2 * T], BF16)
        w0_d = din("w0t", [T, 2 * T], BF16)
        feats_d = din("featsT", [33, T])
        cc_d = din("cc", [128, NCC])
        resetm_d = din("resetm", [64, T])
        identf_d = din("identf", [128, 128])
        triu_d = din("triu", [128, 128])
        tril_d = din("tril", [128, 128])
        c0_d = din("c0aug", [2, 2, 4, 256, 257])
        m0_d = din("m0", [2, 64, 1])
        pp_d = din("pp", [2, 128, NPP])
        w_in_d = din("w_in", [2, D, N_IN])
        w_ada_d = din("w_ada", [2, D, 3 * D])
        w_pa_d = din("w_pa", [2, D, D])
        w_pb_d = din("w_pb", [2, D, D])
        w_out_d = din("w_out", [2, D, D])
        hy_w3_d = din("hy_w3", [2, 64, 4096])
        hy_b3_d = din("hy_b3", [2, 1, 4096])
        hy_dec_d = din("hy_dec", [2, 1, 4096])
        hy_w1_d = din("hy_w1", [2, 33, 64])
        hy_w2_d = din("hy_w2", [2, 64, 64])
        wgi_d = din("wgi", [2, D, 128])
        wgf_d = din("wgf", [2, D, 128])
        y_d = dout("y", [T, D])
        sc_d = dout("sc", [2, 4, 2, 4, 256, 257])
        sm_d = dout("sm", [2, 64, 8])
        dbg_outs = {}

        X = sb("X", [128, 8, T])
        hT = sb("hT", [128, 8, T], BF16)
        tabs = sb("tabs", [128, 2 * 8 * 2 * T], BF16)
        WS = tabs[:, 0:8 * 2 * T].rearrange("p (c f) -> p c f", c=8)
        W0 = tabs[:, 8 * 2 * T:2 * 8 * 2 * T].rearrange("p (c f) -> p c f", c=8)
        arenaB = tabs[:].bitcast(F32)
        yaT = sb("yaT", [128, 8, T], BF16)
        wbufs = [sb("wbuf%d" % i, [128, 8, 512], BF16) for i in range(2)]
        PP = sb("PP", [128, NPP])
        CC = sb("CC", [128, NCC])
        identf = sb("identf_s", [128, 128])
        identb = sb("identb", [128, 128], BF16)
        triu = sb("triu_s", [128, 128])
        tril = sb("tril_s", [128, 128])
        ones_bf = sb("ones_bf", [128, 128], BF16)
        ones_f = sb("ones_f", [128, 128])
        resetm = sb("resetm_s", [64, T], BF16)
        hdn2 = sb("hdn2", [65, T], BF16)
        waT = sb("waT", [128, 8, 8])
        ebT = sb("ebT", [128, 8, 8])
        wmkB = sb("wmkB", [128, 8, 8])
        small = sb("small", [128, 128])
        fixw = sb("fixw", [128, 80])
        arena = sb("arena", [128, ARENA_F32 + 4096])
        ybT = arena[:, ARENA_F32:ARENA_F32 + 4096].bitcast(BF16).rearrange("p (c t) -> p c t", c=8)
        ps = es.enter_context(nc.psum_tensor("ps", [128, 8, 512], F32))

        bX = [Buf("X%d" % i) for i in range(8)]
        bhT = [Buf("hT%d" % i) for i in range(8)]
        bWS, bW0 = Buf("WS"), Buf("W0")
        byaT = [Buf("yaT%d" % i) for i in range(8)]
        bybT = [Buf("ybT%d" % i) for i in range(8)]
        bwbuf = [Buf("wbuf0"), Buf("wbuf1")]
        bPP, bCC, bconst = Buf("PP"), Buf("CC"), Buf("const")
        bhdn2 = Buf("hdn2")
        bwaT, bebT, bwmkB = Buf("waT"), Buf("ebT"), Buf("wmkB")
        bsmall = Buf("small")
        bden, bss, bfix, bzero = Buf("den"), Buf("ss"), Buf("fix"), Buf("zero")
        bden2 = Buf("den2")
        pb = [Buf("psb%d" % i, excl=True) for i in range(8)]

        def pp(name, a=0, n=None, rows=slice(0, 128)):
            o, w = PPO[name]
            n = w - a if n is None else n
            return PP[rows, o + a:o + a + n]

        def ccv(name, a=0, n=None, rows=slice(0, 128)):
            o, w = CCO[name]
            n = w - a if n is None else n
            return CC[rows, o + a:o + a + n]

        SM = {"modsb": (0, 24), "gs": (24, 8), "sc": (32, 8), "fb": (40, 2), "ngbf": (42, 1), "m0": (43, 1), "meff": (44, 1),
              "amax": (48, 8), "negtot": (56, 8), "Rc": (64, 8), "dmc": (72, 8), "mst": (80, 8), "wmk": (88, 8),
              "ss": (96, 8), "rs": (104, 8), "den": (112, 2), "w0fix": (114, 1), "w2fix": (115, 1), "zero": (116, 1), "den2": (118, 2)}

        def sm(name, a=0, n=None, rows=slice(0, 128)):
            o, w = SM[name]
            n = w - a if n is None else n
            return small[rows, o + a:o + a + n]

        scb = sb("scb", [128, 8], BF16)

        astate = {"off": 0, "lim": ARENA_F32}

        def areset(ext=False):
            S.barrier()
            astate["off"] = 0
            astate["lim"] = ARENA_F32 + (4096 if ext else 0)

        def af32(n):
            o = astate["off"]
            astate["off"] = o + n
            assert astate["off"] <= astate["lim"], ("arena overflow", astate["off"])
            return arena[:, o:o + n]

        def abf(n):
            assert n % 2 == 0
            o = astate["off"]
            astate["off"] = o + n // 2
            assert astate["off"] <= astate["lim"], ("arena overflow", astate["off"])
            return arena[:, o:o + n // 2].bitcast(BF16)

        pstate = {"one": 0, "pair": 0}

        def pbank():
            i = pstate["one"]
            pstate["one"] = (i + 1) % 7
            return i, pb[i]

        def ppair():
            k = pstate["pair"]
            pstate["pair"] = (k + 1) % 3
            return 2 * k, [pb[2 * k], pb[2 * k + 1]]

        def dbg(name, ap, bufs, shape, dt=F32):
            if not DEBUG:
                return
            o = dout("dbg_" + name, shape, dt)
            dbg_outs[name] = o
            S.dma("sp", "dbg", lambda e: e.dma_start(out=o, in_=ap), reads=bufs)

        wstate = {"idx": 0}
        plan = []
        for l_ in range(2):
            for cb in range(6):
                plan.append([(w_ada_d[l_][:, cb * 512:(cb + 1) * 512], 0)])
            plan.append([(wgi_d[l_], 0), (wgf_d[l_], 128)])
            for blk in range(8):
                c0 = blk * 128
                plan.append([(w_in_d[l_][:, c0:c0 + 128], 0), (w_in_d[l_][:, 1024 + c0:1024 + c0 + 128], 128),
                             (w_in_d[l_][:, 2048 + c0:2048 + c0 + 128], 256), (w_in_d[l_][:, 3072 + c0:3072 + c0 + 128], 384)])
            for hh in range(4):
                c0 = hh * 256
                plan.append([(w_in_d[l_][:, 4096 + c0:4096 + c0 + 256], 0), (w_in_d[l_][:, 5120 + c0:5120 + c0 + 256], 256)])
                plan.append([(w_in_d[l_][:, 6144 + c0:6144 + c0 + 256], 0), (w_in_d[l_][:, 7168 + c0:7168 + c0 + 256], 256)])
                plan.append([(w_in_d[l_][:, 8192 + c0:8192 + c0 + 256], 0)])
            for r4 in range(2):
                plan.append([(w_in_d[l_][:, 9232 + r4 * 512:9232 + (r4 + 1) * 512], 0)])
                plan.append([(w_pa_d[l_][:, r4 * 512:(r4 + 1) * 512], 0)])
                plan.append([(w_in_d[l_][:, 10256 + r4 * 512:10256 + (r4 + 1) * 512], 0)])
                plan.append([(w_pb_d[l_][:, r4 * 512:(r4 + 1) * 512], 0)])
            for r4 in range(2):
                plan.append([(w_out_d[l_][:, r4 * 512:(r4 + 1) * 512], 0)])
        issued = [False] * len(plan)

        def _issue(i):
            if issued[i]:
                return
            issued[i] = True
            wb, bb = wbufs[i % 2], bwbuf[i % 2]
            for src, off in plan[i]:
                n = src.shape[1]
                S.dma("pool", "w%d" % (i % 2),
                      lambda e, src=src, off=off, n=n, wb=wb: e.dma_start(
                          out=wb[:, :, off:off + n], in_=src.rearrange("(c p) n -> p c n", p=128)),
                      writes=[bb])

        def wload(pieces, prefetch=True):
            i = wstate["idx"]
            wstate["idx"] += 1
            assert len(pieces) == len(plan[i]) and pieces[0][0].shape[1] == plan[i][0][0].shape[1], ("weight plan mismatch", i)
            _issue(i)
            if prefetch and i + 1 < len(plan):
                _issue(i + 1)
            return wbufs[i % 2], bwbuf[i % 2]

        try:
            S.dma("sp", "const", lambda e: e.dma_start(out=CC[:], in_=cc_d[:, :]), writes=[bCC])
            S.dma("sp", "const", lambda e: e.dma_start(out=identf[:], in_=identf_d[:, :]), writes=[bconst])
            S.dma("sp", "const", lambda e: e.dma_start(out=triu[:], in_=triu_d[:, :]), writes=[bconst])
            S.dma("sp", "const", lambda e: e.dma_start(out=tril[:], in_=tril_d[:, :]), writes=[bconst])
            S.dma("pool", "const", lambda e: e.dma_start(out=identb[:], in_=identf_d[:, :]), writes=[bconst])
            S.dma("pool", "const", lambda e: e.dma_start(out=resetm[:], in_=resetm_d[:, :]), writes=[bconst])
            S.op("pool", lambda e: e.memset(ones_bf[:], 1.0), writes=[bconst])
            S.op("pool", lambda e: e.memset(ones_f[:], 1.0), writes=[bconst])
            S.op("pool", lambda e: e.memset(hdn2[64:65, :], 1.0), writes=[bhdn2])
            S.op("pool", lambda e: e.memset(small[:], 0.0), writes=[bsmall, bden, bden2, bss, bfix, bzero])
            S.op("act", lambda e: e.activation(out=scb[:], in_=ccv("cvec"), func=AF.Silu), reads=[bCC], writes=[bsmall])

            phase('const')
            areset()
            xst = [af32(1024), af32(1024)]
            bxst = [Buf("xst0"), Buf("xst1")]
            for i in range(8):
                xt, bx = xst[i % 2], bxst[i % 2]
                S.dma("sp", "xin%d" % (i % 2), lambda e, i=i, xt=xt: e.dma_start(out=xt, in_=xin[i * 128:(i + 1) * 128, :]), writes=[bx])
                for g in range(2):
                    bk, bb = pbank()

                    def f(e, xt=xt, g=g, bk=bk):
                        r = None
                        for q in range(4):
                            r = e.transpose(out=ps[:, bk, q * 128:(q + 1) * 128], in_=xt[:, (4 * g + q) * 128:(4 * g + q + 1) * 128], identity=identf[:])
                        return r
                    S.op("pe", f, reads=[bx, bconst], writes=[bb])
                    eng = "act" if g == 0 else "dve"
                    if eng == "act":
                        S.op("act", lambda e, g=g, i=i, bk=bk: e.copy(out=X[:, 4 * g:4 * g + 4, i * 128:(i + 1) * 128], in_=ps[:, bk, :].rearrange("p (q t) -> p q t", q=4)),
                             reads=[bb], writes=bX[4 * g:4 * g + 4])
                    else:
                        S.op("dve", lambda e, g=g, i=i, bk=bk: e.tensor_copy(out=X[:, 4 * g:4 * g + 4, i * 128:(i + 1) * 128], in_=ps[:, bk, :].rearrange("p (q t) -> p q t", q=4)),
                             reads=[bb], writes=bX[4 * g:4 * g + 4])

            def load_tables():
                for c in range(8):
                    S.dma("sp", "tabs", lambda e, c=c: e.dma_start(out=W0[:, c, :], in_=w0_d[c * 128:(c + 1) * 128, :]), writes=[bW0])
                    S.dma("sp", "tabs", lambda e, c=c: e.dma_start(out=WS[:, c, :], in_=wsym_d[c * 128:(c + 1) * 128, :]), writes=[bWS])
            load_tables()
            phase('xload')
            def rms_rstd():
                xsq = [abf(1024), abf(1024)]
                bxsq = [Buf("xsq0"), Buf("xsq1")]
                r1 = af32(1024)
                rstd = af32(1024)
                br = Buf("rstd")
                pk, pbs = ppair()
                for dch in range(8):
                    xs, bx = xsq[dch % 2], bxsq[dch % 2]
                    S.op("act", lambda e, dch=dch, xs=xs: e.activation(out=xs, in_=X[:, dch, :], func=AF.Square), reads=[bX[dch]], writes=[bx])

                    def f(e, dch=dch, xs=xs):
                        e.matmul(ps[:, pk, :], ones_bf[:], xs[:, 0:512], start=(dch == 0), stop=(dch == 7))
                        return e.matmul(ps[:, pk + 1, :], ones_bf[:], xs[:, 512:1024], start=(dch == 0), stop=(dch == 7))
                    S.op("pe", f, reads=[bx, bconst], writes=pbs)
                S.op("act", lambda e: e.activation(out=r1.rearrange("p (a b) -> p a b", a=2), in_=ps[:, pk:pk + 2, :], func=AF.Sqrt, scale=1.0 / D, bias=EPS),
                     reads=pbs, writes=[br])
                S.op("dve", lambda e: e.reciprocal(out=rstd, in_=r1), reads=[br], writes=[br])
                return rstd, br

            def proj_fm(pk, pbs, w, bw, col, rhs=None, brhs=None):
                rhs = hT if rhs is None else rhs
                brhs = bhT if brhs is None else brhs

                def f(e):
                    r = None
                    for half in range(2):
                        for dch in range(8):
                            r = e.matmul(ps[:, pk + half, :], w[:, dch, col:col + 128], rhs[:, dch, half * 512:(half + 1) * 512],
                                         start=(dch == 0), stop=(dch == 7))
                    return r
                S.op("pe", f, reads=[bw] + list(brhs), writes=pbs)

            def conv3(pk, pbs, acc, bacc, w0, w1, w2, bias, fixw0, fixw2):
                pin = ps[:, pk:pk + 2, :]
                flat = pin.rearrange("p a b -> p (a b)")
                S.op("act", lambda e: e.activation(out=acc.rearrange("p (a b) -> p a b", a=2), in_=pin, func=AF.Identity, scale=w1, bias=bias),
                     reads=pbs + [bPP], writes=[bacc])
                S.op("dve", lambda e: e.scalar_tensor_tensor(out=acc[:, 1:1024], in0=flat[:, 0:1023], scalar=w0, in1=acc[:, 1:1024], op0=ALU.mult, op1=ALU.add),
                     reads=pbs + [bPP], writes=[bacc])
                S.op("dve", lambda e: e.scalar_tensor_tensor(out=acc[:, 0:1023], in0=flat[:, 1:1024], scalar=w2, in1=acc[:, 0:1023], op0=ALU.mult, op1=ALU.add),
                     reads=pbs + [bPP], writes=[bacc])
                S.op("dve", lambda e: e.scalar_tensor_tensor(out=acc[:, 256:1024:256], in0=flat[:, 255:1023:256], scalar=fixw0, in1=acc[:, 256:1024:256], op0=ALU.mult, op1=ALU.add),
                     reads=pbs + [bfix], writes=[bacc])
                S.op("dve", lambda e: e.scalar_tensor_tensor(out=acc[:, 255:1023:256], in0=flat[:, 256:1024:256], scalar=fixw2, in1=acc[:, 255:1023:256], op0=ALU.mult, op1=ALU.add),
                     reads=pbs + [bfix], writes=[bacc])

            def transpose_fm_to_tok(src, bsrc, dst_fn, bdst, scale=None):
                for g in range(2):
                    bk, bb = pbank()
                    psb = ps[:, bk, :].bitcast(BF16)

                    def f(e, g=g, psb=psb):
                        r = None
                        for q in range(4):
                            tch = 4 * g + q
                            r = e.transpose(out=psb[:, q * 128:(q + 1) * 128], in_=src[:, tch * 128:(tch + 1) * 128], identity=identb[:])
                        return r
                    S.op("pe", f, reads=[bsrc, bconst], writes=[bb])
                    for q in range(4):
                        tch = 4 * g + q
                        if scale is None:
                            S.op("act" if q % 2 == 0 else "dve",
                                 (lambda e, q=q, tch=tch, psb=psb: e.copy(out=dst_fn(tch), in_=psb[:, q * 128:(q + 1) * 128])) if q % 2 == 0 else
                                 (lambda e, q=q, tch=tch, psb=psb: e.tensor_copy(out=dst_fn(tch), in_=psb[:, q * 128:(q + 1) * 128])),
                                 reads=[bb], writes=[bdst])
                        else:
                            S.op("act", lambda e, q=q, tch=tch, psb=psb: e.activation(out=dst_fn(tch), in_=psb[:, q * 128:(q + 1) * 128], func=AF.Identity, scale=scale),
                                 reads=[bb], writes=[bdst])

            for l in range(2):
                areset()
                S.dma("sp", "pp", lambda e, l=l: e.dma_start(out=PP[:], in_=pp_d[l]), writes=[bPP])

                for (nm_, j_, o_, n_) in (("hcw", 0, 0, 24), ("hcw", 2, 24, 24), ("mcw", 0, 48, 16), ("mcw", 2, 64, 16)):
                    S.op("dve", lambda e: e.tensor_scalar(out=fixw[:, o_:o_ + n_], in0=pp(nm_, j_ * n_, n_), scalar1=ccv("bfixneg"), scalar2=sm("zero"), op0=ALU.mult, op1=ALU.add),
                         reads=[bPP, bCC, bzero], writes=[bfix])
                phase('norm%d' % l)
                feats = af32(1024)
                arg = af32(1024)
                kq = af32(1024)
                hd1 = af32(1024)
                w1s = af32(64)
                w2s = af32(64)
                bF = Buf("filt")
                S.dma("sp", "hyw", lambda e: e.dma_start(out=feats[0:33, :], in_=feats_d[:, :]), writes=[bF])
                S.dma("sp", "hyw", lambda e, l=l: e.dma_start(out=w1s[0:33, :], in_=hy_w1_d[l]), writes=[bF])
                S.dma("sp", "hyw", lambda e, l=l: e.dma_start(out=w2s[0:64, :], in_=hy_w2_d[l]), writes=[bF])
                S.op("dve", lambda e: e.tensor_tensor(out=sm("fb", 0, 1, slice(0, 64)), in0=pp("hfr", 0, 1, slice(0, 64)), in1=pp("hb1", 0, 1, slice(0, 64)), op=ALU.mult),
                     reads=[bPP], writes=[bsmall])
                S.op("dve", lambda e: e.tensor_tensor(out=sm("fb", 1, 1, slice(0, 64)), in0=pp("hfr", 1, 1, slice(0, 64)), in1=pp("hb2", 0, 1, slice(0, 64)), op=ALU.mult),
                     reads=[bPP], writes=[bsmall])
                for layer in range(2):
                    pk, pbs = ppair()
                    if layer == 0:
                        def f(e, pk=pk):
                            e.matmul(ps[0:64, pk, :], w1s[0:33, 0:64], feats[0:33, 0:512], start=True, stop=True)
                            return e.matmul(ps[0:64, pk + 1, :], w1s[0:33, 0:64], feats[0:33, 512:1024], start=True, stop=True)
                    else:
                        def f(e, pk=pk):
                            e.matmul(ps[0:64, pk, :], w2s[0:64, 0:64], hd1[0:64, 0:512], start=True, stop=True)
                            return e.matmul(ps[0:64, pk + 1, :], w2s[0:64, 0:64], hd1[0:64, 512:1024], start=True, stop=True)
                    S.op("pe", f, reads=[bF], writes=pbs)
                    S.op("dve", lambda e, pk=pk, layer=layer: e.tensor_scalar(
                        out=arg[0:64, :].rearrange("p (a b) -> p a b", a=2), in0=ps[0:64, pk:pk + 2, :],
                        scalar1=pp("hfr", layer, 1, slice(0, 64)), scalar2=sm("fb", layer, 1, slice(0, 64)), op0=ALU.mult, op1=ALU.add),
                        reads=pbs + [bPP, bsmall], writes=[bF])
                    S.op("dve", lambda e: e.tensor_scalar(out=kq[0:64, :], in0=arg[0:64, :], scalar1=1.0 / (2 * math.pi), scalar2=MAGIC, op0=ALU.mult, op1=ALU.add),
                         reads=[bF], writes=[bF])
                    S.op("dve", lambda e: e.tensor_scalar(out=kq[0:64, :], in0=kq[0:64, :], scalar1=-MAGIC, scalar2=0.0, op0=ALU.add, op1=ALU.add),
                         reads=[bF], writes=[bF])
                    S.op("dve", lambda e: e.scalar_tensor_tensor(out=arg[0:64, :], in0=kq[0:64, :], scalar=-2 * math.pi, in1=arg[0:64, :], op0=ALU.mult, op1=ALU.add),
                         reads=[bF], writes=[bF])
                    S.op("dve", lambda e: e.tensor_scalar(out=arg[0:64, :], in0=arg[0:64, :], scalar1=PI_LO, scalar2=-PI_LO, op0=ALU.min, op1=ALU.max),
                         reads=[bF], writes=[bF])
                    if layer == 0:
                        S.op("act", lambda e: e.activation(out=hd1[0:64, :], in_=arg[0:64, :], func=AF.Sin), reads=[bF], writes=[bF])
                    else:
                        S.op("act", lambda e: e.activation(out=hdn2[0:64, :], in_=arg[0:64, :], func=AF.Sin), reads=[bF], writes=[bhdn2])

                rstd, br = rms_rstd()
                bk, bb = 7, pb[7]
                for cb in range(6):
                    w, bw = wload([(w_ada_d[l][:, cb * 512:(cb + 1) * 512], 0)])

                    def f(e, cb=cb, w=w, bk=bk):
                        r = None
                        for q in range(4):
                            for kch in range(8):
                                r = e.matmul(ps[:, bk, cb * 4 + q:cb * 4 + q + 1], w[:, kch, q * 128:(q + 1) * 128], scb[:, kch:kch + 1],
                                             start=(kch == 0), stop=(kch == 7))
                        return r
                    S.op("pe", f, reads=[bw, bsmall], writes=[bb])
                S.op("dve", lambda e, bk=bk: e.tensor_tensor(out=sm("modsb"), in0=ps[:, bk, 0:24], in1=pp("b_ada"), op=ALU.add), reads=[bb, bPP], writes=[bsmall])
                S.op("dve", lambda e: e.scalar_tensor_tensor(out=sm("gs"), in0=sm("modsb", 8, 8), scalar=1.0, in1=pp("norm_g"), op0=ALU.add, op1=ALU.mult),
                     reads=[bsmall, bPP], writes=[bsmall])

                phase('ada%d' % l)
                tmpf = [af32(1024), af32(1024)]
                btmp = [Buf("tmpf0"), Buf("tmpf1")]
                for dch in range(8):
                    tf, bt = tmpf[dch % 2], btmp[dch % 2]
                    S.op("dve", lambda e, dch=dch, tf=tf: e.scalar_tensor_tensor(out=tf, in0=X[:, dch, :], scalar=sm("gs", dch, 1), in1=rstd, op0=ALU.mult, op1=ALU.mult),
                         reads=[bX[dch], bsmall, br], writes=[bt])
                    S.op("act", lambda e, dch=dch, tf=tf: e.activation(out=hT[:, dch, :], in_=tf, func=AF.Identity, bias=sm("modsb", dch, 1), scale=1.0),
                         reads=[bt, bsmall], writes=[bhT[dch]])
                dbg("hT%d" % l, hT[:], bhT, [128, 8, T], BF16)

                phase('F0%d' % l)
                areset()
                w, bw = wload([(wgi_d[l], 0), (wgf_d[l], 128)])
                pki, pbi = ppair()
                pkf, pbf = ppair()
                proj_fm(pki, pbi, w, bw, 0)
                proj_fm(pkf, pbf, w, bw, 128)
                R64 = slice(0, 64)
                Gi = af32(1024)
                ee = af32(1024)
                sp_ = af32(1024)
                cs = af32(1024)
                csx = af32(1024)
                aa = af32(1024)
                bG = Buf("gates")
                mcur0 = sm("m0", 0, 1, R64)
                S.dma("sp", "m0", lambda e, l=l: e.dma_start(out=mcur0, in_=m0_d[l]), writes=[bsmall])
                S.op("dve", lambda e: e.tensor_scalar(out=sm("ngbf", 0, 1, R64), in0=pp("gbf", 0, 1, R64), scalar1=-1.0, scalar2=0.0, op0=ALU.mult, op1=ALU.add),
                     reads=[bPP], writes=[bsmall])
                S.op("act", lambda e: e.activation(out=Gi[R64, :].rearrange("p (a b) -> p a b", a=2), in_=ps[R64, pki:pki + 2, :], func=AF.Identity,
                                                   bias=pp("gbi", 0, 1, R64), scale=1.0), reads=pbi + [bPP], writes=[bG])
                S.op("act", lambda e: e.activation(out=ee[R64, :].rearrange("p (a b) -> p a b", a=2), in_=ps[R64, pkf:pkf + 2, :], func=AF.Exp,
                                                   bias=sm("ngbf", 0, 1, R64), scale=-1.0), reads=pbf + [bsmall], writes=[bG])
                S.op("act", lambda e: e.activation(out=sp_[R64, :], in_=ee[R64, :], func=AF.Ln, bias=1.0, scale=1.0), reads=[bG], writes=[bG])
                S.op("dve", lambda e: e.tensor_tensor_scan(out=cs[R64, :], data0=resetm[R64, :], data1=sp_[R64, :], initial=0.0, op0=ALU.mult, op1=ALU.add),
                     reads=[bG, bconst], writes=[bG])
                R32a, R32b = slice(0, 32), slice(32, 64)
                S.op("pool", lambda e: e.tensor_copy(out=csx[R32a, :], in_=cs[R32a, :]), reads=[bG], writes=[bG])
                S.op("dve", lambda e: e.tensor_tensor(out=csx[R32b, :], in0=sp_[R32b, :], in1=cs[R32b, :], op=ALU.subtract), reads=[bG], writes=[bG])
                S.op("dve", lambda e: e.tensor_tensor(
                    out=csx[R32b, :].rearrange("p (j t) -> p j t", t=128), in0=csx[R32b, :].rearrange("p (j t) -> p j t", t=128),
                    in1=cs[R32b, :].rearrange("p (j t) -> p j t", t=128)[:, :, 127:128].to_broadcast([32, 8, 128]), op=ALU.add), reads=[bG], writes=[bG])
                S.op("dve", lambda e: e.tensor_tensor(out=aa[R64, :], in0=Gi[R64, :], in1=csx[R64, :], op=ALU.add), reads=[bG], writes=[bG])
                S.op("dve", lambda e: e.tensor_reduce(out=sm("amax", 0, 8, R64), in_=aa[R64, :].rearrange("p (j t) -> p j t", t=128), axis=AX.X, op=ALU.max),
                     reads=[bG], writes=[bsmall])
                S.op("dve", lambda e: e.tensor_scalar(out=sm("negtot", 0, 8, R64).unsqueeze(2), in0=cs[R64, :].rearrange("p (j t) -> p j t", t=128)[:, :, 127:128],
                                                      scalar1=-1.0, scalar2=0.0, op0=ALU.mult, op1=ALU.add), reads=[bG], writes=[bsmall])
                for k in range(8):
                    mcur = mcur0 if k == 0 else sm("mst", k - 1, 1, R64)
                    S.op("dve", lambda e, k=k, mcur=mcur: e.tensor_tensor(out=sm("meff", 0, 1, R64), in0=mcur, in1=ccv("keep", k, 1, R64), op=ALU.mult),
                         reads=[bsmall, bCC], writes=[bsmall])
                    for rr, jj in ((R32a, k), (R32b, 7 - k)):
                        S.op("dve", lambda e, rr=rr, jj=jj: e.tensor_tensor(out=sm("Rc", jj, 1, rr), in0=sm("meff", 0, 1, rr), in1=sm("amax", jj, 1, rr), op=ALU.max),
                             reads=[bsmall], writes=[bsmall])
                        S.op("dve", lambda e, rr=rr, jj=jj: e.tensor_tensor(out=sm("dmc", jj, 1, rr), in0=sm("meff", 0, 1, rr), in1=sm("Rc", jj, 1, rr), op=ALU.subtract),
                             reads=[bsmall], writes=[bsmall])
                        S.op("dve", lambda e, rr=rr, jj=jj, k=k: e.tensor_tensor(out=sm("mst", k, 1, rr), in0=sm("negtot", jj, 1, rr), in1=sm("Rc", jj, 1, rr), op=ALU.add),
                             reads=[bsmall], writes=[bsmall])
                S.dma("sp", "smout", lambda e, l=l: e.dma_start(out=sm_d[l], in_=sm("mst", 0, 8, R64)), reads=[bsmall])
                S.op("act", lambda e: e.activation(out=sm("wmk", 0, 8, R64), in_=sm("dmc", 0, 8, R64), func=AF.Exp), reads=[bsmall], writes=[bsmall])
                S.op("dve", lambda e: e.tensor_tensor(out=sm("wmk", 0, 8, R64), in0=sm("wmk", 0, 8, R64), in1=ccv("keepc", 0, 8, R64), op=ALU.mult),
                     reads=[bsmall, bCC], writes=[bsmall])
                Rbc = sm("Rc", 0, 8, R64).unsqueeze(2).to_broadcast([64, 8, 128])
                S.op("dve", lambda e: e.tensor_tensor(out=Gi[R64, :].rearrange("p (j t) -> p j t", t=128), in0=aa[R64, :].rearrange("p (j t) -> p j t", t=128), in1=Rbc, op=ALU.subtract),
                     reads=[bG, bsmall], writes=[bG])
                S.op("dve", lambda e: e.tensor_tensor(out=ee[R64, :].rearrange("p (j t) -> p j t", t=128), in0=csx[R64, :].rearrange("p (j t) -> p j t", t=128), in1=Rbc, op=ALU.subtract),
                     reads=[bG, bsmall], writes=[bG])
                S.op("act", lambda e: e.activation(out=Gi[R64, :], in_=Gi[R64, :], func=AF.Exp), reads=[bG], writes=[bG])
                S.op("act", lambda e: e.activation(out=ee[R64, :], in_=ee[R64, :], func=AF.Exp), reads=[bG], writes=[bG])
                for src, dst, bdst in ((Gi, waT, bwaT), (ee, ebT, bebT)):
                    bk, bb = pbank()

                    def f(e, src=src, bk=bk):
                        r = None
                        for j in range(8):
                            r = e.transpose(out=ps[:, bk, j * 64:(j + 1) * 64], in_=src[R64, j * 128:(j + 1) * 128], identity=identf[0:64, 0:64])
                        return r
                    S.op("pe", f, reads=[bG, bconst], writes=[bb])
                    for d in range(2):
                        S.op("dve", lambda e, dst=dst, bk=bk, d=d: e.tensor_copy(out=dst[:, :, 4 * d:4 * d + 4], in_=ps[:, bk, :].rearrange("p (j c) -> p j c", c=64)[:, :, 32 * d:32 * d + 4]),
                             reads=[bb], writes=[bdst])
                Dm = csx
                S.op("dve", lambda e: e.tensor_tensor(out=Dm[R64, 0:512].rearrange("p (j c) -> p j c", c=64),
                                                      in0=identf[0:64, 0:64].unsqueeze(1).to_broadcast([64, 8, 64]),
                                                      in1=sm("wmk", 0, 8, R64).unsqueeze(2).to_broadcast([64, 8, 64]), op=ALU.mult),
                     reads=[bG, bsmall, bconst], writes=[bG])
                bk, bb = pbank()
                S.op("pe", lambda e, bk=bk: e.matmul(ps[:, bk, :], ones_f[0:64, 0:128], Dm[R64, 0:512], start=True, stop=True), reads=[bG, bconst], writes=[bb])
                for d in range(2):
                    S.op("dve", lambda e, bk=bk, d=d: e.tensor_copy(out=wmkB[:, :, 4 * d:4 * d + 4], in_=ps[:, bk, :].rearrange("p (j c) -> p j c", c=64)[:, :, 32 * d:32 * d + 4]),
                         reads=[bb], writes=[bwmkB])

                phase('M0%d' % l)
                areset(ext=True)
                x1T = abf(1024); x2T = abf(1024); vT = abf(1024); zT = abf(1024)
                z1T = abf(1024)
                vtok = abf(1024)
                KS = abf(2048)
                YY = abf(2048)
                HA = [abf(2048), abf(2048)]
                HB = [abf(2048), abf(2048)]
                acc = af32(1024)
                acc2 = af32(1024)
                kt = [af32(256), af32(256)]
                absk = [abf(256), abf(256), abf(256)]
                Et = [af32(256), af32(256)]
                absdec = [af32(256), af32(256)]
                tmpA = [abf(512)] * 2
                tmpB = [abf(512)] * 2
                w3b = [abf(256), abf(256)]
                hsc = af32(8)
                bx1, bx2, bv, bz, bz1, bvt, bKS, bYY, bacc = (Buf(n) for n in ("x1T", "x2T", "vT", "zT", "z1T", "vtok", "KS", "YY", "acc"))
                bacc2 = Buf("acc2")
                bHs = [Buf("H0"), Buf("H1")]
                bkt = [Buf("kt0"), Buf("kt1")]
                babs = [Buf("absk0"), Buf("absk1"), Buf("absk2")]
                bEt = [Buf("E0"), Buf("E1")]
                bdec, bw3 = [Buf("absdec0"), Buf("absdec1")], [Buf("w3b0"), Buf("w3b1")]
                btA, btB = [Buf("tA0")] * 2, [Buf("tB0")] * 2
                bhsc = [Buf("hsc0"), Buf("hsc1")]
                vtok3 = vtok.rearrange("p (t c) -> p t c", c=128)
                KS3 = KS.rearrange("p (j c) -> p j c", c=128)
                YY3 = YY.rearrange("p (j c) -> p j c", c=128)
                HA3 = [h.rearrange("p (j c) -> p j c", c=256) for h in HA]
                HB3 = [h.rearrange("p (j c) -> p j c", c=256) for h in HB]

                def g_filter(blk, o, hs):
                    c0 = blk * 128
                    wb3, bwb3 = w3b[hs], bw3[hs]
                    dec, bdc = absdec[hs], bdec[hs]
                    stg, bstg = Et[1], bEt[1]
                    for dr in range(2):
                        cc0 = o * 2048 + dr * 1024 + c0
                        S.dma("sp", "w3_%d" % hs, lambda e: e.dma_start(out=stg[0:64, dr * 128:(dr + 1) * 128], in_=hy_w3_d[l][:, cc0:cc0 + 128]), writes=[bstg])
                        S.dma("sp", "w3_%d" % hs, lambda e: e.dma_start(out=stg[64:65, dr * 128:(dr + 1) * 128], in_=hy_b3_d[l][:, cc0:cc0 + 128]), writes=[bstg])
                        S.dma("sp", "dec%d" % hs, lambda e: e.dma_start(out=dec[:, dr * 128:(dr + 1) * 128], in_=hy_dec_d[l][:, cc0:cc0 + 128].partition_broadcast(128)), writes=[bdc])
                    S.op("act", lambda e: e.copy(out=wb3[0:65, 0:256], in_=stg[0:65, 0:256]), reads=[bstg], writes=[bwb3])
                    S.op("act", lambda e: e.activation(out=dec, in_=dec, func=AF.Abs), reads=[bdc], writes=[bdc])
                    yield
                    l1col = ps[:, 7, 32 + hs:33 + hs]
                    pending_l1 = []
                    for tch in range(8):
                        k_t, bk_t = kt[tch % 2], bkt[tch % 2]
                        a_t, ba_t = absk[tch % 3], babs[tch % 3]
                        E_t, bE_t = Et[tch % 2], bEt[tch % 2]
                        hk, hb = pbank()
                        S.op("pe", lambda e: e.matmul(ps[:, hk, 0:256], hdn2[0:65, tch * 128:(tch + 1) * 128], wb3[0:65, 0:256], start=True, stop=True),
                             reads=[bhdn2, bwb3], writes=[hb])
                        if len(pending_l1) >= 2:
                            pending_l1.pop(0)()
                        S.op("act", lambda e: e.activation(out=E_t, in_=dec, func=AF.Exp, scale=ccv("tnneg", tch, 1)), reads=[bdc, bCC], writes=[bE_t])
                        S.op("dve", lambda e: e.scalar_tensor_tensor(out=k_t, in0=E_t, scalar=0.05, in1=ps[:, hk, 0:256], op0=ALU.add, op1=ALU.mult),
                             reads=[bE_t, hb], writes=[bk_t])
                        S.op("dve", lambda e: e.scalar_tensor_tensor(out=KS3[:, tch, :], in0=k_t[:, 128:256], scalar=ccv("maskb", tch, 1), in1=k_t[:, 0:128], op0=ALU.mult, op1=ALU.add),
                             reads=[bk_t, bCC], writes=[bKS])
                        S.op("dve", lambda e: e.scalar_tensor_tensor(out=KS3[:, 8 + tch, :], in0=k_t[:, 128:256], scalar=ccv("nmaskb", tch, 1), in1=k_t[:, 0:128], op0=ALU.mult, op1=ALU.add),
                             reads=[bk_t, bCC], writes=[bKS])
                        S.op("act", lambda e: e.activation(out=a_t[:, 0:128], in_=k_t[:, 0:128], func=AF.Abs), reads=[bk_t], writes=[ba_t])
                        S.op("act", lambda e: e.activation(out=a_t[:, 128:256], in_=k_t[:, 128:256], func=AF.Abs, scale=ccv("maskb", tch, 1)),
                             reads=[bk_t, bCC], writes=[ba_t])

                        def emit_l1(a_t=a_t, ba_t=ba_t, tch=tch):
                            def f(e):
                                e.matmul(l1col, a_t[:, 0:128], ones_bf[:, 0:1], start=(tch == 0), stop=False)
                                e.matmul(l1col, a_t[:, 128:256], ones_bf[:, 0:1], start=False, stop=(tch == 7))
                            S.op("pe", f, reads=[ba_t, bconst], writes=[pb[7]])
                        pending_l1.append(emit_l1)
                        yield
                    while pending_l1:
                        pending_l1.pop(0)()
                    yield ("start", 2 * blk + o)
                    S.op("dve", lambda e: e.tensor_scalar(out=hsc[:, 4 * hs:4 * hs + 1], in0=l1col, scalar1=ccv("invhs"), scalar2=sm("zero"), op0=ALU.mult, op1=ALU.add),
                         reads=[pb[7], bCC, bzero], writes=[bhsc[hs]])
                    S.op("dve", lambda e: e.reciprocal(out=hsc[:, 4 * hs + 1:4 * hs + 2], in_=hsc[:, 4 * hs:4 * hs + 1]), reads=[bhsc[hs]], writes=[bhsc[hs]])
                    S.op("dve", lambda e: e.tensor_tensor(out=hsc[:, 4 * hs + 2:4 * hs + 3], in0=hsc[:, 4 * hs:4 * hs + 1], in1=pp("hbias", o * 8 + blk, 1), op=ALU.mult),
                         reads=[bhsc[hs], bPP], writes=[bhsc[hs]])
                    for jp in range(4):
                        hk, hb = pbank()

                        def f(e):
                            for u in range(2):
                                jj = 2 * jp + u
                                for tch in range(8):
                                    e.matmul(ps[:, hk, u * 256:u * 256 + 128], W0[:, tch, jj * 128:(jj + 1) * 128], KS3[:, tch, :], start=(tch == 0), stop=(tch == 7))
                                for tch in range(8):
                                    e.matmul(ps[:, hk, u * 256 + 128:u * 256 + 256], W0[:, tch, 1024 + jj * 128:1024 + (jj + 1) * 128], KS3[:, 8 + tch, :], start=(tch == 0), stop=(tch == 7))
                        S.op("pe", f, reads=[bW0, bKS], writes=[hb])
                        pv = ps[:, hk, :].rearrange("p (u r c) -> p u r c", u=2, r=2)
                        S.op("act", lambda e: e.copy(out=HA3[hs][:, 2 * jp:2 * jp + 2, :], in_=ps[:, hk, :].rearrange("p (u c) -> p u c", u=2)), reads=[hb], writes=[bHs[hs]])
                        S.op("act", lambda e: e.copy(out=HB3[hs][:, 2 * jp:2 * jp + 2, 0:128], in_=pv[:, :, 1, :]), reads=[hb], writes=[bHs[hs]])
                        S.op("act", lambda e: e.copy(out=HB3[hs][:, 2 * jp:2 * jp + 2, 128:256], in_=pv[:, :, 0, :]), reads=[hb], writes=[bHs[hs]])
                        yield ("fjp", 4 * (2 * blk + o) + jp)
                    if blk <= 1:
                        dbg("HA_%d_%d_%d" % (l, blk, o), HA[hs], [bHs[hs]], [128, 2048], BF16)
                        dbg("KS_%d_%d_%d" % (l, blk, o), KS, [bKS], [128, 2048], BF16)

                def g_pre(blk):
                    c0 = blk * 128
                    w, bw = wload([(w_in_d[l][:, c0:c0 + 128], 0), (w_in_d[l][:, 1024 + c0:1024 + c0 + 128], 128),
                                   (w_in_d[l][:, 2048 + c0:2048 + c0 + 128], 256), (w_in_d[l][:, 3072 + c0:3072 + c0 + 128], 384)])
                    for si, (dstT, bd) in enumerate(((vT, bv), (x1T, bx1), (x2T, bx2))):
                        sec = (2, 0, 1)[si]
                        pk, pbs = ppair()
                        proj_fm(pk, pbs, w, bw, sec * 128)
                        ch = sec * 8 + blk
                        ac_, bac_ = (acc, bacc) if si % 2 == 0 else (acc2, bacc2)
                        conv3(pk, pbs, ac_, bac_, pp("hcw", 0 * 24 + ch, 1), pp("hcw", 1 * 24 + ch, 1), pp("hcw", 2 * 24 + ch, 1), pp("hcb", ch, 1),
                              fixw[:, ch:ch + 1], fixw[:, 24 + ch:25 + ch])
                        S.op("act", lambda e: e.copy(out=dstT, in_=ac_), reads=[bac_], writes=[bd])
                        yield
                        if si == 0:
                            transpose_fm_to_tok(vT, bv, lambda tch: vtok3[:, tch, :], bvt)
                            yield
                    pk, pbs = ppair()
                    proj_fm(pk, pbs, w, bw, 384)
                    S.op("act", lambda e: e.activation(out=zT.rearrange("p (a b) -> p a b", a=2), in_=ps[:, pk:pk + 2, :], func=AF.Silu), reads=pbs, writes=[bz])
                    if blk == 0:
                        dbg("x1T%d" % l, x1T, [bx1], [128, T], BF16)
                        dbg("vT%d" % l, vT, [bv], [128, T], BF16)
                    yield

                def g_data(blk, o, hs):
                    zcurT, bzc = (vT, bv) if o == 0 else (z1T, bz1)
                    for jp in range(4):
                        yield ("dneed", 4 * (2 * blk + o) + jp)
                        hk, hb = pbank()
                        tA, btA_ = tmpA[jp % 2], btA[jp % 2]
                        tB, btB_ = tmpB[jp % 2], btB[jp % 2]

                        def f(e):
                            for u in range(2):
                                jj = 2 * jp + u
                                for tch in range(8):
                                    e.matmul(ps[:, hk, u * 256:u * 256 + 128], WS[:, tch, jj * 128:(jj + 1) * 128], vtok3[:, tch, :], start=(tch == 0), stop=(tch == 7))
                                for tch in range(8):
                                    e.matmul(ps[:, hk, u * 256 + 128:u * 256 + 256], WS[:, tch, 1024 + jj * 128:1024 + (jj + 1) * 128], vtok3[:, tch, :], start=(tch == 0), stop=(tch == 7))
                        S.op("pe", f, reads=[bWS, bvt], writes=[hb])
                        S.op("dve", lambda e: e.tensor_tensor(out=tA, in0=ps[:, hk, :], in1=HA3[hs][:, 2 * jp:2 * jp + 2, :].rearrange("p u c -> p (u c)"), op=ALU.mult),
                             reads=[hb, bHs[hs]], writes=[btA_])
                        S.op("dve", lambda e: e.tensor_tensor(out=tB, in0=ps[:, hk, :], in1=HB3[hs][:, 2 * jp:2 * jp + 2, :].rearrange("p u c -> p (u c)"), op=ALU.mult),
                             reads=[hb, bHs[hs]], writes=[btB_])
                        tAv = tA.rearrange("p (u r c) -> p u r c", u=2, r=2)
                        tBv = tB.rearrange("p (u r c) -> p u r c", u=2, r=2)
                        S.op("pool", lambda e: e.tensor_tensor(out=YY3[:, 2 * jp:2 * jp + 2, :], in0=tAv[:, :, 0, :], in1=tAv[:, :, 1, :], op=ALU.subtract), reads=[btA_], writes=[bYY])
                        S.op("pool", lambda e: e.tensor_tensor(out=YY3[:, 8 + 2 * jp:8 + 2 * jp + 2, :], in0=tBv[:, :, 0, :], in1=tBv[:, :, 1, :], op=ALU.add), reads=[btB_], writes=[bYY])
                        yield
                    if blk <= 1:
                        dbg("YY_%d_%d_%d" % (l, blk, o), YY, [bYY], [128, 2048], BF16)
                        dbg("vtok_%d_%d_%d" % (l, blk, o), vtok, [bvt], [128, 1024], BF16)
                    pk, pbs = ppair()

                    def f(e):
                        for half in range(2):
                            for j in range(16):
                                base = 0 if j < 8 else 1024
                                e.matmul(ps[:, pk + half, :], YY3[:, j, :], WS[:, j % 8, base + half * 512:base + (half + 1) * 512], start=(j == 0), stop=(j == 15))
                    S.op("pe", f, reads=[bWS, bYY], writes=pbs)
                    gT, bg = (x1T, bx1) if o == 0 else (x2T, bx2)
                    S.op("dve", lambda e: e.scalar_tensor_tensor(
                        out=acc.rearrange("p (a b) -> p a b", a=2), in0=zcurT.rearrange("p (a b) -> p a b", a=2), scalar=hsc[:, 4 * hs + 2:4 * hs + 3],
                        in1=ps[:, pk:pk + 2, :], op0=ALU.mult, op1=ALU.add), reads=pbs + [bzc, bhsc[hs]], writes=[bacc])
                    if o == 0:
                        S.op("dve", lambda e: e.scalar_tensor_tensor(out=z1T, in0=acc, scalar=hsc[:, 4 * hs + 1:4 * hs + 2], in1=gT, op0=ALU.mult, op1=ALU.mult),
                             reads=[bacc, bhsc[hs], bg], writes=[bz1])
                        yield
                        transpose_fm_to_tok(z1T, bz1, lambda tch: vtok3[:, tch, :], bvt)
                    else:
                        S.op("dve", lambda e: e.scalar_tensor_tensor(out=acc, in0=acc, scalar=hsc[:, 4 * hs + 1:4 * hs + 2], in1=gT, op0=ALU.mult, op1=ALU.mult),
                             reads=[bacc, bhsc[hs], bg], writes=[bacc])
                        S.op("pool", lambda e: e.tensor_tensor(out=yaT[:, blk, :], in0=acc, in1=zT, op=ALU.mult), reads=[bacc, bz], writes=[byaT[blk]])
                    yield

                stages = [(blk, o) for blk in range(8) for o in range(2)]

                def Dchain():
                    for si_, (blk, o) in enumerate(stages):
                        if o == 0:
                            yield from g_pre(blk)
                        yield from g_data(blk, o, si_ % 2)
                        yield ("done", si_)

                def Fchain():
                    for si_, (blk, o) in enumerate(stages):
                        if si_ == 0:
                            continue
                        yield from g_filter(blk, o, si_ % 2)

                for _ in g_filter(0, 0, 0):
                    pass
                if SEQ_DEBUG:
                    for si_, (blk, o) in enumerate(stages):
                        if si_ + 1 < len(stages):
                            for _ in g_filter(stages[si_ + 1][0], stages[si_ + 1][1], (si_ + 1) % 2):
                                pass
                        for _ in (g_pre(blk) if o == 0 else ()):
                            pass
                        for _ in g_data(blk, o, si_ % 2):
                            pass
                else:
                    Dg, Fg = Dchain(), Fchain()
                    d_done, f_next = -1, None
                    f_done, d_next = 3, None
                    ud = uf = 0.0
                    d_alive = f_alive = True
                    while d_alive or f_alive:
                        f_ok = f_alive and (f_next is None or d_done >= f_next - 2)
                        d_ok = d_alive and (d_next is None or f_done >= d_next)
                        if f_ok and (uf <= ud or not d_ok):
                            try:
                                r = next(Fg)
                                uf += 0.0 if (isinstance(r, tuple) and r[0] == "start") else 16.0
                                f_next = None
                                if isinstance(r, tuple):
                                    if r[0] == "start":
                                        f_next = r[1]
                                    else:
                                        f_done = r[1]
                            except StopIteration:
                                f_alive = False
                        elif d_ok:
                            try:
                                r = next(Dg)
                                ud += 0.0 if (isinstance(r, tuple) and r[0] == "dneed") else UD_UNITS
                                d_next = None
                                if isinstance(r, tuple):
                                    if r[0] == "dneed":
                                        d_next = r[1]
                                    else:
                                        d_done = r[1]
                            except StopIteration:
                                d_alive = False
                        else:
                            raise AssertionError("pipeline driver stuck")
                dbg("yaT%d" % l, yaT[:], byaT, [128, 8, T], BF16)

                phase('hy%d' % l)
                areset()

                def make_alloc(base, limit):
                    st = {"off": 0}

                    def f32(n):
                        o = st["off"]
                        st["off"] = o + n
                        assert st["off"] <= limit, ("arena overflow", st["off"])
                        return base[:, o:o + n]

                    def bf(n):
                        o = st["off"]
                        st["off"] = o + n // 2
                        assert st["off"] <= limit, ("arena overflow", st["off"])
                        return base[:, o:o + n // 2].bitcast(BF16)
                    return f32, bf

                def make_head(f32, bf, tag):
                    H = {"tag": tag}
                    H["qT"] = bf(2048); H["kT"] = bf(2048); H["ktok"] = bf(2048); H["Vaug"] = bf(8 * 258); H["og"] = bf(2048)
                    H["STw"] = [bf(128), bf(128)]; H["kw"] = [bf(256), bf(256)]
                    H["Cst"] = f32(2 * 2 * 257); H["Csb"] = [bf(2 * 258), bf(2 * 258)]
                    H["hsum"] = f32(2048); H["sgt"] = [bf(256), bf(256)]
                    H["den"] = f32(4); H["ss"] = f32(16)
                    for n in ("qT", "kT", "ktok", "og", "hsum", "hsum2", "Vaug", "den0", "den1", "ss"):
                        H["b_" + n] = Buf(tag + n)
                    for n in ("STw", "kw", "C", "Cs", "sgt"):
                        H["b_" + n] = [Buf(tag + n + "0"), Buf(tag + n + "1")]
                    return H

                HS = [make_head(af32, abf, "A"), make_head(*make_alloc(arenaB, 16384), "B")]

                def head_P(hh, H):
                    qT3 = H["qT"].rearrange("p (q t) -> p q t", q=2)
                    kT3 = H["kT"].rearrange("p (q t) -> p q t", q=2)
                    ktok3 = H["ktok"].rearrange("p (j c) -> p j c", c=256)
                    Va3 = H["Vaug"].rearrange("p (j c) -> p j c", c=258)
                    og3 = H["og"].rearrange("p (j c) -> p j c", c=256)
                    Cst4 = H["Cst"].rearrange("p (d q v) -> p d q v", d=2, q=2)
                    hsum = H["hsum"]
                    bq, bkT, bkt_, bog, bhs, bhs2, bVa, bC = H["b_qT"], H["b_kT"], H["b_ktok"], H["b_og"], H["b_hsum"], H["b_hsum2"], H["b_Vaug"], H["b_C"]
                    for d in range(2):
                        S.dma("sp", "cst" + H["tag"], lambda e, d=d: e.dma_start(out=Cst4[:, d, :, :], in_=c0_d[l, d, hh].rearrange("(q p) v -> p q v", p=128)), writes=[bC[d]])
                    S.op("pool", lambda e: e.memset(Va3[:, :, 256:258], 1.0), writes=[bVa])
                    c0 = hh * 256
                    w, bw = wload([(w_in_d[l][:, 4096 + c0:4096 + c0 + 256], 0), (w_in_d[l][:, 5120 + c0:5120 + c0 + 256], 256)])
                    for si, (dst3, bd) in enumerate(((qT3, bq), (kT3, bkT))):
                        for q2 in range(2):
                            pk, pbs = ppair()
                            proj_fm(pk, pbs, w, bw, si * 256 + q2 * 128)
                            ch = si * 8 + hh * 2 + q2
                            ac_, bac_ = (hsum[:, 0:1024], bhs) if q2 == 0 else (hsum[:, 1024:2048], bhs2)
                            conv3(pk, pbs, ac_, bac_, pp("mcw", 0 * 16 + ch, 1), pp("mcw", 1 * 16 + ch, 1), pp("mcw", 2 * 16 + ch, 1), pp("mcb", ch, 1),
                                  fixw[:, 48 + ch:49 + ch], fixw[:, 64 + ch:65 + ch])
                            S.op("act", lambda e: e.activation(out=dst3[:, q2, :], in_=ac_, func=AF.Silu), reads=[bac_], writes=[bd])
                            yield
                    if hh == 0:
                        dbg("qT%d" % l, H["qT"], [bq], [128, 2048], BF16)
                    for q2 in range(2):
                        transpose_fm_to_tok(kT3[:, q2, :], bkT, lambda tch, q2=q2: ktok3[:, tch, q2 * 128:(q2 + 1) * 128], bkt_, scale=1.0 / 16.0)
                        yield
                    wB, bwB = wload([(w_in_d[l][:, 6144 + c0:6144 + c0 + 256], 0), (w_in_d[l][:, 7168 + c0:7168 + c0 + 256], 256)])
                    for i in range(8):
                        bk, bb = pbank()

                        def f(e):
                            for dch in range(8):
                                e.matmul(ps[:, bk, :], hT[:, dch, i * 128:(i + 1) * 128], wB[:, dch, :], start=(dch == 0), stop=(dch == 7))
                        S.op("pe", f, reads=[bwB] + bhT, writes=[bb])
                        S.op("dve", lambda e: e.tensor_copy(out=Va3[:, i, 0:256], in_=ps[:, bk, 0:256]), reads=[bb], writes=[bVa])
                        S.op("act", lambda e: e.activation(out=og3[:, i, :], in_=ps[:, bk, 256:512], func=AF.Sigmoid), reads=[bb], writes=[bog])
                        yield
                    wC, bwC = wload([(w_in_d[l][:, 8192 + c0:8192 + c0 + 256], 0)])
                    for i in range(8):
                        bk, bb = pbank()

                        def f(e):
                            for dch in range(8):
                                e.matmul(ps[:, bk, 0:256], hT[:, dch, i * 128:(i + 1) * 128], wC[:, dch, 0:256], start=(dch == 0), stop=(dch == 7))
                        S.op("pe", f, reads=[bwC] + bhT, writes=[bb])
                        sg_, bsg_ = H["sgt"][i % 2], H["b_sgt"][i % 2]
                        S.op("act", lambda e: e.activation(out=sg_, in_=ps[:, bk, 0:256], func=AF.Silu), reads=[bb], writes=[bsg_])
                        S.op("pool", lambda e: e.tensor_tensor(out=og3[:, i, :], in0=og3[:, i, :], in1=sg_, op=ALU.mult), reads=[bsg_, bog], writes=[bog])
                        yield

                def head_R(hh, H):
                    qT3 = H["qT"].rearrange("p (q t) -> p q t", q=2)
                    kT3 = H["kT"].rearrange("p (q t) -> p q t", q=2)
                    ktok3 = H["ktok"].rearrange("p (j c) -> p j c", c=256)
                    Va3 = H["Vaug"].rearrange("p (j c) -> p j c", c=258)
                    og3 = H["og"].rearrange("p (j c) -> p j c", c=256)
                    Cst4 = H["Cst"].rearrange("p (d q v) -> p d q v", d=2, q=2)
                    Cs3 = [c.rearrange("p (q v) -> p q v", q=2) for c in H["Csb"]]
                    hsum3 = H["hsum"].rearrange("p (j c) -> p j c", c=256)
                    STw, kw, den, ss = H["STw"], H["kw"], H["den"], H["ss"]
                    bq, bkT, bkt_, bog, bhs, bhs2, bVa, bC = H["b_qT"], H["b_kT"], H["b_ktok"], H["b_og"], H["b_hsum"], H["b_hsum2"], H["b_Vaug"], H["b_C"]
                    bSTw, bkw, bCs, bss = H["b_STw"], H["b_kw"], H["b_Cs"], H["b_ss"]
                    bdens = [H["b_den0"], H["b_den1"]]
                    iters = [(k, d) for k in range(8) for d in range(2)]
                    prep = {}

                    def pre(it):
                        k, d = iters[it]
                        j = k if d == 0 else 7 - k
                        col = 4 * d + hh
                        wmk = wmkB[:, j, col:col + 1]
                        wa = waT[:, j, col:col + 1]
                        cs3, bcs = Cs3[d], bCs[d]
                        stw, bstw = STw[it % 2], bSTw[it % 2]
                        kwt, bkwt = kw[it % 2], bkw[it % 2]
                        msk = triu if d == 0 else tril
                        sk, sbk = pbank()

                        def f(e):
                            e.matmul(ps[:, sk, 0:128], kT3[:, 0, j * 128:(j + 1) * 128], qT3[:, 0, j * 128:(j + 1) * 128], start=True, stop=False)
                            e.matmul(ps[:, sk, 0:128], kT3[:, 1, j * 128:(j + 1) * 128], qT3[:, 1, j * 128:(j + 1) * 128], start=False, stop=True)
                        S.op("pe", f, reads=[bkT, bq], writes=[sbk])
                        S.op("dve", lambda e: e.scalar_tensor_tensor(out=stw, in0=ps[:, sk, 0:128], scalar=wa, in1=msk[:], op0=ALU.mult, op1=ALU.mult),
                             reads=[sbk, bwaT, bconst], writes=[bstw])
                        S.op("act", lambda e: e.activation(out=kwt, in_=ktok3[:, j, :], func=AF.Identity, scale=wa),
                             reads=[bkt_, bwaT], writes=[bkwt])
                        S.op("pool", lambda e: e.tensor_scalar(out=cs3[:, :, 0:257], in0=Cst4[:, d, :, :], scalar1=wmk, scalar2=sm("zero"), op0=ALU.mult, op1=ALU.add),
                             reads=[bC[d], bwmkB, bzero], writes=[bcs])
                        prep[it] = (k, d, j, col, wmk, cs3, bcs, stw, bstw, kwt, bkwt)

                    def body(it):
                        k, d, j, col, wmk, cs3, bcs, stw, bstw, kwt, bkwt = prep.pop(it)
                        eb = ebT[:, j, col:col + 1]
                        nk, nb = pbank()

                        def f(e):
                            e.matmul(ps[:, nk, 0:257], stw, Va3[:, j, 0:257], start=True, stop=False)
                            e.matmul(ps[:, nk, 0:257], qT3[:, 0, j * 128:(j + 1) * 128], cs3[:, 0, 0:257], start=False, stop=False)
                            e.matmul(ps[:, nk, 0:257], qT3[:, 1, j * 128:(j + 1) * 128], cs3[:, 1, 0:257], start=False, stop=True)
                        S.op("pe", f, reads=[bstw, bVa, bq, bcs], writes=[nb])
                        uks = []
                        for q2 in range(2):
                            uk, ub = pbank()
                            S.op("pe", lambda e: e.matmul(ps[:, uk, 0:257], kwt[:, q2 * 128:(q2 + 1) * 128], Va3[:, j, 0:257], start=True, stop=True),
                                 reads=[bkwt, bVa], writes=[ub])
                            uks.append((uk, ub))
                        for q2 in range(2):
                            uk, ub = uks[q2]
                            S.op("dve", lambda e: e.scalar_tensor_tensor(out=Cst4[:, d, q2, :], in0=Cst4[:, d, q2, :], scalar=wmk, in1=ps[:, uk, 0:257], op0=ALU.mult, op1=ALU.add),
                                 reads=[ub, bC[d], bwmkB], writes=[bC[d]])
                        dn, rdn, bdn = den[:, 2 * (it % 2):2 * (it % 2) + 1], den[:, 2 * (it % 2) + 1:2 * (it % 2) + 2], bdens[it % 2]
                        S.op("act", lambda e: e.activation(out=dn, in_=ps[:, nk, 256:257], func=AF.Abs), reads=[nb], writes=[bdn])
                        S.op("dve", lambda e: e.tensor_tensor(out=dn, in0=dn, in1=eb, op=ALU.max), reads=[bdn, bebT], writes=[bdn])
                        S.op("dve", lambda e: e.reciprocal(out=rdn, in_=dn), reads=[bdn], writes=[bdn])
                        first = (d == 0 and j <= 3) or (d == 1 and j >= 4)
                        if first:
                            S.op("dve", lambda e: e.tensor_scalar(out=hsum3[:, j, :], in0=ps[:, nk, 0:256], scalar1=rdn, scalar2=sm("zero"), op0=ALU.mult, op1=ALU.add),
                                 reads=[nb, bdn, bzero], writes=[bhs, bhs2])
                        else:
                            S.op("dve", lambda e: e.scalar_tensor_tensor(out=hsum3[:, j, :], in0=ps[:, nk, 0:256], scalar=rdn, in1=hsum3[:, j, :], op0=ALU.mult, op1=ALU.add),
                                 reads=[nb, bdn], writes=[bhs])
                        if k % 2 == 1:
                            slot = (k - 1) // 2 if d == 0 else 3 - (k - 1) // 2
                            S.dma("sp", "scout" + H["tag"], lambda e: e.dma_start(out=sc_d[l, slot, d, hh].rearrange("(q p) v -> p q v", p=128), in_=Cst4[:, d, :, :]),
                                  reads=[bC[d]])

                    pre(0)
                    for it in range(16):
                        if it + 1 < 16:
                            pre(it + 1)
                        body(it)
                        yield
                    junk, bjunk = H["sgt"][0], H["b_sgt"][0]
                    for j in range(8):
                        S.op("act", lambda e: e.activation(out=junk, in_=hsum3[:, j, :], func=AF.Square, accum_out=ss[:, j:j + 1]), reads=[bhs], writes=[bjunk, bss])
                    yield
                    S.op("act", lambda e: e.activation(out=ss[:, 8:16], in_=ss[:, 0:8], func=AF.Sqrt, scale=1.0 / 256.0, bias=EPS), reads=[bss], writes=[bss])
                    S.op("dve", lambda e: e.reciprocal(out=ss[:, 8:16], in_=ss[:, 8:16]), reads=[bss], writes=[bss])
                    for j in range(8):
                        S.op("dve", lambda e: e.scalar_tensor_tensor(out=og3[:, j, :], in0=hsum3[:, j, :], scalar=ss[:, 8 + j:9 + j], in1=og3[:, j, :], op0=ALU.mult, op1=ALU.mult),
                             reads=[bhs, bss, bog], writes=[bog])
                    yield
                    for q2 in range(2):
                        ch = hh * 2 + q2
                        for g in range(2):
                            bk, bb = pbank()
                            psb = ps[:, bk, :].bitcast(BF16)

                            def f(e):
                                for q in range(4):
                                    i = 4 * g + q
                                    e.transpose(out=psb[:, q * 128:(q + 1) * 128], in_=og3[:, i, q2 * 128:(q2 + 1) * 128], identity=identb[:])
                            S.op("pe", f, reads=[bog, bconst], writes=[bb])
                            S.op("act", lambda e: e.activation(out=ybT[:, ch, g * 512:(g + 1) * 512], in_=psb[:, 0:512], func=AF.Identity, scale=pp("mng", ch, 1)),
                                 reads=[bb, bPP], writes=[bybT[ch]])
                            yield

                def interleave2(ga, gb):
                    alive = [ga, gb]
                    while alive:
                        for g in list(alive):
                            try:
                                next(g)
                            except StopIteration:
                                alive.remove(g)

                for _ in head_P(0, HS[0]):
                    pass
                for hh in range(4):
                    if hh + 1 < 4:
                        interleave2(head_R(hh, HS[hh % 2]), head_P(hh + 1, HS[(hh + 1) % 2]))
                    else:
                        for _ in head_R(hh, HS[hh % 2]):
                            pass
                phase('ml%dend' % l)
                dbg("ybT%d" % l, ybT[:], bybT, [128, 8, T], BF16)

                phase('ml%d' % l)
                areset()
                if l == 0:
                    load_tables()
                mT = abf(8 * 1024)
                mT3 = mT.rearrange("p (c t) -> p c t", c=8)
                bm = [Buf("mT%d" % i) for i in range(8)]
                sg = abf(1024)
                t1 = abf(1024)
                bsg_, bt1 = Buf("sg"), Buf("t1")
                for r4 in range(2):
                    wga, bwga = wload([(w_in_d[l][:, 9232 + r4 * 512:9232 + (r4 + 1) * 512], 0)], prefetch=False)
                    wpa, bwpa = wload([(w_pa_d[l][:, r4 * 512:(r4 + 1) * 512], 0)], prefetch=False)
                    pend = []
                    for rq in range(4):
                        r = r4 * 4 + rq
                        pkg, pbg = ppair()
                        pkp, pbp = ppair()
                        proj_fm(pkg, pbg, wga, bwga, rq * 128)
                        proj_fm(pkp, pbp, wpa, bwpa, rq * 128, rhs=yaT, brhs=byaT)
                        S.op("act", lambda e, pkg=pkg: e.activation(out=sg.rearrange("p (a b) -> p a b", a=2), in_=ps[:, pkg:pkg + 2, :], func=AF.Sigmoid), reads=pbg, writes=[bsg_])
                        S.op("dve", lambda e, pkp=pkp, r=r: e.tensor_tensor(out=mT3[:, r, :].rearrange("p (a b) -> p a b", a=2), in0=ps[:, pkp:pkp + 2, :], in1=sg.rearrange("p (a b) -> p a b", a=2), op=ALU.mult),
                             reads=pbp + [bsg_], writes=[bm[r]])
                    wgb, bwgb = wload([(w_in_d[l][:, 10256 + r4 * 512:10256 + (r4 + 1) * 512], 0)], prefetch=False)
                    wpb, bwpb = wload([(w_pb_d[l][:, r4 * 512:(r4 + 1) * 512], 0)], prefetch=False)
                    for rq in range(4):
                        r = r4 * 4 + rq
                        pkg, pbg = ppair()
                        pkp, pbp = ppair()
                        proj_fm(pkg, pbg, wgb, bwgb, rq * 128)
                        proj_fm(pkp, pbp, wpb, bwpb, rq * 128, rhs=ybT, brhs=bybT)
                        S.op("act", lambda e, pkg=pkg: e.activation(out=sg.rearrange("p (a b) -> p a b", a=2), in_=ps[:, pkg:pkg + 2, :], func=AF.Sigmoid), reads=pbg, writes=[bsg_])
                        S.op("dve", lambda e, pkp=pkp: e.tensor_tensor(out=t1.rearrange("p (a b) -> p a b", a=2), in0=ps[:, pkp:pkp + 2, :], in1=sg.rearrange("p (a b) -> p a b", a=2), op=ALU.mult),
                             reads=pbp + [bsg_], writes=[bt1])
                        S.op("pool", lambda e, r=r: e.tensor_tensor(out=mT3[:, r, :], in0=mT3[:, r, :], in1=t1, op=ALU.add), reads=[bt1, bm[r]], writes=[bm[r]])
                for r4 in range(2):
                    wo, bwo = wload([(w_out_d[l][:, r4 * 512:(r4 + 1) * 512], 0)])
                    for rq in range(4):
                        r = r4 * 4 + rq
                        pk, pbs = ppair()
                        proj_fm(pk, pbs, wo, bwo, rq * 128, rhs=mT3, brhs=bm)
                        S.op("dve", lambda e, pk=pk, r=r: e.scalar_tensor_tensor(out=X[:, r, :].rearrange("p (a b) -> p a b", a=2), in0=ps[:, pk:pk + 2, :], scalar=sm("modsb", 16 + r, 1),
                                                                             in1=X[:, r, :].rearrange("p (a b) -> p a b", a=2), op0=ALU.mult, op1=ALU.add),
                             reads=pbs + [bsmall, bX[r]], writes=[bX[r]])
                dbg("X%d" % l, X[:], bX, [128, 8, T])

            phase('merge')
            areset()
            rstd, br = rms_rstd()
            yf = [af32(1024), af32(1024)]
            byf = [Buf("yf0"), Buf("yf1")]
            yst = [af32(512) for _ in range(4)]
            byst = [Buf("yst%d" % i) for i in range(4)]
            y_v = y_d.rearrange("(i p) d -> p i d", p=128)
            n = 0
            for dch in range(8):
                yt, by = yf[dch % 2], byf[dch % 2]
                S.op("dve", lambda e: e.scalar_tensor_tensor(out=yt, in0=X[:, dch, :], scalar=pp("fing", dch, 1), in1=rstd, op0=ALU.mult, op1=ALU.mult),
                     reads=[bX[dch], bPP, br], writes=[by])
                for g in range(2):
                    bk, bb = pbank()
                    ys, bys = yst[n % 4], byst[n % 4]
                    n += 1

                    def f(e):
                        for q in range(4):
                            i = 4 * g + q
                            e.transpose(out=ps[:, bk, q * 128:(q + 1) * 128], in_=yt[:, i * 128:(i + 1) * 128], identity=identf[:])
                    S.op("pe", f, reads=[by, bconst], writes=[bb])
                    if g == 0:
                        S.op("act", lambda e: e.copy(out=ys, in_=ps[:, bk, :]), reads=[bb], writes=[bys])
                    else:
                        S.op("dve", lambda e: e.tensor_copy(out=ys, in_=ps[:, bk, :]), reads=[bb], writes=[bys])
                    S.dma("sp", "yout%d" % (n % 4), lambda e: e.dma_start(out=y_v[:, 4 * g:4 * g + 4, dch * 128:(dch + 1) * 128], in_=ys.rearrange("p (q d) -> p q d", q=4)), reads=[bys])

        except _Stop:
            pass
        S.final_wait("sp")
        S.emit()
        stats = dict(ops=S.nop, waits=S.nwait)
    return nc, stats, list(dbg_outs.keys())


def _tables(L):
    NS = T // L
    t = np.arange(T)
    tl = (t % L).astype(np.float64)
    f = np.arange(L)
    Wsym = np.zeros((T, 2 * T), np.float32)
    W0 = np.zeros((T, 2 * T), np.float32)
    tt = np.arange(L)
    th = np.pi * np.outer(tt + 0.5, f + 0.5) / L
    th0 = np.pi * np.outer(tt, f + 0.5) / L
    for s in range(NS):
        r = slice(s * L, (s + 1) * L)
        Wsym[r, s * L:(s + 1) * L] = np.cos(th)
        Wsym[r, T + s * L:T + (s + 1) * L] = -np.sin(th)
        W0[r, s * L:(s + 1) * L] = np.cos(th0)
        W0[r, T + s * L:T + (s + 1) * L] = -np.sin(th0)
    bands = np.arange(1, 17)
    tn = tl / L
    ang = 2 * np.pi * tn[:, None] * bands[None, :]
    feats = np.concatenate([tn[:, None], np.cos(ang), np.sin(ang)], axis=-1).astype(np.float32)
    maskb = (tl != 0).astype(np.float32)
    cpb = L // 128
    keep = np.array([0.0 if (k % cpb == 0 and k > 0) else 1.0 for k in range(8)], np.float32)
    cc = np.zeros((128, NCC), np.float32)

    def put(name, arr):
        o, w = CCO[name]
        cc[:, o:o + w] = arr
    put("tnneg", -(tn.reshape(8, 128).T))
    put("maskb", maskb.reshape(8, 128).T)
    put("nmaskb", -(maskb.reshape(8, 128).T))
    put("bfixneg", -1.0 if L == 256 else 0.0)
    put("invhs", float(L) / NS)
    put("keep", keep[None, :])
    keepc = np.zeros((128, 8), np.float32)
    keepc[:, :] = keep[None, :]
    keepc[32:64, :] = keep[::-1][None, :]
    put("keepc", keepc)
    import ml_dtypes
    return dict(wsym=Wsym.astype(ml_dtypes.bfloat16), w0t=W0.astype(ml_dtypes.bfloat16), featsT=np.ascontiguousarray(feats.T), cc=cc)


def _chunks(v, n):
    return np.ascontiguousarray(np.asarray(v, np.float32).reshape(n, 128).T)


_PROG = {}


def kernel(x_prompt, x_sample, state_C, state_n, state_m, c, c_ctx, norm_g, w_ada, b_ada,
           w_in, hy_conv_w, hy_conv_b, hy_w1, hy_b1, hy_w2, hy_b2, hy_w3, hy_b3, hy_freq,
           hy_decay, hy_bias, ml_conv_w, ml_conv_b, ml_if_b, ml_norm_g, w_pa, w_pb, w_out, final_g):
    f32 = np.float32
    A = lambda a: np.ascontiguousarray(np.asarray(a, f32))
    x_prompt, x_sample = A(x_prompt), A(x_sample)
    pp = np.zeros((2, 128, NPP), f32)

    def put(l, name, arr, rows=slice(0, 128)):
        o, w = PPO[name]
        pp[l, rows, o:o + w] = arr
    wgi = np.zeros((2, D, 128), f32)
    wgf = np.zeros((2, D, 128), f32)
    for l in range(2):
        put(l, "b_ada", _chunks(b_ada[l], 24))
        put(l, "norm_g", _chunks(norm_g[l], 8))
        put(l, "hcw", np.concatenate([_chunks(hy_conv_w[l][j], 24) for j in range(3)], 1))
        put(l, "hcb", _chunks(hy_conv_b[l], 24))
        put(l, "hbias", np.concatenate([_chunks(hy_bias[l][o], 8) for o in range(2)], 1))
        put(l, "mcw", np.concatenate([_chunks(ml_conv_w[l][j], 16) for j in range(3)], 1))
        put(l, "mcb", _chunks(ml_conv_b[l], 16))
        put(l, "mng", _chunks(ml_norm_g[l], 8))
        put(l, "hb1", np.asarray(hy_b1[l], f32)[:, None], slice(0, 64))
        put(l, "hb2", np.asarray(hy_b2[l], f32)[:, None], slice(0, 64))
        put(l, "hfr", np.asarray(hy_freq[l], f32).T, slice(0, 64))
        put(l, "fing", _chunks(final_g, 8))
        g0 = 9216
        for d in range(2):
            for h in range(4):
                wgi[l, :, 32 * d + h] = w_in[l][:, g0 + d * 8 + 0 * 4 + h]
                wgf[l, :, 32 * d + h] = w_in[l][:, g0 + d * 8 + 1 * 4 + h]
                pp[l, 32 * d + h, PPO["gbi"][0]] = ml_if_b[l][d, 0, h]
                pp[l, 32 * d + h, PPO["gbf"][0]] = ml_if_b[l][d, 1, h]
    shared = dict(
        pp=pp, w_in=A(w_in), w_ada=A(w_ada), w_pa=A(w_pa), w_pb=A(w_pb), w_out=A(w_out),
        hy_w3=A(hy_w3), hy_b3=A(hy_b3).reshape(2, 1, 4096), hy_dec=A(hy_decay).reshape(2, 1, 4096),
        hy_w1=A(hy_w1), hy_w2=A(hy_w2), wgi=wgi, wgf=wgf,
        identf=np.eye(128, dtype=f32), triu=(np.triu(np.ones((128, 128), f32)) / 16.0).astype(f32),
        tril=(np.tril(np.ones((128, 128), f32)) / 16.0).astype(f32),
    )
    resetm = np.ones((64, T), f32)
    resetm[:, ::128] = 0.0
    shared["resetm"] = resetm
    tabs = {256: _tables(256), 1024: _tables(1024)}
    in_maps = []
    for core in range(8):
        m = dict(shared)
        if core < 4:
            L = 1024
            m["xin"] = x_sample[core]
            cv = np.asarray(c[core], f32)
            c0 = np.concatenate([np.asarray(state_C[core], f32), np.asarray(state_n[core], f32)[..., None]], axis=-1)
            m0 = np.zeros((2, 64, 1), f32)
            for l in range(2):
                for d in range(2):
                    m0[l, 32 * d:32 * d + 4, 0] = state_m[core][l, d]
        else:
            L = 256
            b0 = 4 * (core - 4)
            m["xin"] = x_prompt[b0:b0 + 4].reshape(T, D)
            cv = np.asarray(c_ctx, f32)
            c0 = np.zeros((2, 2, 4, 256, 257), f32)
            m0 = np.zeros((2, 64, 1), f32)
        tb = tabs[L]
        cc = tb["cc"].copy()
        o, w = CCO["cvec"]
        cc[:, o:o + w] = _chunks(cv, 8)
        m.update(wsym=tb["wsym"], w0t=tb["w0t"], featsT=tb["featsT"], cc=cc, c0aug=np.ascontiguousarray(c0), m0=m0)
        in_maps.append(m)

    if "nc" not in _PROG:
        _PROG["nc"] = build_program()
    nc, stats, dbgn = _PROG["nc"]
    res = run_bass_kernel_spmd(nc, in_maps, core_ids=list(range(8)))
    R = res.results
    _PROG["last"] = R
    y_sample = np.stack([np.asarray(R[i]["y"], f32) for i in range(4)], 0)
    y_prompt = np.concatenate([np.asarray(R[i]["y"], f32).reshape(4, 256, D) for i in range(4, 8)], 0)
    new_C = np.zeros((16, 2, 2, 4, 256, 256), f32)
    new_n = np.zeros((16, 2, 2, 4, 256), f32)
    new_m = np.zeros((16, 2, 2, 4), f32)
    for i in range(4, 8):
        sc = np.asarray(R[i]["sc"], f32)
        smo = np.asarray(R[i]["sm"], f32)
        for slot in range(4):
            b = 4 * (i - 4) + slot
            new_C[b] = sc[:, slot, :, :, :, :256]
            new_n[b] = sc[:, slot, :, :, :, 256]
            for d in range(2):
                k = 2 * slot + 1 if d == 0 else 2 * (3 - slot) + 1
                new_m[b, :, d, :] = smo[:, 32 * d:32 * d + 4, k]
    return (y_prompt, y_sample, new_C, new_n, new_m)
```

```python
import math
from contextlib import ExitStack

import numpy as np
import concourse.bass as bass
import concourse.mybir as mybir
from concourse.alu_op_type import AluOpType as ALU
from concourse.bass_utils import run_bass_kernel_spmd

F32 = mybir.dt.float32
BF16 = mybir.dt.bfloat16
AF = mybir.ActivationFunctionType
AX = mybir.AxisListType

T = 1024
D = 1024
N_IN = 11280
MAGIC = 12582912.0
PI_LO = 3.1415925
EPS = 1e-6
DEBUG = False
SEQ_DEBUG = False
UD_UNITS = 26.0


class Buf:
    __slots__ = ("name", "w", "r", "excl")

    def __init__(self, name, excl=False):
        self.name = name
        self.w = None
        self.r = []
        self.excl = excl


class _Rec:
    def __init__(self):
        self.calls = []

    def __getattr__(self, name):
        def m(*a, **k):
            self.calls.append((name, a, k))
            return None
        return m


def _record(fn):
    r = _Rec()
    fn(r)
    assert r.calls
    return r.calls


class Sched:
    ENGS = ("pe", "act", "dve", "pool", "sp")

    def __init__(self, nc, es):
        self.nc = nc
        self.es = es
        self.streams = {e: [] for e in self.ENGS}
        self.esem = {}
        for e in ("pe", "act", "dve", "pool"):
            self.esem[e] = es.enter_context(nc.semaphore("s_" + e))
        self.ecount = {e: 0 for e in ("pe", "act", "dve", "pool")}
        self.dsem = {}
        self.dcount = {}
        self.waited = {e: {} for e in self.ENGS}
        self.nwait = 0
        self.nop = 0

    def _dma_sem(self, key):
        if key not in self.dsem:
            self.dsem[key] = self.es.enter_context(self.nc.semaphore("d_" + key))
            self.dcount[key] = 0
        return self.dsem[key]

    def _resolve(self, ev):
        kind, key, val = ev
        if kind == "eng":
            return ("e:" + key, self.esem[key], val)
        return ("d:" + key, self.dsem[key], self.dcount[key])

    def _emit_waits(self, eng, evs):
        need = {}
        for ev in evs:
            if ev is None:
                continue
            if ev[0] == "eng" and ev[1] == eng and eng == "pe":
                continue
            name, sem, val = self._resolve(ev)
            if val <= self.waited[eng].get(name, 0):
                continue
            if name not in need or need[name][1] < val:
                need[name] = (sem, val)
        for name, (sem, val) in need.items():
            self.waited[eng][name] = val
            self.streams[eng].append(("wait", sem, val))
            self.nwait += 1

    @staticmethod
    def _deps(reads, writes, eng=None):
        evs = []
        for b in reads:
            evs.append(b.w)
            if b.excl:
                evs.extend(r for r in b.r if not (r[0] == "eng" and r[1] == eng))
        for b in writes:
            evs.append(b.w)
            evs.extend(b.r)
        return evs

    def _commit(self, ev, reads, writes):
        for b in reads:
            b.r = [r for r in b.r if not (r[0] == ev[0] and r[1] == ev[1])]
            b.r.append(ev)
        for b in writes:
            b.w = ev
            b.r = []
        self.nop += 1

    def op(self, eng, fn, reads=(), writes=()):
        self._emit_waits(eng, self._deps(reads, writes, eng))
        self.ecount[eng] += 1
        ev = ("eng", eng, self.ecount[eng])
        self.streams[eng].append(("op", _record(fn), self.esem[eng], 1))
        self._commit(ev, reads, writes)
        return ev

    def dma(self, queue, key, fn, reads=(), writes=()):
        key = queue + "_" + key
        sem = self._dma_sem(key)
        deps = [ev for ev in self._deps(reads, writes) if not (ev is not None and ev[0] == "dma" and ev[1] == key)]
        self._emit_waits(queue, deps)
        self.dcount[key] += 16
        ev = ("dma", key, None)
        self.streams[queue].append(("op", _record(fn), sem, 16))
        self._commit(ev, reads, writes)
        return ev

    def barrier(self):
        for eng in ("pe", "act", "dve", "pool"):
            evs = [("eng", e, self.ecount[e]) for e in ("pe", "act", "dve", "pool") if self.ecount[e] > 0]
            evs += [("dma", k, None) for k in self.dsem]
            self._emit_waits(eng, evs)
        evs = [("eng", e, self.ecount[e]) for e in ("pe", "act", "dve", "pool") if self.ecount[e] > 0]
        evs += [("dma", k, None) for k in self.dsem]
        self._emit_waits("sp", evs)

    def final_wait(self, eng="sp"):
        for key, sem in self.dsem.items():
            name = "d:" + key
            val = self.dcount[key]
            if val > self.waited[eng].get(name, 0):
                self.waited[eng][name] = val
                self.streams[eng].append(("wait", sem, val))

    def emit(self):
        nc = self.nc
        handles = {"pe": "tensor", "act": "scalar", "dve": "vector", "pool": "gpsimd", "sp": "sync"}
        with nc.Block() as block:
            for e in self.ENGS:
                items = self.streams[e]

                def body(engine, items=items):
                    for it in items:
                        if it[0] == "wait":
                            engine.wait_ge(it[1], it[2])
                        else:
                            inst = None
                            for (name, a, k) in it[1]:
                                inst = getattr(engine, name)(*a, **k)
                            inst.then_inc(it[2], it[3])

                getattr(block, handles[e])(body)


PPO = {}
_o = 0
for _n, _w in [("b_ada", 24), ("norm_g", 8), ("hcw", 72), ("hcb", 24), ("hbias", 16), ("mcw", 48), ("mcb", 16),
               ("mng", 8), ("hb1", 1), ("hb2", 1), ("hfr", 2), ("gbi", 1), ("gbf", 1), ("fing", 8)]:
    PPO[_n] = (_o, _w)
    _o += _w
NPP = _o
CCO = {}
_o = 0
for _n, _w in [("tnneg", 8), ("maskb", 8), ("nmaskb", 8), ("bfixneg", 1), ("invhs", 1), ("keep", 8), ("keepc", 8), ("cvec", 8)]:
    CCO[_n] = (_o, _w)
    _o += _w
NCC = _o

ARENA_F32 = 9888


class _Stop(Exception):
    pass


STOP_AT = None


def build_program():
    nc = bass.Bass("TRN2", target_bir_lowering=False)
    es = ExitStack()
    with es:
        S = Sched(nc, es)
        phases = []

        def phase(name):
            phases.append(name)
            if STOP_AT is not None and name == STOP_AT:
                raise _Stop()

        def din(name, shape, dt=F32):
            return nc.dram_tensor(name, shape, dt, kind="ExternalInput").ap()

        def dout(name, shape, dt=F32):
            return nc.dram_tensor(name, shape, dt, kind="ExternalOutput").ap()

        def sb(name, shape, dt=F32):
            return es.enter_context(nc.sbuf_tensor(name, shape, dt))

        xin = din("xin", [T, D])
        wsym_d = din("wsym", [T, 2 * T], BF16)
        w0_d = din("w0t", [T, 2 * T], BF16)
        feats_d = din("featsT", [33, T])
        cc_d = din("cc", [128, NCC])
        resetm_d = din("resetm", [64, T])
        identf_d = din("identf", [128, 128])
        triu_d = din("triu", [128, 128])
        tril_d = din("tril", [128, 128])
        c0_d = din("c0aug", [2, 2, 4, 256, 257])
        m0_d = din("m0", [2, 64, 1])
        pp_d = din("pp", [2, 128, NPP])
        w_in_d = din("w_in", [2, D, N_IN])
        w_ada_d = din("w_ada", [2, D, 3 * D])
        w_pa_d = din("w_pa", [2, D, D])
        w_pb_d = din("w_pb", [2, D, D])
        w_out_d = din("w_out", [2, D, D])
        hy_w3_d = din("hy_w3", [2, 64, 4096])
        hy_b3_d = din("hy_b3", [2, 1, 4096])
        hy_dec_d = din("hy_dec", [2, 1, 4096])
        hy_w1_d = din("hy_w1", [2, 33, 64])
        hy_w2_d = din("hy_w2", [2, 64, 64])
        wgi_d = din("wgi", [2, D, 128])
        wgf_d = din("wgf", [2, D, 128])
        y_d = dout("y", [T, D])
        sc_d = dout("sc", [2, 4, 2, 4, 256, 257])
        sm_d = dout("sm", [2, 64, 8])
        dbg_outs = {}

        X = sb("X", [128, 8, T])
        hT = sb("hT", [128, 8, T], BF16)
        tabs = sb("tabs", [128, 2 * 8 * 2 * T], BF16)
        WS = tabs[:, 0:8 * 2 * T].rearrange("p (c f) -> p c f", c=8)
        W0 = tabs[:, 8 * 2 * T:2 * 8 * 2 * T].rearrange("p (c f) -> p c f", c=8)
        arenaB = tabs[:].bitcast(F32)
        yaT = sb("yaT", [128, 8, T], BF16)
        wbufs = [sb("wbuf%d" % i, [128, 8, 512], BF16) for i in range(2)]
        PP = sb("PP", [128, NPP])
        CC = sb("CC", [128, NCC])
        identf = sb("identf_s", [128, 128])
        identb = sb("identb", [128, 128], BF16)
        triu = sb("triu_s", [128, 128])
        tril = sb("tril_s", [128, 128])
        ones_bf = sb("ones_bf", [128, 128], BF16)
        ones_f = sb("ones_f", [128, 128])
        resetm = sb("resetm_s", [64, T], BF16)
        hdn2 = sb("hdn2", [65, T], BF16)
        waT = sb("waT", [128, 8, 8])
        ebT = sb("ebT", [128, 8, 8])
        wmkB = sb("wmkB", [128, 8, 8])
        small = sb("small", [128, 128])
        fixw = sb("fixw", [128, 80])
        arena = sb("arena", [128, ARENA_F32 + 4096])
        ybT = arena[:, ARENA_F32:ARENA_F32 + 4096].bitcast(BF16).rearrange("p (c t) -> p c t", c=8)
        ps = es.enter_context(nc.psum_tensor("ps", [128, 8, 512], F32))

        bX = [Buf("X%d" % i) for i in range(8)]
        bhT = [Buf("hT%d" % i) for i in range(8)]
        bWS, bW0 = Buf("WS"), Buf("W0")
        byaT = [Buf("yaT%d" % i) for i in range(8)]
        bybT = [Buf("ybT%d" % i) for i in range(8)]
        bwbuf = [Buf("wbuf0"), Buf("wbuf1")]
        bPP, bCC, bconst = Buf("PP"), Buf("CC"), Buf("const")
        bhdn2 = Buf("hdn2")
        bwaT, bebT, bwmkB = Buf("waT"), Buf("ebT"), Buf("wmkB")
        bsmall = Buf("small")
        bden, bss, bfix, bzero = Buf("den"), Buf("ss"), Buf("fix"), Buf("zero")
        bden2 = Buf("den2")
        pb = [Buf("psb%d" % i, excl=True) for i in range(8)]

        def pp(name, a=0, n=None, rows=slice(0, 128)):
            o, w = PPO[name]
            n = w - a if n is None else n
            return PP[rows, o + a:o + a + n]

        def ccv(name, a=0, n=None, rows=slice(0, 128)):
            o, w = CCO[name]
            n = w - a if n is None else n
            return CC[rows, o + a:o + a + n]

        SM = {"modsb": (0, 24), "gs": (24, 8), "sc": (32, 8), "fb": (40, 2), "ngbf": (42, 1), "m0": (43, 1), "meff": (44, 1),
              "amax": (48, 8), "negtot": (56, 8), "Rc": (64, 8), "dmc": (72, 8), "mst": (80, 8), "wmk": (88, 8),
              "ss": (96, 8), "rs": (104, 8), "den": (112, 2), "w0fix": (114, 1), "w2fix": (115, 1), "zero": (116, 1), "den2": (118, 2)}

        def sm(name, a=0, n=None, rows=slice(0, 128)):
            o, w = SM[name]
            n = w - a if n is None else n
            return small[rows, o + a:o + a + n]

        scb = sb("scb", [128, 8], BF16)

        astate = {"off": 0, "lim": ARENA_F32}

        def areset(ext=False):
            S.barrier()
            astate["off"] = 0
            astate["lim"] = ARENA_F32 + (4096 if ext else 0)

        def af32(n):
            o = astate["off"]
            astate["off"] = o + n
            assert astate["off"] <= astate["lim"], ("arena overflow", astate["off"])
            return arena[:, o:o + n]

        def abf(n):
            assert n % 2 == 0
            o = astate["off"]
            astate["off"] = o + n // 2
            assert astate["off"] <= astate["lim"], ("arena overflow", astate["off"])
            return arena[:, o:o + n // 2].bitcast(BF16)

        pstate = {"one": 0, "pair": 0}

        def pbank():
            i = pstate["one"]
            pstate["one"] = (i + 1) % 7
            return i, pb[i]

        def ppair():
            k = pstate["pair"]
            pstate["pair"] = (k + 1) % 3
            return 2 * k, [pb[2 * k], pb[2 * k + 1]]

        def dbg(name, ap, bufs, shape, dt=F32):
            if not DEBUG:
                return
            o = dout("dbg_" + name, shape, dt)
            dbg_outs[name] = o
            S.dma("sp", "dbg", lambda e: e.dma_start(out=o, in_=ap), reads=bufs)

        wstate = {"idx": 0}
        plan = []
        for l_ in range(2):
            for cb in range(6):
                plan.append([(w_ada_d[l_][:, cb * 512:(cb + 1) * 512], 0)])
            plan.append([(wgi_d[l_], 0), (wgf_d[l_], 128)])
            for blk in range(8):
                c0 = blk * 128
                plan.append([(w_in_d[l_][:, c0:c0 + 128], 0), (w_in_d[l_][:, 1024 + c0:1024 + c0 + 128], 128),
                             (w_in_d[l_][:, 2048 + c0:2048 + c0 + 128], 256), (w_in_d[l_][:, 3072 + c0:3072 + c0 + 128], 384)])
            for hh in range(4):
                c0 = hh * 256
                plan.append([(w_in_d[l_][:, 4096 + c0:4096 + c0 + 256], 0), (w_in_d[l_][:, 5120 + c0:5120 + c0 + 256], 256)])
                plan.append([(w_in_d[l_][:, 6144 + c0:6144 + c0 + 256], 0), (w_in_d[l_][:, 7168 + c0:7168 + c0 + 256], 256)])
                plan.append([(w_in_d[l_][:, 8192 + c0:8192 + c0 + 256], 0)])
            for r4 in range(2):
                plan.append([(w_in_d[l_][:, 9232 + r4 * 512:9232 + (r4 + 1) * 512], 0)])
                plan.append([(w_pa_d[l_][:, r4 * 512:(r4 + 1) * 512], 0)])
                plan.append([(w_in_d[l_][:, 10256 + r4 * 512:10256 + (r4 + 1) * 512], 0)])
                plan.append([(w_pb_d[l_][:, r4 * 512:(r4 + 1) * 512], 0)])
            for r4 in range(2):
                plan.append([(w_out_d[l_][:, r4 * 512:(r4 + 1) * 512], 0)])
        issued = [False] * len(plan)

        wdest = {}

        def _issue(i, dest=None):
            if issued[i]:
                return
            issued[i] = True
            if dest is not None:
                wdest[i] = dest
            wb, bb = wdest.get(i, (wbufs[i % 2], bwbuf[i % 2]))
            for src, off in plan[i]:
                n = src.shape[1]
                S.dma("pool", ("w%d" % (i % 2)) if i not in wdest else ("wx%d" % (i % 2)),
                      lambda e, src=src, off=off, n=n, wb=wb: e.dma_start(
                          out=wb[:, :, off:off + n], in_=src.rearrange("(c p) n -> p c n", p=128)),
                      writes=[bb])

        def wload_to(dest):
            pass

        def wload(pieces, prefetch=True, dest=None):
            i = wstate["idx"]
            wstate["idx"] += 1
            assert len(pieces) == len(plan[i]) and pieces[0][0].shape[1] == plan[i][0][0].shape[1], ("weight plan mismatch", i)
            _issue(i, dest)
            if prefetch and i + 1 < len(plan):
                _issue(i + 1)
            return wdest.get(i, (wbufs[i % 2], bwbuf[i % 2]))

        try:
            S.dma("sp", "const", lambda e: e.dma_start(out=CC[:], in_=cc_d[:, :]), writes=[bCC])
            S.dma("sp", "const", lambda e: e.dma_start(out=identf[:], in_=identf_d[:, :]), writes=[bconst])
            S.dma("sp", "const", lambda e: e.dma_start(out=triu[:], in_=triu_d[:, :]), writes=[bconst])
            S.dma("sp", "const", lambda e: e.dma_start(out=tril[:], in_=tril_d[:, :]), writes=[bconst])
            S.dma("pool", "const", lambda e: e.dma_start(out=identb[:], in_=identf_d[:, :]), writes=[bconst])
            S.dma("pool", "const", lambda e: e.dma_start(out=resetm[:], in_=resetm_d[:, :]), writes=[bconst])
            S.op("pool", lambda e: e.memset(ones_bf[:], 1.0), writes=[bconst])
            S.op("pool", lambda e: e.memset(ones_f[:], 1.0), writes=[bconst])
            S.op("pool", lambda e: e.memset(hdn2[64:65, :], 1.0), writes=[bhdn2])
            S.op("pool", lambda e: e.memset(small[:], 0.0), writes=[bsmall, bden, bden2, bss, bfix, bzero])
            S.op("act", lambda e: e.activation(out=scb[:], in_=ccv("cvec"), func=AF.Silu), reads=[bCC], writes=[bsmall])

            phase('const')
            areset()
            xst = [af32(1024), af32(1024)]
            bxst = [Buf("xst0"), Buf("xst1")]
            for i in range(8):
                xt, bx = xst[i % 2], bxst[i % 2]
                S.dma("sp", "xin%d" % (i % 2), lambda e, i=i, xt=xt: e.dma_start(out=xt, in_=xin[i * 128:(i + 1) * 128, :]), writes=[bx])
                for g in range(2):
                    bk, bb = pbank()

                    def f(e, xt=xt, g=g, bk=bk):
                        r = None
                        for q in range(4):
                            r = e.transpose(out=ps[:, bk, q * 128:(q + 1) * 128], in_=xt[:, (4 * g + q) * 128:(4 * g + q + 1) * 128], identity=identf[:])
                        return r
                    S.op("pe", f, reads=[bx, bconst], writes=[bb])
                    eng = "act" if g == 0 else "dve"
                    if eng == "act":
                        S.op("act", lambda e, g=g, i=i, bk=bk: e.copy(out=X[:, 4 * g:4 * g + 4, i * 128:(i + 1) * 128], in_=ps[:, bk, :].rearrange("p (q t) -> p q t", q=4)),
                             reads=[bb], writes=bX[4 * g:4 * g + 4])
                    else:
                        S.op("dve", lambda e, g=g, i=i, bk=bk: e.tensor_copy(out=X[:, 4 * g:4 * g + 4, i * 128:(i + 1) * 128], in_=ps[:, bk, :].rearrange("p (q t) -> p q t", q=4)),
                             reads=[bb], writes=bX[4 * g:4 * g + 4])

            def load_tables():
                for c in range(8):
                    S.dma("sp", "tabs", lambda e, c=c: e.dma_start(out=W0[:, c, :], in_=w0_d[c * 128:(c + 1) * 128, :]), writes=[bW0])
                    S.dma("sp", "tabs", lambda e, c=c: e.dma_start(out=WS[:, c, :], in_=wsym_d[c * 128:(c + 1) * 128, :]), writes=[bWS])
            load_tables()
            phase('xload')
            def rms_rstd():
                xsq = [abf(1024), abf(1024)]
                bxsq = [Buf("xsq0"), Buf("xsq1")]
                r1 = af32(1024)
                rstd = af32(1024)
                br = Buf("rstd")
                pk, pbs = ppair()
                for dch in range(8):
                    xs, bx = xsq[dch % 2], bxsq[dch % 2]
                    S.op("act", lambda e, dch=dch, xs=xs: e.activation(out=xs, in_=X[:, dch, :], func=AF.Square), reads=[bX[dch]], writes=[bx])

                    def f(e, dch=dch, xs=xs):
                        e.matmul(ps[:, pk, :], ones_bf[:], xs[:, 0:512], start=(dch == 0), stop=(dch == 7))
                        return e.matmul(ps[:, pk + 1, :], ones_bf[:], xs[:, 512:1024], start=(dch == 0), stop=(dch == 7))
                    S.op("pe", f, reads=[bx, bconst], writes=pbs)
                S.op("act", lambda e: e.activation(out=r1.rearrange("p (a b) -> p a b", a=2), in_=ps[:, pk:pk + 2, :], func=AF.Sqrt, scale=1.0 / D, bias=EPS),
                     reads=pbs, writes=[br])
                S.op("dve", lambda e: e.reciprocal(out=rstd, in_=r1), reads=[br], writes=[br])
                return rstd, br

            def proj_fm(pk, pbs, w, bw, col, rhs=None, brhs=None):
                rhs = hT if rhs is None else rhs
                brhs = bhT if brhs is None else brhs

                def f(e):
                    r = None
                    for half in range(2):
                        for dch in range(8):
                            r = e.matmul(ps[:, pk + half, :], w[:, dch, col:col + 128], rhs[:, dch, half * 512:(half + 1) * 512],
                                         start=(dch == 0), stop=(dch == 7))
                    return r
                S.op("pe", f, reads=[bw] + list(brhs), writes=pbs)

            def conv3(pk, pbs, acc, bacc, w0, w1, w2, bias, fixw0, fixw2):
                pin = ps[:, pk:pk + 2, :]
                flat = pin.rearrange("p a b -> p (a b)")
                S.op("act", lambda e: e.activation(out=acc.rearrange("p (a b) -> p a b", a=2), in_=pin, func=AF.Identity, scale=w1, bias=bias),
                     reads=pbs + [bPP], writes=[bacc])
                S.op("dve", lambda e: e.scalar_tensor_tensor(out=acc[:, 1:1024], in0=flat[:, 0:1023], scalar=w0, in1=acc[:, 1:1024], op0=ALU.mult, op1=ALU.add),
                     reads=pbs + [bPP], writes=[bacc])
                S.op("dve", lambda e: e.scalar_tensor_tensor(out=acc[:, 0:1023], in0=flat[:, 1:1024], scalar=w2, in1=acc[:, 0:1023], op0=ALU.mult, op1=ALU.add),
                     reads=pbs + [bPP], writes=[bacc])
                S.op("dve", lambda e: e.scalar_tensor_tensor(out=acc[:, 256:1024:256], in0=flat[:, 255:1023:256], scalar=fixw0, in1=acc[:, 256:1024:256], op0=ALU.mult, op1=ALU.add),
                     reads=pbs + [bfix], writes=[bacc])
                S.op("dve", lambda e: e.scalar_tensor_tensor(out=acc[:, 255:1023:256], in0=flat[:, 256:1024:256], scalar=fixw2, in1=acc[:, 255:1023:256], op0=ALU.mult, op1=ALU.add),
                     reads=pbs + [bfix], writes=[bacc])

            def transpose_fm_to_tok(src, bsrc, dst_fn, bdst, scale=None):
                for g in range(2):
                    bk, bb = pbank()
                    psb = ps[:, bk, :].bitcast(BF16)

                    def f(e, g=g, psb=psb):
                        r = None
                        for q in range(4):
                            tch = 4 * g + q
                            r = e.transpose(out=psb[:, q * 128:(q + 1) * 128], in_=src[:, tch * 128:(tch + 1) * 128], identity=identb[:])
                        return r
                    S.op("pe", f, reads=[bsrc, bconst], writes=[bb])
                    for q in range(4):
                        tch = 4 * g + q
                        if scale is None:
                            S.op("act" if q % 2 == 0 else "dve",
                                 (lambda e, q=q, tch=tch, psb=psb: e.copy(out=dst_fn(tch), in_=psb[:, q * 128:(q + 1) * 128])) if q % 2 == 0 else
                                 (lambda e, q=q, tch=tch, psb=psb: e.tensor_copy(out=dst_fn(tch), in_=psb[:, q * 128:(q + 1) * 128])),
                                 reads=[bb], writes=[bdst])
                        else:
                            S.op("act", lambda e, q=q, tch=tch, psb=psb: e.activation(out=dst_fn(tch), in_=psb[:, q * 128:(q + 1) * 128], func=AF.Identity, scale=scale),
                                 reads=[bb], writes=[bdst])

            for l in range(2):
                areset()
                S.dma("sp", "pp", lambda e, l=l: e.dma_start(out=PP[:], in_=pp_d[l]), writes=[bPP])

                for (nm_, j_, o_, n_) in (("hcw", 0, 0, 24), ("hcw", 2, 24, 24), ("mcw", 0, 48, 16), ("mcw", 2, 64, 16)):
                    S.op("dve", lambda e: e.tensor_scalar(out=fixw[:, o_:o_ + n_], in0=pp(nm_, j_ * n_, n_), scalar1=ccv("bfixneg"), scalar2=sm("zero"), op0=ALU.mult, op1=ALU.add),
                         reads=[bPP, bCC, bzero], writes=[bfix])
                phase('norm%d' % l)
                feats = af32(1024)
                arg = af32(1024)
                kq = af32(1024)
                hd1 = af32(1024)
                w1s = af32(64)
                w2s = af32(64)
                bF = Buf("filt")
                S.dma("sp", "hyw", lambda e: e.dma_start(out=feats[0:33, :], in_=feats_d[:, :]), writes=[bF])
                S.dma("sp", "hyw", lambda e, l=l: e.dma_start(out=w1s[0:33, :], in_=hy_w1_d[l]), writes=[bF])
                S.dma("sp", "hyw", lambda e, l=l: e.dma_start(out=w2s[0:64, :], in_=hy_w2_d[l]), writes=[bF])
                S.op("dve", lambda e: e.tensor_tensor(out=sm("fb", 0, 1, slice(0, 64)), in0=pp("hfr", 0, 1, slice(0, 64)), in1=pp("hb1", 0, 1, slice(0, 64)), op=ALU.mult),
                     reads=[bPP], writes=[bsmall])
                S.op("dve", lambda e: e.tensor_tensor(out=sm("fb", 1, 1, slice(0, 64)), in0=pp("hfr", 1, 1, slice(0, 64)), in1=pp("hb2", 0, 1, slice(0, 64)), op=ALU.mult),
                     reads=[bPP], writes=[bsmall])
                for layer in range(2):
                    pk, pbs = ppair()
                    if layer == 0:
                        def f(e, pk=pk):
                            e.matmul(ps[0:64, pk, :], w1s[0:33, 0:64], feats[0:33, 0:512], start=True, stop=True)
                            return e.matmul(ps[0:64, pk + 1, :], w1s[0:33, 0:64], feats[0:33, 512:1024], start=True, stop=True)
                    else:
                        def f(e, pk=pk):
                            e.matmul(ps[0:64, pk, :], w2s[0:64, 0:64], hd1[0:64, 0:512], start=True, stop=True)
                            return e.matmul(ps[0:64, pk + 1, :], w2s[0:64, 0:64], hd1[0:64, 512:1024], start=True, stop=True)
                    S.op("pe", f, reads=[bF], writes=pbs)
                    S.op("dve", lambda e, pk=pk, layer=layer: e.tensor_scalar(
                        out=arg[0:64, :].rearrange("p (a b) -> p a b", a=2), in0=ps[0:64, pk:pk + 2, :],
                        scalar1=pp("hfr", layer, 1, slice(0, 64)), scalar2=sm("fb", layer, 1, slice(0, 64)), op0=ALU.mult, op1=ALU.add),
                        reads=pbs + [bPP, bsmall], writes=[bF])
                    S.op("dve", lambda e: e.tensor_scalar(out=kq[0:64, :], in0=arg[0:64, :], scalar1=1.0 / (2 * math.pi), scalar2=MAGIC, op0=ALU.mult, op1=ALU.add),
                         reads=[bF], writes=[bF])
                    S.op("dve", lambda e: e.tensor_scalar(out=kq[0:64, :], in0=kq[0:64, :], scalar1=-MAGIC, scalar2=0.0, op0=ALU.add, op1=ALU.add),
                         reads=[bF], writes=[bF])
                    S.op("dve", lambda e: e.scalar_tensor_tensor(out=arg[0:64, :], in0=kq[0:64, :], scalar=-2 * math.pi, in1=arg[0:64, :], op0=ALU.mult, op1=ALU.add),
                         reads=[bF], writes=[bF])
                    S.op("dve", lambda e: e.tensor_scalar(out=arg[0:64, :], in0=arg[0:64, :], scalar1=PI_LO, scalar2=-PI_LO, op0=ALU.min, op1=ALU.max),
                         reads=[bF], writes=[bF])
                    if layer == 0:
                        S.op("act", lambda e: e.activation(out=hd1[0:64, :], in_=arg[0:64, :], func=AF.Sin), reads=[bF], writes=[bF])
                    else:
                        S.op("act", lambda e: e.activation(out=hdn2[0:64, :], in_=arg[0:64, :], func=AF.Sin), reads=[bF], writes=[bhdn2])

                rstd, br = rms_rstd()
                bk, bb = 7, pb[7]
                for cb in range(6):
                    w, bw = wload([(w_ada_d[l][:, cb * 512:(cb + 1) * 512], 0)])

                    def f(e, cb=cb, w=w, bk=bk):
                        r = None
                        for q in range(4):
                            for kch in range(8):
                                r = e.matmul(ps[:, bk, cb * 4 + q:cb * 4 + q + 1], w[:, kch, q * 128:(q + 1) * 128], scb[:, kch:kch + 1],
                                             start=(kch == 0), stop=(kch == 7))
                        return r
                    S.op("pe", f, reads=[bw, bsmall], writes=[bb])
                S.op("dve", lambda e, bk=bk: e.tensor_tensor(out=sm("modsb"), in0=ps[:, bk, 0:24], in1=pp("b_ada"), op=ALU.add), reads=[bb, bPP], writes=[bsmall])
                S.op("dve", lambda e: e.scalar_tensor_tensor(out=sm("gs"), in0=sm("modsb", 8, 8), scalar=1.0, in1=pp("norm_g"), op0=ALU.add, op1=ALU.mult),
                     reads=[bsmall, bPP], writes=[bsmall])

                phase('ada%d' % l)
                tmpf = [af32(1024), af32(1024)]
                btmp = [Buf("tmpf0"), Buf("tmpf1")]
                for dch in range(8):
                    tf, bt = tmpf[dch % 2], btmp[dch % 2]
                    S.op("dve", lambda e, dch=dch, tf=tf: e.scalar_tensor_tensor(out=tf, in0=X[:, dch, :], scalar=sm("gs", dch, 1), in1=rstd, op0=ALU.mult, op1=ALU.mult),
                         reads=[bX[dch], bsmall, br], writes=[bt])
                    S.op("act", lambda e, dch=dch, tf=tf: e.activation(out=hT[:, dch, :], in_=tf, func=AF.Identity, bias=sm("modsb", dch, 1), scale=1.0),
                         reads=[bt, bsmall], writes=[bhT[dch]])
                dbg("hT%d" % l, hT[:], bhT, [128, 8, T], BF16)

                phase('F0%d' % l)
                areset()
                w, bw = wload([(wgi_d[l], 0), (wgf_d[l], 128)])
                pki, pbi = ppair()
                pkf, pbf = ppair()
                proj_fm(pki, pbi, w, bw, 0)
                proj_fm(pkf, pbf, w, bw, 128)
                R64 = slice(0, 64)
                Gi = af32(1024)
                ee = af32(1024)
                sp_ = af32(1024)
                cs = af32(1024)
                csx = af32(1024)
                aa = af32(1024)
                bG = Buf("gates")
                mcur0 = sm("m0", 0, 1, R64)
                S.dma("sp", "m0", lambda e, l=l: e.dma_start(out=mcur0, in_=m0_d[l]), writes=[bsmall])
                S.op("dve", lambda e: e.tensor_scalar(out=sm("ngbf", 0, 1, R64), in0=pp("gbf", 0, 1, R64), scalar1=-1.0, scalar2=0.0, op0=ALU.mult, op1=ALU.add),
                     reads=[bPP], writes=[bsmall])
                S.op("act", lambda e: e.activation(out=Gi[R64, :].rearrange("p (a b) -> p a b", a=2), in_=ps[R64, pki:pki + 2, :], func=AF.Identity,
                                                   bias=pp("gbi", 0, 1, R64), scale=1.0), reads=pbi + [bPP], writes=[bG])
                S.op("act", lambda e: e.activation(out=ee[R64, :].rearrange("p (a b) -> p a b", a=2), in_=ps[R64, pkf:pkf + 2, :], func=AF.Exp,
                                                   bias=sm("ngbf", 0, 1, R64), scale=-1.0), reads=pbf + [bsmall], writes=[bG])
                S.op("act", lambda e: e.activation(out=sp_[R64, :], in_=ee[R64, :], func=AF.Ln, bias=1.0, scale=1.0), reads=[bG], writes=[bG])
                S.op("dve", lambda e: e.tensor_tensor_scan(out=cs[R64, :], data0=resetm[R64, :], data1=sp_[R64, :], initial=0.0, op0=ALU.mult, op1=ALU.add),
                     reads=[bG, bconst], writes=[bG])
                R32a, R32b = slice(0, 32), slice(32, 64)
                S.op("pool", lambda e: e.tensor_copy(out=csx[R32a, :], in_=cs[R32a, :]), reads=[bG], writes=[bG])
                S.op("dve", lambda e: e.tensor_tensor(out=csx[R32b, :], in0=sp_[R32b, :], in1=cs[R32b, :], op=ALU.subtract), reads=[bG], writes=[bG])
                S.op("dve", lambda e: e.tensor_tensor(
                    out=csx[R32b, :].rearrange("p (j t) -> p j t", t=128), in0=csx[R32b, :].rearrange("p (j t) -> p j t", t=128),
                    in1=cs[R32b, :].rearrange("p (j t) -> p j t", t=128)[:, :, 127:128].to_broadcast([32, 8, 128]), op=ALU.add), reads=[bG], writes=[bG])
                S.op("dve", lambda e: e.tensor_tensor(out=aa[R64, :], in0=Gi[R64, :], in1=csx[R64, :], op=ALU.add), reads=[bG], writes=[bG])
                S.op("dve", lambda e: e.tensor_reduce(out=sm("amax", 0, 8, R64), in_=aa[R64, :].rearrange("p (j t) -> p j t", t=128), axis=AX.X, op=ALU.max),
                     reads=[bG], writes=[bsmall])
                S.op("dve", lambda e: e.tensor_scalar(out=sm("negtot", 0, 8, R64).unsqueeze(2), in0=cs[R64, :].rearrange("p (j t) -> p j t", t=128)[:, :, 127:128],
                                                      scalar1=-1.0, scalar2=0.0, op0=ALU.mult, op1=ALU.add), reads=[bG], writes=[bsmall])
                for k in range(8):
                    mcur = mcur0 if k == 0 else sm("mst", k - 1, 1, R64)
                    S.op("dve", lambda e, k=k, mcur=mcur: e.tensor_tensor(out=sm("meff", 0, 1, R64), in0=mcur, in1=ccv("keep", k, 1, R64), op=ALU.mult),
                         reads=[bsmall, bCC], writes=[bsmall])
                    for rr, jj in ((R32a, k), (R32b, 7 - k)):
                        S.op("dve", lambda e, rr=rr, jj=jj: e.tensor_tensor(out=sm("Rc", jj, 1, rr), in0=sm("meff", 0, 1, rr), in1=sm("amax", jj, 1, rr), op=ALU.max),
                             reads=[bsmall], writes=[bsmall])
                        S.op("dve", lambda e, rr=rr, jj=jj: e.tensor_tensor(out=sm("dmc", jj, 1, rr), in0=sm("meff", 0, 1, rr), in1=sm("Rc", jj, 1, rr), op=ALU.subtract),
                             reads=[bsmall], writes=[bsmall])
                        S.op("dve", lambda e, rr=rr, jj=jj, k=k: e.tensor_tensor(out=sm("mst", k, 1, rr), in0=sm("negtot", jj, 1, rr), in1=sm("Rc", jj, 1, rr), op=ALU.add),
                             reads=[bsmall], writes=[bsmall])
                S.dma("sp", "smout", lambda e, l=l: e.dma_start(out=sm_d[l], in_=sm("mst", 0, 8, R64)), reads=[bsmall])
                S.op("act", lambda e: e.activation(out=sm("wmk", 0, 8, R64), in_=sm("dmc", 0, 8, R64), func=AF.Exp), reads=[bsmall], writes=[bsmall])
                S.op("dve", lambda e: e.tensor_tensor(out=sm("wmk", 0, 8, R64), in0=sm("wmk", 0, 8, R64), in1=ccv("keepc", 0, 8, R64), op=ALU.mult),
                     reads=[bsmall, bCC], writes=[bsmall])
                Rbc = sm("Rc", 0, 8, R64).unsqueeze(2).to_broadcast([64, 8, 128])
                S.op("dve", lambda e: e.tensor_tensor(out=Gi[R64, :].rearrange("p (j t) -> p j t", t=128), in0=aa[R64, :].rearrange("p (j t) -> p j t", t=128), in1=Rbc, op=ALU.subtract),
                     reads=[bG, bsmall], writes=[bG])
                S.op("dve", lambda e: e.tensor_tensor(out=ee[R64, :].rearrange("p (j t) -> p j t", t=128), in0=csx[R64, :].rearrange("p (j t) -> p j t", t=128), in1=Rbc, op=ALU.subtract),
                     reads=[bG, bsmall], writes=[bG])
                S.op("act", lambda e: e.activation(out=Gi[R64, :], in_=Gi[R64, :], func=AF.Exp), reads=[bG], writes=[bG])
                S.op("act", lambda e: e.activation(out=ee[R64, :], in_=ee[R64, :], func=AF.Exp), reads=[bG], writes=[bG])
                for src, dst, bdst in ((Gi, waT, bwaT), (ee, ebT, bebT)):
                    bk, bb = pbank()

                    def f(e, src=src, bk=bk):
                        r = None
                        for j in range(8):
                            r = e.transpose(out=ps[:, bk, j * 64:(j + 1) * 64], in_=src[R64, j * 128:(j + 1) * 128], identity=identf[0:64, 0:64])
                        return r
                    S.op("pe", f, reads=[bG, bconst], writes=[bb])
                    for d in range(2):
                        S.op("dve", lambda e, dst=dst, bk=bk, d=d: e.tensor_copy(out=dst[:, :, 4 * d:4 * d + 4], in_=ps[:, bk, :].rearrange("p (j c) -> p j c", c=64)[:, :, 32 * d:32 * d + 4]),
                             reads=[bb], writes=[bdst])
                Dm = csx
                S.op("dve", lambda e: e.tensor_tensor(out=Dm[R64, 0:512].rearrange("p (j c) -> p j c", c=64),
                                                      in0=identf[0:64, 0:64].unsqueeze(1).to_broadcast([64, 8, 64]),
                                                      in1=sm("wmk", 0, 8, R64).unsqueeze(2).to_broadcast([64, 8, 64]), op=ALU.mult),
                     reads=[bG, bsmall, bconst], writes=[bG])
                bk, bb = pbank()
                S.op("pe", lambda e, bk=bk: e.matmul(ps[:, bk, :], ones_f[0:64, 0:128], Dm[R64, 0:512], start=True, stop=True), reads=[bG, bconst], writes=[bb])
                for d in range(2):
                    S.op("dve", lambda e, bk=bk, d=d: e.tensor_copy(out=wmkB[:, :, 4 * d:4 * d + 4], in_=ps[:, bk, :].rearrange("p (j c) -> p j c", c=64)[:, :, 32 * d:32 * d + 4]),
                         reads=[bb], writes=[bwmkB])

                phase('M0%d' % l)
                areset(ext=True)
                x1T = abf(1024); x2T = abf(1024); vT = abf(1024); zT = abf(1024)
                z1T = abf(1024)
                vtok = abf(1024)
                KS = abf(2048)
                YY = abf(2048)
                HA = [abf(2048), abf(2048)]
                HB = [abf(2048), abf(2048)]
                acc = af32(1024)
                acc2 = af32(1024)
                kt = [af32(256), af32(256)]
                absk = [abf(256), abf(256), abf(256)]
                Et = [af32(256), af32(256)]
                absdec = [af32(256), af32(256)]
                tmpA = [abf(512)] * 2
                tmpB = [abf(512)] * 2
                w3b = [abf(256), abf(256)]
                hsc = af32(8)
                bx1, bx2, bv, bz, bz1, bvt, bKS, bYY, bacc = (Buf(n) for n in ("x1T", "x2T", "vT", "zT", "z1T", "vtok", "KS", "YY", "acc"))
                bacc2 = Buf("acc2")
                bHs = [Buf("H0"), Buf("H1")]
                bkt = [Buf("kt0"), Buf("kt1")]
                babs = [Buf("absk0"), Buf("absk1"), Buf("absk2")]
                bEt = [Buf("E0"), Buf("E1")]
                bdec, bw3 = [Buf("absdec0"), Buf("absdec1")], [Buf("w3b0"), Buf("w3b1")]
                btA, btB = [Buf("tA0")] * 2, [Buf("tB0")] * 2
                bhsc = [Buf("hsc0"), Buf("hsc1")]
                vtok3 = vtok.rearrange("p (t c) -> p t c", c=128)
                KS3 = KS.rearrange("p (j c) -> p j c", c=128)
                YY3 = YY.rearrange("p (j c) -> p j c", c=128)
                HA3 = [h.rearrange("p (j c) -> p j c", c=256) for h in HA]
                HB3 = [h.rearrange("p (j c) -> p j c", c=256) for h in HB]

                def g_filter(blk, o, hs):
                    c0 = blk * 128
                    wb3, bwb3 = w3b[hs], bw3[hs]
                    dec, bdc = absdec[hs], bdec[hs]
                    stg, bstg = Et[1], bEt[1]
                    for dr in range(2):
                        cc0 = o * 2048 + dr * 1024 + c0
                        S.dma("sp", "w3_%d" % hs, lambda e: e.dma_start(out=stg[0:64, dr * 128:(dr + 1) * 128], in_=hy_w3_d[l][:, cc0:cc0 + 128]), writes=[bstg])
                        S.dma("sp", "w3_%d" % hs, lambda e: e.dma_start(out=stg[64:65, dr * 128:(dr + 1) * 128], in_=hy_b3_d[l][:, cc0:cc0 + 128]), writes=[bstg])
                        S.dma("sp", "dec%d" % hs, lambda e: e.dma_start(out=dec[:, dr * 128:(dr + 1) * 128], in_=hy_dec_d[l][:, cc0:cc0 + 128].partition_broadcast(128)), writes=[bdc])
                    S.op("act", lambda e: e.copy(out=wb3[0:65, 0:256], in_=stg[0:65, 0:256]), reads=[bstg], writes=[bwb3])
                    S.op("act", lambda e: e.activation(out=dec, in_=dec, func=AF.Abs), reads=[bdc], writes=[bdc])
                    yield
                    l1col = ps[:, 7, 32 + hs:33 + hs]
                    pending_l1 = []
                    for tch in range(8):
                        k_t, bk_t = kt[tch % 2], bkt[tch % 2]
                        a_t, ba_t = absk[tch % 3], babs[tch % 3]
                        E_t, bE_t = Et[tch % 2], bEt[tch % 2]
                        hk, hb = pbank()
                        S.op("pe", lambda e: e.matmul(ps[:, hk, 0:256], hdn2[0:65, tch * 128:(tch + 1) * 128], wb3[0:65, 0:256], start=True, stop=True),
                             reads=[bhdn2, bwb3], writes=[hb])
                        if len(pending_l1) >= 2:
                            pending_l1.pop(0)()
                        S.op("act", lambda e: e.activation(out=E_t, in_=dec, func=AF.Exp, scale=ccv("tnneg", tch, 1)), reads=[bdc, bCC], writes=[bE_t])
                        S.op("dve", lambda e: e.scalar_tensor_tensor(out=k_t, in0=E_t, scalar=0.05, in1=ps[:, hk, 0:256], op0=ALU.add, op1=ALU.mult),
                             reads=[bE_t, hb], writes=[bk_t])
                        S.op("dve", lambda e: e.scalar_tensor_tensor(out=KS3[:, tch, :], in0=k_t[:, 128:256], scalar=ccv("maskb", tch, 1), in1=k_t[:, 0:128], op0=ALU.mult, op1=ALU.add),
                             reads=[bk_t, bCC], writes=[bKS])
                        S.op("dve", lambda e: e.scalar_tensor_tensor(out=KS3[:, 8 + tch, :], in0=k_t[:, 128:256], scalar=ccv("nmaskb", tch, 1), in1=k_t[:, 0:128], op0=ALU.mult, op1=ALU.add),
                             reads=[bk_t, bCC], writes=[bKS])
                        S.op("act", lambda e: e.activation(out=a_t[:, 0:128], in_=k_t[:, 0:128], func=AF.Abs), reads=[bk_t], writes=[ba_t])
                        S.op("act", lambda e: e.activation(out=a_t[:, 128:256], in_=k_t[:, 128:256], func=AF.Abs, scale=ccv("maskb", tch, 1)),
                             reads=[bk_t, bCC], writes=[ba_t])

                        def emit_l1(a_t=a_t, ba_t=ba_t, tch=tch):
                            def f(e):
                                e.matmul(l1col, a_t[:, 0:128], ones_bf[:, 0:1], start=(tch == 0), stop=False)
                                e.matmul(l1col, a_t[:, 128:256], ones_bf[:, 0:1], start=False, stop=(tch == 7))
                            S.op("pe", f, reads=[ba_t, bconst], writes=[pb[7]])
                        pending_l1.append(emit_l1)
                        yield
                    while pending_l1:
                        pending_l1.pop(0)()
                    yield ("start", 2 * blk + o)
                    S.op("dve", lambda e: e.tensor_scalar(out=hsc[:, 4 * hs:4 * hs + 1], in0=l1col, scalar1=ccv("invhs"), scalar2=sm("zero"), op0=ALU.mult, op1=ALU.add),
                         reads=[pb[7], bCC, bzero], writes=[bhsc[hs]])
                    S.op("dve", lambda e: e.reciprocal(out=hsc[:, 4 * hs + 1:4 * hs + 2], in_=hsc[:, 4 * hs:4 * hs + 1]), reads=[bhsc[hs]], writes=[bhsc[hs]])
                    S.op("dve", lambda e: e.tensor_tensor(out=hsc[:, 4 * hs + 2:4 * hs + 3], in0=hsc[:, 4 * hs:4 * hs + 1], in1=pp("hbias", o * 8 + blk, 1), op=ALU.mult),
                         reads=[bhsc[hs], bPP], writes=[bhsc[hs]])
                    for jp in range(4):
                        hk, hb = pbank()

                        def f(e):
                            for u in range(2):
                                jj = 2 * jp + u
                                for tch in range(8):
                                    e.matmul(ps[:, hk, u * 256:u * 256 + 128], W0[:, tch, jj * 128:(jj + 1) * 128], KS3[:, tch, :], start=(tch == 0), stop=(tch == 7))
                                for tch in range(8):
                                    e.matmul(ps[:, hk, u * 256 + 128:u * 256 + 256], W0[:, tch, 1024 + jj * 128:1024 + (jj + 1) * 128], KS3[:, 8 + tch, :], start=(tch == 0), stop=(tch == 7))
                        S.op("pe", f, reads=[bW0, bKS], writes=[hb])
                        pv = ps[:, hk, :].rearrange("p (u r c) -> p u r c", u=2, r=2)
                        S.op("act", lambda e: e.copy(out=HA3[hs][:, 2 * jp:2 * jp + 2, :], in_=ps[:, hk, :].rearrange("p (u c) -> p u c", u=2)), reads=[hb], writes=[bHs[hs]])
                        S.op("act", lambda e: e.copy(out=HB3[hs][:, 2 * jp:2 * jp + 2, 0:128], in_=pv[:, :, 1, :]), reads=[hb], writes=[bHs[hs]])
                        S.op("act", lambda e: e.copy(out=HB3[hs][:, 2 * jp:2 * jp + 2, 128:256], in_=pv[:, :, 0, :]), reads=[hb], writes=[bHs[hs]])
                        yield ("fjp", 4 * (2 * blk + o) + jp)
                    if blk <= 1:
                        dbg("HA_%d_%d_%d" % (l, blk, o), HA[hs], [bHs[hs]], [128, 2048], BF16)
                        dbg("KS_%d_%d_%d" % (l, blk, o), KS, [bKS], [128, 2048], BF16)

                def g_pre(blk):
                    c0 = blk * 128
                    w, bw = wload([(w_in_d[l][:, c0:c0 + 128], 0), (w_in_d[l][:, 1024 + c0:1024 + c0 + 128], 128),
                                   (w_in_d[l][:, 2048 + c0:2048 + c0 + 128], 256), (w_in_d[l][:, 3072 + c0:3072 + c0 + 128], 384)])
                    for si, (dstT, bd) in enumerate(((vT, bv), (x1T, bx1), (x2T, bx2))):
                        sec = (2, 0, 1)[si]
                        pk, pbs = ppair()
                        proj_fm(pk, pbs, w, bw, sec * 128)
                        ch = sec * 8 + blk
                        ac_, bac_ = (acc, bacc) if si % 2 == 0 else (acc2, bacc2)
                        conv3(pk, pbs, ac_, bac_, pp("hcw", 0 * 24 + ch, 1), pp("hcw", 1 * 24 + ch, 1), pp("hcw", 2 * 24 + ch, 1), pp("hcb", ch, 1),
                              fixw[:, ch:ch + 1], fixw[:, 24 + ch:25 + ch])
                        S.op("act", lambda e: e.copy(out=dstT, in_=ac_), reads=[bac_], writes=[bd])
                        yield
                        if si == 0:
                            transpose_fm_to_tok(vT, bv, lambda tch: vtok3[:, tch, :], bvt)
                            yield
                    pk, pbs = ppair()
                    proj_fm(pk, pbs, w, bw, 384)
                    S.op("act", lambda e: e.activation(out=zT.rearrange("p (a b) -> p a b", a=2), in_=ps[:, pk:pk + 2, :], func=AF.Silu), reads=pbs, writes=[bz])
                    if blk == 0:
                        dbg("x1T%d" % l, x1T, [bx1], [128, T], BF16)
                        dbg("vT%d" % l, vT, [bv], [128, T], BF16)
                    yield

                def g_data(blk, o, hs):
                    zcurT, bzc = (vT, bv) if o == 0 else (z1T, bz1)
                    for jp in range(4):
                        yield ("dneed", 4 * (2 * blk + o) + jp)
                        hk, hb = pbank()
                        tA, btA_ = tmpA[jp % 2], btA[jp % 2]
                        tB, btB_ = tmpB[jp % 2], btB[jp % 2]

                        def f(e):
                            for u in range(2):
                                jj = 2 * jp + u
                                for tch in range(8):
                                    e.matmul(ps[:, hk, u * 256:u * 256 + 128], WS[:, tch, jj * 128:(jj + 1) * 128], vtok3[:, tch, :], start=(tch == 0), stop=(tch == 7))
                                for tch in range(8):
                                    e.matmul(ps[:, hk, u * 256 + 128:u * 256 + 256], WS[:, tch, 1024 + jj * 128:1024 + (jj + 1) * 128], vtok3[:, tch, :], start=(tch == 0), stop=(tch == 7))
                        S.op("pe", f, reads=[bWS, bvt], writes=[hb])
                        S.op("dve", lambda e: e.tensor_tensor(out=tA, in0=ps[:, hk, :], in1=HA3[hs][:, 2 * jp:2 * jp + 2, :].rearrange("p u c -> p (u c)"), op=ALU.mult),
                             reads=[hb, bHs[hs]], writes=[btA_])
                        S.op("dve", lambda e: e.tensor_tensor(out=tB, in0=ps[:, hk, :], in1=HB3[hs][:, 2 * jp:2 * jp + 2, :].rearrange("p u c -> p (u c)"), op=ALU.mult),
                             reads=[hb, bHs[hs]], writes=[btB_])
                        tAv = tA.rearrange("p (u r c) -> p u r c", u=2, r=2)
                        tBv = tB.rearrange("p (u r c) -> p u r c", u=2, r=2)
                        S.op("pool", lambda e: e.tensor_tensor(out=YY3[:, 2 * jp:2 * jp + 2, :], in0=tAv[:, :, 0, :], in1=tAv[:, :, 1, :], op=ALU.subtract), reads=[btA_], writes=[bYY])
                        S.op("pool", lambda e: e.tensor_tensor(out=YY3[:, 8 + 2 * jp:8 + 2 * jp + 2, :], in0=tBv[:, :, 0, :], in1=tBv[:, :, 1, :], op=ALU.add), reads=[btB_], writes=[bYY])
                        yield
                    if blk <= 1:
                        dbg("YY_%d_%d_%d" % (l, blk, o), YY, [bYY], [128, 2048], BF16)
                        dbg("vtok_%d_%d_%d" % (l, blk, o), vtok, [bvt], [128, 1024], BF16)
                    pk, pbs = ppair()

                    def f(e):
                        for half in range(2):
                            for j in range(16):
                                base = 0 if j < 8 else 1024
                                e.matmul(ps[:, pk + half, :], YY3[:, j, :], WS[:, j % 8, base + half * 512:base + (half + 1) * 512], start=(j == 0), stop=(j == 15))
                    S.op("pe", f, reads=[bWS, bYY], writes=pbs)
                    gT, bg = (x1T, bx1) if o == 0 else (x2T, bx2)
                    S.op("dve", lambda e: e.scalar_tensor_tensor(
                        out=acc.rearrange("p (a b) -> p a b", a=2), in0=zcurT.rearrange("p (a b) -> p a b", a=2), scalar=hsc[:, 4 * hs + 2:4 * hs + 3],
                        in1=ps[:, pk:pk + 2, :], op0=ALU.mult, op1=ALU.add), reads=pbs + [bzc, bhsc[hs]], writes=[bacc])
                    if o == 0:
                        S.op("dve", lambda e: e.scalar_tensor_tensor(out=z1T, in0=acc, scalar=hsc[:, 4 * hs + 1:4 * hs + 2], in1=gT, op0=ALU.mult, op1=ALU.mult),
                             reads=[bacc, bhsc[hs], bg], writes=[bz1])
                        yield
                        transpose_fm_to_tok(z1T, bz1, lambda tch: vtok3[:, tch, :], bvt)
                    else:
                        S.op("dve", lambda e: e.scalar_tensor_tensor(out=acc, in0=acc, scalar=hsc[:, 4 * hs + 1:4 * hs + 2], in1=gT, op0=ALU.mult, op1=ALU.mult),
                             reads=[bacc, bhsc[hs], bg], writes=[bacc])
                        S.op("pool", lambda e: e.tensor_tensor(out=yaT[:, blk, :], in0=acc, in1=zT, op=ALU.mult), reads=[bacc, bz], writes=[byaT[blk]])
                    yield

                stages = [(blk, o) for blk in range(8) for o in range(2)]

                def Dchain():
                    for si_, (blk, o) in enumerate(stages):
                        if o == 0:
                            yield from g_pre(blk)
                        yield from g_data(blk, o, si_ % 2)
                        yield ("done", si_)

                def Fchain():
                    for si_, (blk, o) in enumerate(stages):
                        if si_ == 0:
                            continue
                        yield from g_filter(blk, o, si_ % 2)

                for _ in g_filter(0, 0, 0):
                    pass
                if SEQ_DEBUG:
                    for si_, (blk, o) in enumerate(stages):
                        if si_ + 1 < len(stages):
                            for _ in g_filter(stages[si_ + 1][0], stages[si_ + 1][1], (si_ + 1) % 2):
                                pass
                        for _ in (g_pre(blk) if o == 0 else ()):
                            pass
                        for _ in g_data(blk, o, si_ % 2):
                            pass
                else:
                    Dg, Fg = Dchain(), Fchain()
                    d_done, f_next = -1, None
                    f_done, d_next = 3, None
                    ud = uf = 0.0
                    d_alive = f_alive = True
                    while d_alive or f_alive:
                        f_ok = f_alive and (f_next is None or d_done >= f_next - 2)
                        d_ok = d_alive and (d_next is None or f_done >= d_next)
                        if f_ok and (uf <= ud or not d_ok):
                            try:
                                r = next(Fg)
                                uf += 0.0 if (isinstance(r, tuple) and r[0] == "start") else 16.0
                                f_next = None
                                if isinstance(r, tuple):
                                    if r[0] == "start":
                                        f_next = r[1]
                                    else:
                                        f_done = r[1]
                            except StopIteration:
                                f_alive = False
                        elif d_ok:
                            try:
                                r = next(Dg)
                                ud += 0.0 if (isinstance(r, tuple) and r[0] == "dneed") else UD_UNITS
                                d_next = None
                                if isinstance(r, tuple):
                                    if r[0] == "dneed":
                                        d_next = r[1]
                                    else:
                                        d_done = r[1]
                            except StopIteration:
                                d_alive = False
                        else:
                            raise AssertionError("pipeline driver stuck")
                dbg("yaT%d" % l, yaT[:], byaT, [128, 8, T], BF16)

                phase('hy%d' % l)
                areset()

                def make_alloc(base, limit):
                    st = {"off": 0}

                    def f32(n):
                        o = st["off"]
                        st["off"] = o + n
                        assert st["off"] <= limit, ("arena overflow", st["off"])
                        return base[:, o:o + n]

                    def bf(n):
                        o = st["off"]
                        st["off"] = o + n // 2
                        assert st["off"] <= limit, ("arena overflow", st["off"])
                        return base[:, o:o + n // 2].bitcast(BF16)
                    return f32, bf

                def make_head(f32, bf, tag):
                    H = {"tag": tag}
                    H["qT"] = bf(2048); H["kT"] = bf(2048); H["ktok"] = bf(2048); H["Vaug"] = bf(8 * 258); H["og"] = bf(2048)
                    H["STw"] = [bf(128), bf(128)]; H["kw"] = [bf(256), bf(256)]
                    H["Cst"] = f32(2 * 2 * 257); H["Csb"] = [bf(2 * 258), bf(2 * 258)]
                    H["hsum"] = f32(2048); H["sgt"] = [bf(256), bf(256)]
                    H["den"] = f32(4); H["ss"] = f32(16)
                    for n in ("qT", "kT", "ktok", "og", "hsum", "hsum2", "Vaug", "den0", "den1", "ss"):
                        H["b_" + n] = Buf(tag + n)
                    for n in ("STw", "kw", "C", "Cs", "sgt"):
                        H["b_" + n] = [Buf(tag + n + "0"), Buf(tag + n + "1")]
                    return H

                HS = [make_head(af32, abf, "A"), make_head(*make_alloc(arenaB, 16384), "B")]

                def head_P(hh, H):
                    qT3 = H["qT"].rearrange("p (q t) -> p q t", q=2)
                    kT3 = H["kT"].rearrange("p (q t) -> p q t", q=2)
                    ktok3 = H["ktok"].rearrange("p (j c) -> p j c", c=256)
                    Va3 = H["Vaug"].rearrange("p (j c) -> p j c", c=258)
                    og3 = H["og"].rearrange("p (j c) -> p j c", c=256)
                    Cst4 = H["Cst"].rearrange("p (d q v) -> p d q v", d=2, q=2)
                    hsum = H["hsum"]
                    bq, bkT, bkt_, bog, bhs, bhs2, bVa, bC = H["b_qT"], H["b_kT"], H["b_ktok"], H["b_og"], H["b_hsum"], H["b_hsum2"], H["b_Vaug"], H["b_C"]
                    for d in range(2):
                        S.dma("sp", "cst" + H["tag"], lambda e, d=d: e.dma_start(out=Cst4[:, d, :, :], in_=c0_d[l, d, hh].rearrange("(q p) v -> p q v", p=128)), writes=[bC[d]])
                    S.op("pool", lambda e: e.memset(Va3[:, :, 256:258], 1.0), writes=[bVa])
                    c0 = hh * 256
                    w, bw = wload([(w_in_d[l][:, 4096 + c0:4096 + c0 + 256], 0), (w_in_d[l][:, 5120 + c0:5120 + c0 + 256], 256)])
                    for si, (dst3, bd) in enumerate(((qT3, bq), (kT3, bkT))):
                        for q2 in range(2):
                            pk, pbs = ppair()
                            proj_fm(pk, pbs, w, bw, si * 256 + q2 * 128)
                            ch = si * 8 + hh * 2 + q2
                            ac_, bac_ = (hsum[:, 0:1024], bhs) if q2 == 0 else (hsum[:, 1024:2048], bhs2)
                            conv3(pk, pbs, ac_, bac_, pp("mcw", 0 * 16 + ch, 1), pp("mcw", 1 * 16 + ch, 1), pp("mcw", 2 * 16 + ch, 1), pp("mcb", ch, 1),
                                  fixw[:, 48 + ch:49 + ch], fixw[:, 64 + ch:65 + ch])
                            S.op("act", lambda e: e.activation(out=dst3[:, q2, :], in_=ac_, func=AF.Silu), reads=[bac_], writes=[bd])
                            yield
                    if hh == 0:
                        dbg("qT%d" % l, H["qT"], [bq], [128, 2048], BF16)
                    for q2 in range(2):
                        transpose_fm_to_tok(kT3[:, q2, :], bkT, lambda tch, q2=q2: ktok3[:, tch, q2 * 128:(q2 + 1) * 128], bkt_, scale=1.0 / 16.0)
                        yield
                    wB, bwB = wload([(w_in_d[l][:, 6144 + c0:6144 + c0 + 256], 0), (w_in_d[l][:, 7168 + c0:7168 + c0 + 256], 256)])
                    for i in range(8):
                        bk, bb = pbank()

                        def f(e):
                            for dch in range(8):
                                e.matmul(ps[:, bk, :], hT[:, dch, i * 128:(i + 1) * 128], wB[:, dch, :], start=(dch == 0), stop=(dch == 7))
                        S.op("pe", f, reads=[bwB] + bhT, writes=[bb])
                        S.op("dve", lambda e: e.tensor_copy(out=Va3[:, i, 0:256], in_=ps[:, bk, 0:256]), reads=[bb], writes=[bVa])
                        S.op("act", lambda e: e.activation(out=og3[:, i, :], in_=ps[:, bk, 256:512], func=AF.Sigmoid), reads=[bb], writes=[bog])
                        yield
                    wC, bwC = wload([(w_in_d[l][:, 8192 + c0:8192 + c0 + 256], 0)])
                    for i in range(8):
                        bk, bb = pbank()

                        def f(e):
                            for dch in range(8):
                                e.matmul(ps[:, bk, 0:256], hT[:, dch, i * 128:(i + 1) * 128], wC[:, dch, 0:256], start=(dch == 0), stop=(dch == 7))
                        S.op("pe", f, reads=[bwC] + bhT, writes=[bb])
                        sg_, bsg_ = H["sgt"][i % 2], H["b_sgt"][i % 2]
                        S.op("act", lambda e: e.activation(out=sg_, in_=ps[:, bk, 0:256], func=AF.Silu), reads=[bb], writes=[bsg_])
                        S.op("pool", lambda e: e.tensor_tensor(out=og3[:, i, :], in0=og3[:, i, :], in1=sg_, op=ALU.mult), reads=[bsg_, bog], writes=[bog])
                        yield

                def head_R(hh, H):
                    qT3 = H["qT"].rearrange("p (q t) -> p q t", q=2)
                    kT3 = H["kT"].rearrange("p (q t) -> p q t", q=2)
                    ktok3 = H["ktok"].rearrange("p (j c) -> p j c", c=256)
                    Va3 = H["Vaug"].rearrange("p (j c) -> p j c", c=258)
                    og3 = H["og"].rearrange("p (j c) -> p j c", c=256)
                    Cst4 = H["Cst"].rearrange("p (d q v) -> p d q v", d=2, q=2)
                    Cs3 = [c.rearrange("p (q v) -> p q v", q=2) for c in H["Csb"]]
                    hsum3 = H["hsum"].rearrange("p (j c) -> p j c", c=256)
                    STw, kw, den, ss = H["STw"], H["kw"], H["den"], H["ss"]
                    bq, bkT, bkt_, bog, bhs, bhs2, bVa, bC = H["b_qT"], H["b_kT"], H["b_ktok"], H["b_og"], H["b_hsum"], H["b_hsum2"], H["b_Vaug"], H["b_C"]
                    bSTw, bkw, bCs, bss = H["b_STw"], H["b_kw"], H["b_Cs"], H["b_ss"]
                    bdens = [H["b_den0"], H["b_den1"]]
                    iters = [(k, d) for k in range(8) for d in range(2)]
                    prep = {}

                    def pre(it):
                        k, d = iters[it]
                        j = k if d == 0 else 7 - k
                        col = 4 * d + hh
                        wmk = wmkB[:, j, col:col + 1]
                        wa = waT[:, j, col:col + 1]
                        cs3, bcs = Cs3[d], bCs[d]
                        stw, bstw = STw[it % 2], bSTw[it % 2]
                        kwt, bkwt = kw[it % 2], bkw[it % 2]
                        msk = triu if d == 0 else tril
                        sk, sbk = pbank()

                        def f(e):
                            e.matmul(ps[:, sk, 0:128], kT3[:, 0, j * 128:(j + 1) * 128], qT3[:, 0, j * 128:(j + 1) * 128], start=True, stop=False)
                            e.matmul(ps[:, sk, 0:128], kT3[:, 1, j * 128:(j + 1) * 128], qT3[:, 1, j * 128:(j + 1) * 128], start=False, stop=True)
                        S.op("pe", f, reads=[bkT, bq], writes=[sbk])
                        S.op("dve", lambda e: e.scalar_tensor_tensor(out=stw, in0=ps[:, sk, 0:128], scalar=wa, in1=msk[:], op0=ALU.mult, op1=ALU.mult),
                             reads=[sbk, bwaT, bconst], writes=[bstw])
                        S.op("act", lambda e: e.activation(out=kwt, in_=ktok3[:, j, :], func=AF.Identity, scale=wa),
                             reads=[bkt_, bwaT], writes=[bkwt])
                        S.op("pool", lambda e: e.tensor_scalar(out=cs3[:, :, 0:257], in0=Cst4[:, d, :, :], scalar1=wmk, scalar2=sm("zero"), op0=ALU.mult, op1=ALU.add),
                             reads=[bC[d], bwmkB, bzero], writes=[bcs])
                        prep[it] = (k, d, j, col, wmk, cs3, bcs, stw, bstw, kwt, bkwt)

                    def body(it):
                        k, d, j, col, wmk, cs3, bcs, stw, bstw, kwt, bkwt = prep.pop(it)
                        eb = ebT[:, j, col:col + 1]
                        nk, nb = pbank()

                        def f(e):
                            e.matmul(ps[:, nk, 0:257], stw, Va3[:, j, 0:257], start=True, stop=False)
                            e.matmul(ps[:, nk, 0:257], qT3[:, 0, j * 128:(j + 1) * 128], cs3[:, 0, 0:257], start=False, stop=False)
                            e.matmul(ps[:, nk, 0:257], qT3[:, 1, j * 128:(j + 1) * 128], cs3[:, 1, 0:257], start=False, stop=True)
                        S.op("pe", f, reads=[bstw, bVa, bq, bcs], writes=[nb])
                        uks = []
                        for q2 in range(2):
                            uk, ub = pbank()
                            S.op("pe", lambda e: e.matmul(ps[:, uk, 0:257], kwt[:, q2 * 128:(q2 + 1) * 128], Va3[:, j, 0:257], start=True, stop=True),
                                 reads=[bkwt, bVa], writes=[ub])
                            uks.append((uk, ub))
                        for q2 in range(2):
                            uk, ub = uks[q2]
                            S.op("dve", lambda e: e.scalar_tensor_tensor(out=Cst4[:, d, q2, :], in0=Cst4[:, d, q2, :], scalar=wmk, in1=ps[:, uk, 0:257], op0=ALU.mult, op1=ALU.add),
                                 reads=[ub, bC[d], bwmkB], writes=[bC[d]])
                        dn, rdn, bdn = den[:, 2 * (it % 2):2 * (it % 2) + 1], den[:, 2 * (it % 2) + 1:2 * (it % 2) + 2], bdens[it % 2]
                        S.op("act", lambda e: e.activation(out=dn, in_=ps[:, nk, 256:257], func=AF.Abs), reads=[nb], writes=[bdn])
                        S.op("dve", lambda e: e.tensor_tensor(out=dn, in0=dn, in1=eb, op=ALU.max), reads=[bdn, bebT], writes=[bdn])
                        S.op("dve", lambda e: e.reciprocal(out=rdn, in_=dn), reads=[bdn], writes=[bdn])
                        first = (d == 0 and j <= 3) or (d == 1 and j >= 4)
                        if first:
                            S.op("dve", lambda e: e.tensor_scalar(out=hsum3[:, j, :], in0=ps[:, nk, 0:256], scalar1=rdn, scalar2=sm("zero"), op0=ALU.mult, op1=ALU.add),
                                 reads=[nb, bdn, bzero], writes=[bhs, bhs2])
                        else:
                            S.op("dve", lambda e: e.scalar_tensor_tensor(out=hsum3[:, j, :], in0=ps[:, nk, 0:256], scalar=rdn, in1=hsum3[:, j, :], op0=ALU.mult, op1=ALU.add),
                                 reads=[nb, bdn], writes=[bhs])
                        if k % 2 == 1:
                            slot = (k - 1) // 2 if d == 0 else 3 - (k - 1) // 2
                            S.dma("sp", "scout" + H["tag"], lambda e: e.dma_start(out=sc_d[l, slot, d, hh].rearrange("(q p) v -> p q v", p=128), in_=Cst4[:, d, :, :]),
                                  reads=[bC[d]])

                    pre(0)
                    for it in range(16):
                        if it + 1 < 16:
                            pre(it + 1)
                        body(it)
                        yield
                    junk, bjunk = H["sgt"][0], H["b_sgt"][0]
                    for j in range(8):
                        S.op("act", lambda e: e.activation(out=junk, in_=hsum3[:, j, :], func=AF.Square, accum_out=ss[:, j:j + 1]), reads=[bhs], writes=[bjunk, bss])
                    yield
                    S.op("act", lambda e: e.activation(out=ss[:, 8:16], in_=ss[:, 0:8], func=AF.Sqrt, scale=1.0 / 256.0, bias=EPS), reads=[bss], writes=[bss])
                    S.op("dve", lambda e: e.reciprocal(out=ss[:, 8:16], in_=ss[:, 8:16]), reads=[bss], writes=[bss])
                    for j in range(8):
                        S.op("dve", lambda e: e.scalar_tensor_tensor(out=og3[:, j, :], in0=hsum3[:, j, :], scalar=ss[:, 8 + j:9 + j], in1=og3[:, j, :], op0=ALU.mult, op1=ALU.mult),
                             reads=[bhs, bss, bog], writes=[bog])
                    yield
                    for q2 in range(2):
                        ch = hh * 2 + q2
                        for g in range(2):
                            bk, bb = pbank()
                            psb = ps[:, bk, :].bitcast(BF16)

                            def f(e):
                                for q in range(4):
                                    i = 4 * g + q
                                    e.transpose(out=psb[:, q * 128:(q + 1) * 128], in_=og3[:, i, q2 * 128:(q2 + 1) * 128], identity=identb[:])
                            S.op("pe", f, reads=[bog, bconst], writes=[bb])
                            S.op("act", lambda e: e.activation(out=ybT[:, ch, g * 512:(g + 1) * 512], in_=psb[:, 0:512], func=AF.Identity, scale=pp("mng", ch, 1)),
                                 reads=[bb, bPP], writes=[bybT[ch]])
                            yield

                def interleave2(ga, gb):
                    alive = [ga, gb]
                    while alive:
                        for g in list(alive):
                            try:
                                next(g)
                            except StopIteration:
                                alive.remove(g)

                for _ in head_P(0, HS[0]):
                    pass
                for hh in range(4):
                    if hh + 1 < 4:
                        interleave2(head_R(hh, HS[hh % 2]), head_P(hh + 1, HS[(hh + 1) % 2]))
                    else:
                        for _ in head_R(hh, HS[hh % 2]):
                            pass
                phase('ml%dend' % l)
                dbg("ybT%d" % l, ybT[:], bybT, [128, 8, T], BF16)

                phase('ml%d' % l)
                areset()
                if l == 0:
                    load_tables()
                mT = abf(8 * 1024)
                mT3 = mT.rearrange("p (c t) -> p c t", c=8)
                bm = [Buf("mT%d" % i) for i in range(8)]
                sg = abf(1024)
                t1 = abf(1024)
                bsg_, bt1 = Buf("sg"), Buf("t1")
                XW = [abf(8 * 512).rearrange("p (c n) -> p c n", c=8), abf(8 * 512).rearrange("p (c n) -> p c n", c=8)]
                bXW = [Buf("xw0"), Buf("xw1")]
                for r4 in range(2):
                    wga, bwga = wload([(w_in_d[l][:, 9232 + r4 * 512:9232 + (r4 + 1) * 512], 0)], prefetch=False)
                    wpa, bwpa = wload([(w_pa_d[l][:, r4 * 512:(r4 + 1) * 512], 0)], prefetch=False)
                    _issue(wstate["idx"], (XW[0], bXW[0]))
                    _issue(wstate["idx"] + 1, (XW[1], bXW[1]))
                    pend = []
                    for rq in range(4):
                        r = r4 * 4 + rq
                        pkg, pbg = ppair()
                        pkp, pbp = ppair()
                        proj_fm(pkg, pbg, wga, bwga, rq * 128)
                        proj_fm(pkp, pbp, wpa, bwpa, rq * 128, rhs=yaT, brhs=byaT)
                        S.op("act", lambda e, pkg=pkg: e.activation(out=sg.rearrange("p (a b) -> p a b", a=2), in_=ps[:, pkg:pkg + 2, :], func=AF.Sigmoid), reads=pbg, writes=[bsg_])
                        S.op("dve", lambda e, pkp=pkp, r=r: e.tensor_tensor(out=mT3[:, r, :].rearrange("p (a b) -> p a b", a=2), in0=ps[:, pkp:pkp + 2, :], in1=sg.rearrange("p (a b) -> p a b", a=2), op=ALU.mult),
                             reads=pbp + [bsg_], writes=[bm[r]])
                    wgb, bwgb = wload([(w_in_d[l][:, 10256 + r4 * 512:10256 + (r4 + 1) * 512], 0)], prefetch=False)
                    wpb, bwpb = wload([(w_pb_d[l][:, r4 * 512:(r4 + 1) * 512], 0)], prefetch=False)
                    for rq in range(4):
                        r = r4 * 4 + rq
                        pkg, pbg = ppair()
                        pkp, pbp = ppair()
                        proj_fm(pkg, pbg, wgb, bwgb, rq * 128)
                        proj_fm(pkp, pbp, wpb, bwpb, rq * 128, rhs=ybT, brhs=bybT)
                        S.op("act", lambda e, pkg=pkg: e.activation(out=sg.rearrange("p (a b) -> p a b", a=2), in_=ps[:, pkg:pkg + 2, :], func=AF.Sigmoid), reads=pbg, writes=[bsg_])
                        S.op("dve", lambda e, pkp=pkp: e.tensor_tensor(out=t1.rearrange("p (a b) -> p a b", a=2), in0=ps[:, pkp:pkp + 2, :], in1=sg.rearrange("p (a b) -> p a b", a=2), op=ALU.mult),
                             reads=pbp + [bsg_], writes=[bt1])
                        S.op("pool", lambda e, r=r: e.tensor_tensor(out=mT3[:, r, :], in0=mT3[:, r, :], in1=t1, op=ALU.add), reads=[bt1, bm[r]], writes=[bm[r]])
                for r4 in range(2):
                    wo, bwo = wload([(w_out_d[l][:, r4 * 512:(r4 + 1) * 512], 0)])
                    for rq in range(4):
                        r = r4 * 4 + rq
                        pk, pbs = ppair()
                        proj_fm(pk, pbs, wo, bwo, rq * 128, rhs=mT3, brhs=bm)
                        S.op("dve", lambda e, pk=pk, r=r: e.scalar_tensor_tensor(out=X[:, r, :].rearrange("p (a b) -> p a b", a=2), in0=ps[:, pk:pk + 2, :], scalar=sm("modsb", 16 + r, 1),
                                                                             in1=X[:, r, :].rearrange("p (a b) -> p a b", a=2), op0=ALU.mult, op1=ALU.add),
                             reads=pbs + [bsmall, bX[r]], writes=[bX[r]])
                dbg("X%d" % l, X[:], bX, [128, 8, T])

            phase('merge')
            areset()
            rstd, br = rms_rstd()
            yf = [af32(1024), af32(1024)]
            byf = [Buf("yf0"), Buf("yf1")]
            yst = [af32(512) for _ in range(4)]
            byst = [Buf("yst%d" % i) for i in range(4)]
            y_v = y_d.rearrange("(i p) d -> p i d", p=128)
            n = 0
            for dch in range(8):
                yt, by = yf[dch % 2], byf[dch % 2]
                S.op("dve", lambda e: e.scalar_tensor_tensor(out=yt, in0=X[:, dch, :], scalar=pp("fing", dch, 1), in1=rstd, op0=ALU.mult, op1=ALU.mult),
                     reads=[bX[dch], bPP, br], writes=[by])
                for g in range(2):
                    bk, bb = pbank()
                    ys, bys = yst[n % 4], byst[n % 4]
                    n += 1

                    def f(e):
                        for q in range(4):
                            i = 4 * g + q
                            e.transpose(out=ps[:, bk, q * 128:(q + 1) * 128], in_=yt[:, i * 128:(i + 1) * 128], identity=identf[:])
                    S.op("pe", f, reads=[by, bconst], writes=[bb])
                    if g == 0:
                        S.op("act", lambda e: e.copy(out=ys, in_=ps[:, bk, :]), reads=[bb], writes=[bys])
                    else:
                        S.op("dve", lambda e: e.tensor_copy(out=ys, in_=ps[:, bk, :]), reads=[bb], writes=[bys])
                    S.dma("sp", "yout%d" % (n % 4), lambda e: e.dma_start(out=y_v[:, 4 * g:4 * g + 4, dch * 128:(dch + 1) * 128], in_=ys.rearrange("p (q d) -> p q d", q=4)), reads=[bys])

        except _Stop:
            pass
        S.final_wait("sp")
        S.emit()
        stats = dict(ops=S.nop, waits=S.nwait)
    return nc, stats, list(dbg_outs.keys())


def _tables(L):
    NS = T // L
    t = np.arange(T)
    tl = (t % L).astype(np.float64)
    f = np.arange(L)
    Wsym = np.zeros((T, 2 * T), np.float32)
    W0 = np.zeros((T, 2 * T), np.float32)
    tt = np.arange(L)
    th = np.pi * np.outer(tt + 0.5, f + 0.5) / L
    th0 = np.pi * np.outer(tt, f + 0.5) / L
    for s in range(NS):
        r = slice(s * L, (s + 1) * L)
        Wsym[r, s * L:(s + 1) * L] = np.cos(th)
        Wsym[r, T + s * L:T + (s + 1) * L] = -np.sin(th)
        W0[r, s * L:(s + 1) * L] = np.cos(th0)
        W0[r, T + s * L:T + (s + 1) * L] = -np.sin(th0)
    bands = np.arange(1, 17)
    tn = tl / L
    ang = 2 * np.pi * tn[:, None] * bands[None, :]
    feats = np.concatenate([tn[:, None], np.cos(ang), np.sin(ang)], axis=-1).astype(np.float32)
    maskb = (tl != 0).astype(np.float32)
    cpb = L // 128
    keep = np.array([0.0 if (k % cpb == 0 and k > 0) else 1.0 for k in range(8)], np.float32)
    cc = np.zeros((128, NCC), np.float32)

    def put(name, arr):
        o, w = CCO[name]
        cc[:, o:o + w] = arr
    put("tnneg", -(tn.reshape(8, 128).T))
    put("maskb", maskb.reshape(8, 128).T)
    put("nmaskb", -(maskb.reshape(8, 128).T))
    put("bfixneg", -1.0 if L == 256 else 0.0)
    put("invhs", float(L) / NS)
    put("keep", keep[None, :])
    keepc = np.zeros((128, 8), np.float32)
    keepc[:, :] = keep[None, :]
    keepc[32:64, :] = keep[::-1][None, :]
    put("keepc", keepc)
    import ml_dtypes
    return dict(wsym=Wsym.astype(ml_dtypes.bfloat16), w0t=W0.astype(ml_dtypes.bfloat16), featsT=np.ascontiguousarray(feats.T), cc=cc)


def _chunks(v, n):
    return np.ascontiguousarray(np.asarray(v, np.float32).reshape(n, 128).T)


_PROG = {}


def kernel(x_prompt, x_sample, state_C, state_n, state_m, c, c_ctx, norm_g, w_ada, b_ada,
           w_in, hy_conv_w, hy_conv_b, hy_w1, hy_b1, hy_w2, hy_b2, hy_w3, hy_b3, hy_freq,
           hy_decay, hy_bias, ml_conv_w, ml_conv_b, ml_if_b, ml_norm_g, w_pa, w_pb, w_out, final_g):
    f32 = np.float32
    A = lambda a: np.ascontiguousarray(np.asarray(a, f32))
    x_prompt, x_sample = A(x_prompt), A(x_sample)
    pp = np.zeros((2, 128, NPP), f32)

    def put(l, name, arr, rows=slice(0, 128)):
        o, w = PPO[name]
        pp[l, rows, o:o + w] = arr
    wgi = np.zeros((2, D, 128), f32)
    wgf = np.zeros((2, D, 128), f32)
    for l in range(2):
        put(l, "b_ada", _chunks(b_ada[l], 24))
        put(l, "norm_g", _chunks(norm_g[l], 8))
        put(l, "hcw", np.concatenate([_chunks(hy_conv_w[l][j], 24) for j in range(3)], 1))
        put(l, "hcb", _chunks(hy_conv_b[l], 24))
        put(l, "hbias", np.concatenate([_chunks(hy_bias[l][o], 8) for o in range(2)], 1))
        put(l, "mcw", np.concatenate([_chunks(ml_conv_w[l][j], 16) for j in range(3)], 1))
        put(l, "mcb", _chunks(ml_conv_b[l], 16))
        put(l, "mng", _chunks(ml_norm_g[l], 8))
        put(l, "hb1", np.asarray(hy_b1[l], f32)[:, None], slice(0, 64))
        put(l, "hb2", np.asarray(hy_b2[l], f32)[:, None], slice(0, 64))
        put(l, "hfr", np.asarray(hy_freq[l], f32).T, slice(0, 64))
        put(l, "fing", _chunks(final_g, 8))
        g0 = 9216
        for d in range(2):
            for h in range(4):
                wgi[l, :, 32 * d + h] = w_in[l][:, g0 + d * 8 + 0 * 4 + h]
                wgf[l, :, 32 * d + h] = w_in[l][:, g0 + d * 8 + 1 * 4 + h]
                pp[l, 32 * d + h, PPO["gbi"][0]] = ml_if_b[l][d, 0, h]
                pp[l, 32 * d + h, PPO["gbf"][0]] = ml_if_b[l][d, 1, h]
    shared = dict(
        pp=pp, w_in=A(w_in), w_ada=A(w_ada), w_pa=A(w_pa), w_pb=A(w_pb), w_out=A(w_out),
        hy_w3=A(hy_w3), hy_b3=A(hy_b3).reshape(2, 1, 4096), hy_dec=A(hy_decay).reshape(2, 1, 4096),
        hy_w1=A(hy_w1), hy_w2=A(hy_w2), wgi=wgi, wgf=wgf,
        identf=np.eye(128, dtype=f32), triu=(np.triu(np.ones((128, 128), f32)) / 16.0).astype(f32),
        tril=(np.tril(np.ones((128, 128), f32)) / 16.0).astype(f32),
    )
    resetm = np.ones((64, T), f32)
    resetm[:, ::128] = 0.0
    shared["resetm"] = resetm
    tabs = {256: _tables(256), 1024: _tables(1024)}
    in_maps = []
    for core in range(8):
        m = dict(shared)
        if core < 4:
            L = 1024
            m["xin"] = x_sample[core]
            cv = np.asarray(c[core], f32)
            c0 = np.concatenate([np.asarray(state_C[core], f32), np.asarray(state_n[core], f32)[..., None]], axis=-1)
            m0 = np.zeros((2, 64, 1), f32)
            for l in range(2):
                for d in range(2):
                    m0[l, 32 * d:32 * d + 4, 0] = state_m[core][l, d]
        else:
            L = 256
            b0 = 4 * (core - 4)
            m["xin"] = x_prompt[b0:b0 + 4].reshape(T, D)
            cv = np.asarray(c_ctx, f32)
            c0 = np.zeros((2, 2, 4, 256, 257), f32)
            m0 = np.zeros((2, 64, 1), f32)
        tb = tabs[L]
        cc = tb["cc"].copy()
        o, w = CCO["cvec"]
        cc[:, o:o + w] = _chunks(cv, 8)
        m.update(wsym=tb["wsym"], w0t=tb["w0t"], featsT=tb["featsT"], cc=cc, c0aug=np.ascontiguousarray(c0), m0=m0)
        in_maps.append(m)

    if "nc" not in _PROG:
        _PROG["nc"] = build_program()
    nc, stats, dbgn = _PROG["nc"]
    res = run_bass_kernel_spmd(nc, in_maps, core_ids=list(range(8)))
    R = res.results
    _PROG["last"] = R
    y_sample = np.stack([np.asarray(R[i]["y"], f32) for i in range(4)], 0)
    y_prompt = np.concatenate([np.asarray(R[i]["y"], f32).reshape(4, 256, D) for i in range(4, 8)], 0)
    new_C = np.zeros((16, 2, 2, 4, 256, 256), f32)
    new_n = np.zeros((16, 2, 2, 4, 256), f32)
    new_m = np.zeros((16, 2, 2, 4), f32)
    for i in range(4, 8):
        sc = np.asarray(R[i]["sc"], f32)
        smo = np.asarray(R[i]["sm"], f32)
        for slot in range(4):
            b = 4 * (i - 4) + slot
            new_C[b] = sc[:, slot, :, :, :, :256]
            new_n[b] = sc[:, slot, :, :, :, 256]
            for d in range(2):
                k = 2 * slot + 1 if d == 0 else 2 * (3 - slot) + 1
                new_m[b, :, d, :] = smo[:, 32 * d:32 * d + 4, k]
    return (y_prompt, y_sample, new_C, new_n, new_m)
```
